# Optimizing a Trainium2 kernel written in Bass

```python
import jax, jax.numpy as jnp
from jax import lax
import numpy as np

D_MODEL = 2048
BATCH = 4
SEQ = 2048
DEPTH = 4
DEC_BATCH = 128
DEC_SEQ = 4
PAST_LEN = 16384
PAGE_SIZE = 128

D_MIX = D_MODEL
D_CONV = D_MIX // 2
N_CONV_GROUPS = 8
CONV_GROUP = D_CONV // N_CONV_GROUPS
CONV_A_W = 3
D_DN = D_MIX - D_CONV
N_DN_HEADS = 8
DK = D_DN // N_DN_HEADS
DV = DK
CONV_B_W = 4
CHUNK = 64
D_FF = 4 * D_MODEL
EPS = 1e-6
SPLITS = [D_CONV, 2 * D_CONV, 3 * D_CONV, 3 * D_CONV + 3 * D_DN,
          3 * D_CONV + 4 * D_DN, 3 * D_CONV + 4 * D_DN + N_DN_HEADS]
D_IN = 3 * D_CONV + 4 * D_DN + 2 * N_DN_HEADS

kernel_name = "hymba_conv_gdn_hybrid_step"


def rms_norm(x, w):
    xf = x.astype(jnp.float32)
    y = xf * lax.rsqrt(jnp.mean(xf * xf, axis=-1, keepdims=True) + EPS)
    return (y * w.astype(jnp.float32)).astype(x.dtype)


def causal_dwconv(inp, buf, w):
    width = w.shape[0]
    L = inp.shape[1]
    xp = jnp.concatenate([buf.astype(inp.dtype), inp], axis=1)
    out = sum(xp[:, i:i + L] * w[i] for i in range(width))
    return out, xp[:, -(width - 1):]


def gated_delta_chunked(q, k, v, g, beta, s0):
    bsz, L, H, _ = q.shape
    C = CHUNK if L % CHUNK == 0 else L
    N = L // C

    def blk(t):
        t = t.reshape((bsz, N, C, H) + t.shape[3:])
        return jnp.moveaxis(t, 3, 1)

    qc, kc, vc, gc, bc = blk(q), blk(k), blk(v), blk(g), blk(beta)
    G = jnp.cumsum(gc, axis=-1)
    incl = jnp.tril(jnp.ones((C, C), bool))
    decay = jnp.exp(jnp.where(incl, G[..., :, None] - G[..., None, :], -jnp.inf))
    strict = jnp.tril(jnp.ones((C, C), jnp.float32), -1)
    kk = jnp.einsum('bhncd,bhnmd->bhncm', kc, kc)
    a_mat = jnp.eye(C, dtype=jnp.float32) + bc[..., :, None] * decay * kk * strict
    rhs = jnp.concatenate([bc[..., None] * vc, (bc * jnp.exp(G))[..., None] * kc], axis=-1)
    sol = lax.linalg.triangular_solve(a_mat, rhs, left_side=True, lower=True, unit_diagonal=True)
    u_base, w_mat = sol[..., :DV], sol[..., DV:]
    p_mat = jnp.einsum('bhncd,bhnmd->bhncm', qc, kc) * decay
    q_dec = qc * jnp.exp(G)[..., None]
    k_end = kc * jnp.exp(G[..., -1:] - G)[..., None]
    gam_end = jnp.exp(G[..., -1])
    xs = tuple(jnp.moveaxis(t, 2, 0) for t in (u_base, w_mat, p_mat, q_dec, k_end, gam_end))

    def step(s, xn):
        ub, wm, pm, qd, ke, ge = xn
        u = ub - jnp.einsum('bhck,bhvk->bhcv', wm, s)
        o = jnp.einsum('bhck,bhvk->bhcv', qd, s) + jnp.einsum('bhcm,bhmv->bhcv', pm, u)
        s = ge[..., None, None] * s + jnp.einsum('bhcv,bhck->bhvk', u, ke)
        return s, o

    s_fin, o = lax.scan(step, s0, xs)
    o = jnp.transpose(o, (1, 0, 3, 2, 4)).reshape(bsz, L, H, DV)
    return o, s_fin


def mixer(h, buf_a, buf_qkv, s0, w_in, conv_a_w, conv_a_norm_w, conv_qkv_w,
          a_log, dt_bias, dn_norm_w, w_out):
    bsz, L, _ = h.shape
    proj = h @ w_in
    b_a, c_a, h_a, qkv, z, a_in, b_in = jnp.split(proj, SPLITS, axis=-1)
    conv_out, new_buf_a = causal_dwconv(c_a * h_a, buf_a, conv_a_w)
    y_a = (b_a * conv_out).reshape(bsz, L, N_CONV_GROUPS, CONV_GROUP)
    y_a = rms_norm(y_a, conv_a_norm_w.reshape(N_CONV_GROUPS, CONV_GROUP)).reshape(bsz, L, D_CONV)
    qkv_c, new_buf_qkv = causal_dwconv(qkv, buf_qkv, conv_qkv_w)
    qkv_c = jax.nn.silu(qkv_c.astype(jnp.float32)).reshape(bsz, L, 3, N_DN_HEADS, DK)
    q, k, v = qkv_c[:, :, 0], qkv_c[:, :, 1], qkv_c[:, :, 2]
    q = q * lax.rsqrt(jnp.sum(q * q, -1, keepdims=True) + EPS) * (DK ** -0.5)
    k = k * lax.rsqrt(jnp.sum(k * k, -1, keepdims=True) + EPS)
    g = -jnp.exp(a_log.astype(jnp.float32)) * jax.nn.softplus(a_in.astype(jnp.float32) + dt_bias.astype(jnp.float32))
    beta = jax.nn.sigmoid(b_in.astype(jnp.float32))
    o, s_new = gated_delta_chunked(q, k, v, g, beta, s0.astype(jnp.float32))
    zf = jax.nn.silu(z.astype(jnp.float32)).reshape(bsz, L, N_DN_HEADS, DV)
    o = (rms_norm(o, dn_norm_w) * zf).reshape(bsz, L, D_DN).astype(h.dtype)
    out = jnp.concatenate([y_a, o], axis=-1) @ w_out
    return out, new_buf_a, new_buf_qkv, s_new.astype(s0.dtype)


def run_trunk(x, bufs_a, bufs_qkv, states, norm_mix_w, w_in, conv_a_w, conv_a_norm_w,
              conv_qkv_w, a_log, dt_bias, dn_norm_w, w_out, norm_ffn_w, w_up, w_down, final_norm_w):
    new_a, new_qkv, new_s = [], [], []
    for l in range(DEPTH):
        h = rms_norm(x, norm_mix_w[l])
        m, ba, bq, s = mixer(h, bufs_a[l], bufs_qkv[l], states[l], w_in[l], conv_a_w[l],
                             conv_a_norm_w[l], conv_qkv_w[l], a_log[l], dt_bias[l],
                             dn_norm_w[l], w_out[l])
        x = x + m
        h = rms_norm(x, norm_ffn_w[l])
        x = x + jnp.square(jax.nn.relu(h @ w_up[l])) @ w_down[l]
        new_a.append(ba)
        new_qkv.append(bq)
        new_s.append(s)
    return rms_norm(x, final_norm_w), jnp.stack(new_a), jnp.stack(new_qkv), jnp.stack(new_s)


def setup_inputs(seed: int = 0) -> dict:
    key = jax.random.key(seed)
    ks = jax.random.split(key, 20)
    f32 = jnp.float32
    nrm = lambda k, s, sc: jax.random.normal(k, s, f32) * sc
    dt = jnp.exp(jax.random.uniform(ks[11], (DEPTH, N_DN_HEADS), f32, np.log(1e-3), np.log(1e-1)))
    return {
        "x_prompt": nrm(ks[0], (BATCH, SEQ, D_MODEL), 1.0),
        "x_sample": nrm(ks[1], (DEC_BATCH, DEC_SEQ, D_MODEL), 1.0),
        "state_conv_a": nrm(ks[2], (DEPTH, DEC_BATCH, CONV_A_W - 1, D_CONV), 1.0),
        "state_conv_qkv": nrm(ks[3], (DEPTH, DEC_BATCH, CONV_B_W - 1, 3 * D_DN), 1.0),
        "state_delta": nrm(ks[4], (DEPTH, DEC_BATCH, N_DN_HEADS, DV, DK), 0.05),
        "norm_mix_w": 1.0 + nrm(ks[5], (DEPTH, D_MODEL), 0.02),
        "w_in": nrm(ks[6], (DEPTH, D_MODEL, D_IN), D_MODEL ** -0.5),
        "conv_a_w": nrm(ks[7], (DEPTH, CONV_A_W, D_CONV), CONV_A_W ** -0.5),
        "conv_a_norm_w": 1.0 + nrm(ks[8], (DEPTH, D_CONV), 0.02),
        "conv_qkv_w": nrm(ks[9], (DEPTH, CONV_B_W, 3 * D_DN), CONV_B_W ** -0.5),
        "a_log": jnp.log(jax.random.uniform(ks[10], (DEPTH, N_DN_HEADS), f32, 1.0, 16.0)),
        "dt_bias": dt + jnp.log(-jnp.expm1(-dt)),
        "dn_norm_w": 1.0 + nrm(ks[12], (DEPTH, DV), 0.02),
        "w_out": nrm(ks[13], (DEPTH, D_MIX, D_MODEL), D_MIX ** -0.5),
        "norm_ffn_w": 1.0 + nrm(ks[14], (DEPTH, D_MODEL), 0.02),
        "w_up": nrm(ks[15], (DEPTH, D_MODEL, D_FF), D_MODEL ** -0.5),
        "w_down": nrm(ks[16], (DEPTH, D_FF, D_MODEL), D_FF ** -0.5),
        "final_norm_w": 1.0 + nrm(ks[17], (D_MODEL,), 0.02),
    }


def reference(x_prompt, x_sample, state_conv_a, state_conv_qkv, state_delta, norm_mix_w, w_in,
              conv_a_w, conv_a_norm_w, conv_qkv_w, a_log, dt_bias, dn_norm_w, w_out,
              norm_ffn_w, w_up, w_down, final_norm_w):
    params = (norm_mix_w, w_in, conv_a_w, conv_a_norm_w, conv_qkv_w, a_log, dt_bias,
              dn_norm_w, w_out, norm_ffn_w, w_up, w_down, final_norm_w)
    zero_a = jnp.zeros((DEPTH, BATCH, CONV_A_W - 1, D_CONV), x_prompt.dtype)
    zero_qkv = jnp.zeros((DEPTH, BATCH, CONV_B_W - 1, 3 * D_DN), x_prompt.dtype)
    zero_s = jnp.zeros((DEPTH, BATCH, N_DN_HEADS, DV, DK), state_delta.dtype)
    y_prompt, new_conv_a_p, new_conv_qkv_p, new_delta_p = run_trunk(
        x_prompt, zero_a, zero_qkv, zero_s, *params)
    y_sample, new_conv_a_s, new_conv_qkv_s, new_delta_s = run_trunk(
        x_sample, state_conv_a, state_conv_qkv, state_delta, *params)
    return (y_prompt, y_sample, new_conv_a_p, new_conv_qkv_p, new_delta_p,
            new_conv_a_s, new_conv_qkv_s, new_delta_s)
```

```python
import contextlib
import numpy as np
import concourse.bass as bass
import concourse.mybir as mybir
from concourse.bass_utils import run_bass_kernel_spmd

DT = mybir.dt
F32 = DT.float32
BF16 = DT.bfloat16
ACT = mybir.ActivationFunctionType
ALU = mybir.AluOpType
EPS = 1e-6

ENGS = ["pe", "act", "dve", "pool", "sp"]
EPOCH = 16000


class Buf:
    __slots__ = ("name", "w", "r")

    def __init__(self, name):
        self.name = name
        self.w = None
        self.r = []


class Tile:
    def __init__(self, t, b):
        self.t = t
        self.b = b


class Prog:
    def __init__(self, nc):
        self.nc = nc
        self.q = {e: [] for e in ENGS}
        self.cnt = {}
        self.seen = {e: {} for e in ENGS}
        self.ecnt = {e: 0 for e in ENGS}
        self.dma_last = {}
        self.stack = contextlib.ExitStack()
        self.nbuf = 0

    def sbuf(self, name, shape, dtype=F32):
        return self.stack.enter_context(self.nc.sbuf_tensor("sb_" + name, list(shape), dtype))

    def psum(self, name, shape, dtype=F32):
        return self.stack.enter_context(self.nc.psum_tensor("ps_" + name, list(shape), dtype))

    def buf(self, name=None):
        self.nbuf += 1
        return Buf(name or f"b{self.nbuf}")

    def tile(self, name, shape, dtype=F32):
        return Tile(self.sbuf(name, shape, dtype), self.buf(name))

    def _waits(self, eng, reads, writes):
        need = {}

        def add(t):
            if t is None:
                return
            k, v = t
            if need.get(k, 0) < v:
                need[k] = v
        for b in reads:
            add(b.w)
        for b in writes:
            add(b.w)
            for t in b.r:
                add(t)
        out = []
        seen = self.seen[eng]
        for k, v in need.items():
            if seen.get(k, 0) < v:
                seen[k] = v
                out.append((k, v))
        return out

    def op(self, eng, fn, reads=(), writes=(), signal=True):
        waits = self._waits(eng, reads, writes)
        tick = None
        if signal:
            self.ecnt[eng] += 1
            key = (eng, self.ecnt[eng] // EPOCH)
            self.cnt[key] = self.cnt.get(key, 0) + 1
            tick = (key, self.cnt[key])
        self.q[eng].append((waits, fn, tick, 1))
        if tick is not None:
            for b in reads:
                if len(b.r) > 24:
                    b.r = b.r[-24:] if False else b.r
                b.r.append(tick)
            for b in writes:
                b.w = tick
                b.r = []
        return tick

    def group(self, eng, fns, reads=(), writes=()):
        n = len(fns)
        for i, fn in enumerate(fns):
            if i == n - 1:
                return self.op(eng, fn, reads, writes, signal=True)
            if i == 0:
                self.op(eng, fn, reads, writes, signal=False)
            else:
                self.op(eng, fn, (), (), signal=False)

    def dma(self, eng, fn, semkey, reads=(), writes=()):
        key = ("dma", semkey)
        waits = self._waits(eng, reads, writes)
        prev = self.dma_last.get(key)
        if prev is not None and self.seen[eng].get(key, 0) < prev[1]:
            self.seen[eng][key] = prev[1]
            waits.append(prev)
        self.cnt[key] = self.cnt.get(key, 0) + 16
        tick = (key, self.cnt[key])
        self.dma_last[key] = tick
        self.q[eng].append((waits, fn, tick, 16))
        for b in reads:
            b.r.append(tick)
        for b in writes:
            b.w = tick
            b.r = []
        return tick

    def wait_all(self, eng, ticks):
        waits = []
        for t in ticks:
            if t is None:
                continue
            k, v = t
            if self.seen[eng].get(k, 0) < v:
                self.seen[eng][k] = v
                waits.append((k, v))
        self.q[eng].append((waits, None, None, 0))

    def finish(self):
        nc = self.nc
        sems = {}
        for i, k in enumerate(self.cnt):
            sems[k] = self.stack.enter_context(nc.semaphore(f"s{i}"))
        engobj = {"pe": "tensor", "act": "scalar", "dve": "vector", "pool": "gpsimd", "sp": "sync"}
        with nc.Block() as block:
            for e in ENGS:
                items = self.q[e]
                if not items:
                    continue

                def body(engine, items=items):
                    for waits, fn, tick, n in items:
                        for k, v in waits:
                            engine.wait_ge(sems[k], v)
                        if fn is None:
                            continue
                        ins = fn(engine)
                        if tick is not None:
                            ins.then_inc(sems[tick[0]], n)
                getattr(block, engobj[e])(body)
        self.stack.close()


class Cfg:
    def __init__(self, D=2048, L=4, NST=4, TP=512, SS=4):
        self.D = D
        self.L = L
        self.NST = NST
        self.TP = TP
        self.SS = SS
        self.KC = D // 128
        self.DC = D // 2
        self.NG = self.DC // 128
        self.NH = self.DC // 128
        self.DFF = 4 * D
        self.NTS = SS * 4
        self.NT = TP + self.NTS
        self.NCH = TP // 64
        self.NSEQ = NST * SS
        self.CWO = min(512, D)
        self.NOT = D // self.CWO
        self.HGS = min(2048, self.DFF)
        self.NHG = self.DFF // self.HGS
        self.HK = self.HGS // 128
        self.NUT = self.HGS // 512
        self.YC = D // 128
        o = 0
        self.p_n1 = o; o += self.KC
        self.p_n2 = o; o += self.KC
        self.p_caw = o; o += self.NG * 3
        self.p_canw = o; o += self.NG
        self.p_cbw = o; o += self.NH * 3 * 4
        self.p_dnw = o; o += 1
        self.p_dtb = o; o += self.NH
        self.p_alog = o; o += self.NH
        self.NPAR = o
        o = 0
        self.c_id = o; o += 128
        self.c_ones = o; o += 128
        self.c_trili = o; o += 65
        self.c_sgt = o; o += 64
        self.c_msu = o; o += 64
        self.c_miu = o; o += 64
        self.c_trili4 = o; o += 5
        self.c_mm64 = o; o += 6 * 128
        self.c_ii64 = o; o += 128
        self.c_mm4 = o; o += 2 * 8
        self.c_ii4 = o; o += 8
        self.NCONST = o
        self.WSLOT = 16 * 512


def make_consts(cfg):
    c = np.zeros((128, cfg.NCONST), np.float32)
    c[:, cfg.c_id:cfg.c_id + 128] = np.eye(128, dtype=np.float32)
    c[:, cfg.c_ones:cfg.c_ones + 128] = 1.0
    j = np.arange(64)[:, None]
    m = np.arange(64)[None, :]
    c[:64, cfg.c_trili:cfg.c_trili + 64] = (j <= m)
    c[:64, cfg.c_trili + 64] = 1.0
    c[:64, cfg.c_sgt:cfg.c_sgt + 64] = (j > m)
    c[:64, cfg.c_msu:cfg.c_msu + 64] = (m > j)
    c[:64, cfg.c_miu:cfg.c_miu + 64] = (m >= j)
    c[:4, cfg.c_trili4:cfg.c_trili4 + 4] = (j[:4] <= m[:, :4])
    c[:4, cfg.c_trili4 + 4] = 1.0
    mi = np.arange(64)[:, None]
    ci = np.arange(64)[None, :]
    for i in range(6):
        sz = 2 ** i
        M = ((mi // (2 * sz)) == (ci // (2 * sz))) & ((mi // sz) % 2 == 0) & ((ci // sz) % 2 == 1)
        M = M.astype(np.float32)
        c[:64, cfg.c_mm64 + i * 128: cfg.c_mm64 + i * 128 + 64] = M
        c[:64, cfg.c_mm64 + i * 128 + 64: cfg.c_mm64 + (i + 1) * 128] = M.T
        if i < 2:
            c[:4, cfg.c_mm4 + i * 8: cfg.c_mm4 + i * 8 + 4] = M[:4, :4]
            c[:4, cfg.c_mm4 + i * 8 + 4: cfg.c_mm4 + (i + 1) * 8] = M[:4, :4].T
    c[:64, cfg.c_ii64:cfg.c_ii64 + 64] = np.eye(64)
    c[:64, cfg.c_ii64 + 64:cfg.c_ii64 + 128] = np.eye(64)
    c[:4, cfg.c_ii4:cfg.c_ii4 + 4] = np.eye(4)
    c[:4, cfg.c_ii4 + 4:cfg.c_ii4 + 8] = np.eye(4)
    return c


def build_program(cfg):
    nc = bass.Bass("TRN2", target_bir_lowering=False)
    P = Prog(nc)
    D, L, NST, TP, SS, KC, NG, NH = cfg.D, cfg.L, cfg.NST, cfg.TP, cfg.SS, cfg.KC, cfg.NG, cfg.NH
    NT, NTS, NCH, NSEQ = cfg.NT, cfg.NTS, cfg.NCH, cfg.NSEQ
    YC = cfg.YC

    def din(name, shape):
        return nc.dram_tensor(name, list(shape), F32, kind="ExternalInput").ap()

    def dout(name, shape):
        return nc.dram_tensor(name, list(shape), F32, kind="ExternalOutput").ap()

    xin = din("xin", [NST, 128, KC, NT])
    w_ab = din("w_ab", [L, 128, KC, 2 * NH])
    w_A = din("w_A", [L, NG, 128, KC, 384])
    w_B = din("w_B", [L, NH, 128, KC, 512])
    w_o = din("w_o", [L, cfg.NOT, 128, YC, cfg.CWO])
    w_u = din("w_u", [L, cfg.NHG * cfg.NUT, 128, KC, 512])
    w_d = din("w_d", [L, cfg.NHG, cfg.NOT, 128, cfg.HK, cfg.CWO])
    par_d = din("par", [L, 128, cfg.NPAR])
    fnw_d = din("fnw", [128, KC])
    const_d = din("consts", [128, cfg.NCONST])
    s_ca = din("s_ca", [L, 128, NG, NSEQ, 2])
    s_cq = din("s_cq", [L, 128, NH, 3, NSEQ, 3])
    s_dl = din("s_dl", [L, NSEQ, NH, 128, 128])
    yout = dout("yout", [NST, 128, KC, NT])
    o_ca_p = dout("o_ca_p", [L, 128, NG, 2])
    o_cq_p = dout("o_cq_p", [L, 128, NH, 3, 3])
    o_dl_p = dout("o_dl_p", [L, NH, 128, 128])
    o_ca_s = dout("o_ca_s", [L, 128, NG, NSEQ, 2])
    o_cq_s = dout("o_cq_s", [L, 128, NH, 3, NSEQ, 3])
    o_dl_s = dout("o_dl_s", [L, NSEQ, NH, 128, 128])
    xs = nc.dram_tensor("xs", [NST, 128, KC, NT], F32).ap()
    b_xs = [P.buf(f"xs{i}") for i in range(NST)]
    out_ticks = []

    xT = P.sbuf("xT", [128, KC, NT]); b_x = [P.buf(f"x{k}") for k in range(KC)]
    hT = P.sbuf("hT", [128, KC, NT], BF16); b_h = [P.buf(f"h{k}") for k in range(KC)]
    NYH = max(YC, cfg.HK)
    yh = P.sbuf("yh", [128, NYH, NT], BF16); b_yh = [P.buf(f"yh{k}") for k in range(NYH)]
    NWS = 3
    wsl = [P.tile(f"ws{i}", [128, cfg.WSLOT], BF16) for i in range(NWS)]
    wab_sb = P.tile("wab", [128, KC, 2 * NH], BF16)
    cst = P.tile("cst", [128, cfg.NCONST])
    idb = P.tile("idb", [128, 128], BF16)
    onesb = P.tile("onesb", [128, 128], BF16)
    par = P.tile("par", [128, cfg.NPAR])
    fnw = P.tile("fnw", [128, KC])
    negA = P.tile("negA", [128, NH])
    sq = [P.tile(f"sq{i}", [128, NT], BF16) for i in range(2)]
    rstdB = P.tile("rstdB", [128, NT])
    pj = [P.tile(f"pj{i}", [128, NT]) for i in range(2)]
    WU = TP + 3 + 7 * SS
    U = [P.tile(f"U{i}", [128, WU]) for i in range(3)]
    acc = P.tile("acc", [128, NT])
    tmpN = acc
    cv = [P.tile(f"cv{i}", [128, NT]) for i in range(2)]
    zs = [P.tile(f"zs{i}", [128, NT]) for i in range(2)]
    qn = [P.tile(f"qn{i}", [128, NT], BF16) for i in range(2)]
    kn = [P.tile(f"kn{i}", [128, NT], BF16) for i in range(2)]
    vb = [P.tile(f"vb{i}", [128, NT], BF16) for i in range(2)]
    haloA = P.tile("haloA", [128, NG, 2])
    haloB = P.tile("haloB", [128, NH, 3, 3])
    S32 = [P.tile(f"S32_{h}", [128, 128]) for h in range(NH)]
    Sbf = [P.tile(f"Sbf_{h}", [128, 128], BF16) for h in range(NH)]
    Ss32 = [P.tile(f"Ss32_{i}", [128, SS, 128]) for i in range(2)]
    Ssbf = [[P.tile(f"Ssbf_{i}_{s}", [128, 128], BF16) for s in range(SS)] for i in range(2)]
    Ssn = [[P.tile(f"Ssn_{i}_{s}", [128, 128]) for s in range(SS)] for i in range(1)]
    NU = NCH + SS
    UC = [64 if u < NCH else 4 for u in range(NU)]
    gt = [P.tile(f"g{u}", [UC[u], NH]) for u in range(NU)]
    bt = [P.tile(f"bt{u}", [UC[u], NH]) for u in range(NU)]
    t1 = [P.tile(f"t1_{u}", [UC[u], NH]) for u in range(NU)]
    eG = [P.tile(f"eG{u}", [UC[u], NH]) for u in range(NU)]
    neG = [P.tile(f"neG{u}", [UC[u], NH]) for u in range(NU)]
    geb = [P.tile(f"geb{u}", [128, NH]) for u in range(NU)]
    _gm = [P.tile(f"gm{i}", [64, 65]) for i in range(2)]
    gm = [_gm[u % 2] for u in range(NU)]
    E = [P.tile(f"E{u}", [UC[u], UC[u] + 1]) for u in range(NU)]
    _Es = [P.tile(f"Es{i}", [64, 64]) for i in range(2)]
    _Ei = [P.tile(f"Ei{i}", [64, 64]) for i in range(2)]
    Es = [_Es[u % 2] for u in range(NU)]
    Ei = [_Ei[u % 2] for u in range(NU)]
    LL = [P.tile(f"LL{u}", [UC[u], 2 * UC[u]]) for u in range(NU)]
    BBs = [P.tile(f"BBs{u}", [UC[u], 2 * UC[u]]) for u in range(NU)]
    PQ = [P.tile(f"PQ{u}", [UC[u], 2 * UC[u]]) for u in range(NU)]
    TTp = [[P.tile(f"TT{u}_{i}", [UC[u], 2 * UC[u]]) for i in range(2)] for u in range(NU)]
    Ttb = [P.tile(f"Ttb{u}", [UC[u], UC[u]], BF16) for u in range(NU)]
    pmT = [P.tile(f"pmT{u}", [UC[u], UC[u]], BF16) for u in range(NU)]
    Vtm = [P.tile(f"Vtm{u}", [UC[u], 128], BF16) for u in range(NU)]
    ke = [P.tile(f"ke{u}", [UC[u], 128], BF16) for u in range(NU)]
    NR = 2
    Yt = [P.tile(f"Y{i}", [64, 128], BF16) for i in range(NR)]
    ut = [P.tile(f"u{i}", [64, 128], BF16) for i in range(NR)]
    o1s = [P.tile(f"o1s{i}", [64, 128]) for i in range(NR)]
    ot = [P.tile(f"o{i}", [64, 128]) for i in range(NR)]
    junk = o1s
    ssq = [P.tile(f"ssq{i}", [64, 1]) for i in range(NR)]
    rsq = [P.tile(f"rsq{i}", [64, 1]) for i in range(NR)]
    onb = [P.tile(f"onb{i}", [64, 128], BF16) for i in range(NR)]
    NBIG = 2
    pbig = [Tile(P.psum(f"pb{i}", [128, 1024]), P.buf(f"pb{i}")) for i in range(NBIG)]
    NSM = 4
    psm = [Tile(P.psum(f"psmb{i}", [128, 512])[:, 0:128], P.buf(f"psm{i}")) for i in range(NSM)]
    rr = {"big": 0, "sm": 0, "ws": 0, "rot": 0}

    def big():
        rr["big"] += 1
        return pbig[rr["big"] % NBIG]

    def small():
        rr["sm"] += 1
        return psm[rr["sm"] % NSM]

    def A_(eng, fn, reads=(), writes=()):
        return P.op(eng, fn, [t.b if isinstance(t, Tile) else t for t in reads],
                    [t.b if isinstance(t, Tile) else t for t in writes])

    def act(out, in_, func, reads, writes, scale=None, bias=None, accum=None):
        kw = {}
        if scale is not None:
            kw["scale"] = scale
        if bias is not None:
            kw["bias"] = bias
        if accum is not None:
            kw["accum_out"] = accum
        return A_("act", lambda e: e.activation(out=out, in_=in_, func=func, **kw), reads, writes)

    def tt(out, in0, in1, op, reads, writes):
        return A_("dve", lambda e: e.tensor_tensor(out=out, in0=in0, in1=in1, op=op), reads, writes)

    def stt(out, in0, scalar, in1, op0, op1, reads, writes):
        return A_("dve", lambda e: e.scalar_tensor_tensor(out=out, in0=in0, scalar=scalar, in1=in1,
                                                          op0=op0, op1=op1), reads, writes)

    def ts(out, in0, s1, op0, reads, writes, s2=None, op1=None):
        if op1 is None:
            return A_("dve", lambda e: e.tensor_scalar(out=out, in0=in0, scalar1=s1, scalar2=None, op0=op0),
                      reads, writes)
        return A_("dve", lambda e: e.tensor_scalar(out=out, in0=in0, scalar1=s1, scalar2=s2, op0=op0, op1=op1),
                  reads, writes)

    def recip(out, in_, reads, writes):
        return A_("dve", lambda e: e.reciprocal(out=out, in_=in_), reads, writes)

    def mm(out, lhsT, rhs, reads, writes, start=True, stop=True):
        return A_("pe", lambda e: e.matmul(out, lhsT, rhs, start=start, stop=stop), reads, writes)

    def tr(out, in_, ident, reads, writes):
        return A_("pe", lambda e: e.transpose(out, in_, ident), reads, writes)

    def sdma(out, in_, key, reads=(), writes=()):
        return P.dma("sp", lambda e: e.dma_start(out=out, in_=in_), key,
                     [t.b if isinstance(t, Tile) else t for t in reads],
                     [t.b if isinstance(t, Tile) else t for t in writes])

    def wload(src_ap, kch, cols):
        rr["ws"] += 1
        sl = wsl[rr["ws"] % NWS]
        i = rr["ws"] % NWS
        view = sl.t[:, 0:kch * cols].rearrange("p (k c) -> p k c", k=kch)
        P.dma("pool", lambda e: e.dma_start(out=view, in_=src_ap), f"w{i}", [], [sl.b])
        return sl, view

    def parcol(c0, n=1):
        return par.t[:, c0:c0 + n]

    TTS = [(0, min(512, NT))]
    if NT > 512:
        TTS.append((512, NT))

    def big_mm(wview, kch, col0, rhs_t, rhs_bufs, wslot):
        pb = big()
        fns = []
        for k in range(kch):
            for (a, b) in TTS:
                fns.append(lambda e, k=k, a=a, b=b: e.matmul(
                    pb.t[:, a:b], wview[:, k, col0:col0 + 128], rhs_t[:, k, a:b],
                    start=(k == 0), stop=(k == kch - 1)))
        P.group("pe", fns, [wslot.b] + list(rhs_bufs), [pb.b])
        return pb

    def ones_sum(src_tiles, nparts=128):
        pb = big()
        n = len(src_tiles)
        fns = []
        for i, s in enumerate(src_tiles):
            for (a, b) in TTS:
                fns.append(lambda e, i=i, s=s, a=a, b=b: e.matmul(
                    pb.t[:, a:b], onesb.t[:, :], s.t[:, a:b], start=(i == 0), stop=(i == n - 1)))
        P.group("pe", fns, [onesb.b] + [s.b for s in src_tiles], [pb.b])
        return pb

    def rstd_from(pb, scale, out_tile):
        act(tmpN.t[:, :], pb.t[:, 0:NT], ACT.Sqrt, [pb], [tmpN], scale=scale, bias=epsc.t[:, 0:1])
        recip(out_tile.t[:, :], tmpN.t[:, :], [tmpN], [out_tile])

    epsc = P.tile("epsc", [128, 1])
    A_("dve", lambda e: e.memset(epsc.t[:, :], EPS), [], [epsc])
    sdma(cst.t[:, :], const_d, "c0", [], [cst])
    sdma(fnw.t[:, :], fnw_d, "c1", [], [fnw])
    A_("dve", lambda e: e.tensor_copy(out=idb.t[:, :], in_=cst.t[:, cfg.c_id:cfg.c_id + 128]), [cst], [idb])
    A_("dve", lambda e: e.tensor_copy(out=onesb.t[:, :], in_=cst.t[:, cfg.c_ones:cfg.c_ones + 128]), [cst], [onesb])
    TRILI = lambda C: (cst.t[0:C, cfg.c_trili:cfg.c_trili + 65] if C == 64
                       else cst.t[0:C, cfg.c_trili4:cfg.c_trili4 + C + 1])
    SGT = lambda C: cst.t[0:C, cfg.c_sgt:cfg.c_sgt + C]
    MSU = lambda C: cst.t[0:C, cfg.c_msu:cfg.c_msu + C]
    MIU = lambda C: cst.t[0:C, cfg.c_miu:cfg.c_miu + C]
    ONESF = lambda C: cst.t[0:C, cfg.c_ones:cfg.c_ones + 128]

    def norm_to_h(pcol):
        srcs = []
        pb = big()
        fns = []
        reads = [onesb.b]
        for k in range(KC):
            s = sq[k % 2]
            act(s.t[:, :], xT[:, k, :], ACT.Square, [b_x[k]], [s])
            for (a, b) in TTS:
                P.op("pe", lambda e, k=k, s=s, a=a, b=b: e.matmul(
                    pb.t[:, a:b], onesb.t[:, :], s.t[:, a:b], start=(k == 0), stop=(k == KC - 1)),
                    [onesb.b, s.b], [pb.b], signal=True)
        rstd_from(pb, 1.0 / D, rstdB)
        for k in range(KC):
            stt(hT[:, k, :], xT[:, k, :], parcol(pcol + k), rstdB.t[:, :], ALU.mult, ALU.mult,
                [b_x[k], par, rstdB], [b_h[k]])

    def unit_cols(u):
        if u < NCH:
            return u * 64, 64
        return TP + (u - NCH) * 4, 4

    def ab_proj():
        pss = []
        for u in range(NU):
            c0, C = unit_cols(u)
            ps = small()
            fns = [lambda e, k=k, ps=ps, c0=c0, C=C: e.matmul(
                ps.t[0:C, 0:2 * NH], hT[:, k, c0:c0 + C], wab_sb.t[:, k, :],
                start=(k == 0), stop=(k == KC - 1)) for k in range(KC)]
            P.group("pe", fns, [wab_sb.b] + b_h, [ps.b])
            tt(t1[u].t[0:C, :], ps.t[0:C, 0:NH], parcol(cfg.p_dtb, NH)[0:C, :], ALU.add, [ps, par], [t1[u]])
            act(bt[u].t[0:C, :], ps.t[0:C, NH:2 * NH], ACT.Sigmoid, [ps], [bt[u]])
        for u in range(NU):
            c0, C = unit_cols(u)
            act(t1[u].t[0:C, :], t1[u].t[0:C, :], ACT.Exp, [t1[u]], [t1[u]])
        for u in range(NU):
            c0, C = unit_cols(u)
            act(t1[u].t[0:C, :], t1[u].t[0:C, :], ACT.Ln, [t1[u]], [t1[u]], bias=onec.t[0:C, 0:1])
            tt(gt[u].t[0:C, :], t1[u].t[0:C, :], negA.t[0:C, :], ALU.mult, [t1[u], negA], [gt[u]])
        for u in range(NU):
            c0, C = unit_cols(u)
            ps = small()
            mm(ps.t[0:C, 0:NH], TRILI(C)[:, 0:C], gt[u].t[0:C, :], [cst, gt[u]], [ps])
            act(eG[u].t[0:C, :], ps.t[0:C, 0:NH], ACT.Exp, [ps], [eG[u]])
            ts(neG[u].t[0:C, :], eG[u].t[0:C, :], -1.0, ALU.mult, [eG[u]], [neG[u]])
            ps2 = small()
            mm(ps2.t[:, 0:NH], ONESF(C), gt[u].t[0:C, :], [cst, gt[u]], [ps2])
            act(geb[u].t[:, :], ps2.t[:, 0:NH], ACT.Exp, [ps2], [geb[u]])

    def delta_head(hd, par2, st, l, sset):
        q_, k_, v_, z_ = qn[par2], kn[par2], vb[par2], zs[par2]
        units = list(range(NU))
        kkq = {}
        for u in units:
            c0, C = unit_cols(u)
            ts(gm[u].t[0:C, 0:C + 1], TRILI(C), gt[u].t[0:C, hd:hd + 1], ALU.mult, [cst, gt[u]], [gm[u]])
            ps = small()
            mm(ps.t[0:C, 0:C + 1], SGT(C), gm[u].t[0:C, 0:C + 1], [cst, gm[u]], [ps])
            act(E[u].t[0:C, 0:C + 1], ps.t[0:C, 0:C + 1], ACT.Exp, [ps], [E[u]])
        for u in units:
            c0, C = unit_cols(u)
            tt(Es[u].t[0:C, 0:C], E[u].t[0:C, 0:C], MSU(C), ALU.mult, [E[u], cst], [Es[u]])
            tt(Ei[u].t[0:C, 0:C], E[u].t[0:C, 0:C], MIU(C), ALU.mult, [E[u], cst], [Ei[u]])
            ps = small()
            P.group("pe", [
                lambda e, ps=ps, c0=c0, C=C: e.matmul(ps.t[0:C, 0:C], k_.t[:, c0:c0 + C], k_.t[:, c0:c0 + C],
                                                      start=True, stop=True),
                lambda e, ps=ps, c0=c0, C=C: e.matmul(ps.t[0:C, 64:64 + C], k_.t[:, c0:c0 + C], q_.t[:, c0:c0 + C],
                                                      start=True, stop=True)],
                [k_.b, q_.b], [ps.b])
            stt(LL[u].t[0:C, 0:C], ps.t[0:C, 0:C], bt[u].t[0:C, hd:hd + 1], Es[u].t[0:C, 0:C],
                ALU.mult, ALU.mult, [ps, bt[u], Es[u]], [LL[u]])
            tt(pmT[u].t[0:C, 0:C], ps.t[0:C, 64:64 + C], Ei[u].t[0:C, 0:C], ALU.mult, [ps, Ei[u]], [pmT[u]])
        for u in units:
            c0, C = unit_cols(u)
            ps = small()
            tr(ps.t[0:C, 0:C], LL[u].t[0:C, 0:C], cst.t[0:C, cfg.c_id:cfg.c_id + C], [LL[u], cst], [ps])
            act(LL[u].t[0:C, C:2 * C], ps.t[0:C, 0:C], ACT.Copy, [ps], [LL[u]])
            ps2 = small()
            pv2 = ps2.t[:, :].bitcast(BF16)
            tr(pv2[0:C, 0:128], v_.t[:, c0:c0 + C], idb.t[:, :], [v_, idb], [ps2])
            act(Vtm[u].t[0:C, :], pv2[0:C, 0:128], ACT.Copy, [ps2], [Vtm[u]])
            ps3 = small()
            pv3 = ps3.t[:, :].bitcast(BF16)
            tr(pv3[0:C, 0:128], k_.t[:, c0:c0 + C], idb.t[:, :], [k_, idb], [ps3])
            act(ke[u].t[0:C, :], pv3[0:C, 0:128], ACT.Copy, [ps3, E[u]], [ke[u]], scale=E[u].t[0:C, C:C + 1])
        def MMc(C, i):
            if C == 64:
                return cst.t[0:64, cfg.c_mm64 + i * 128: cfg.c_mm64 + (i + 1) * 128]
            return cst.t[0:4, cfg.c_mm4 + i * 8: cfg.c_mm4 + (i + 1) * 8]

        def IIc(C):
            if C == 64:
                return cst.t[0:64, cfg.c_ii64:cfg.c_ii64 + 128]
            return cst.t[0:4, cfg.c_ii4:cfg.c_ii4 + 8]
        tcur = {u: 0 for u in units}
        nlev = {u: (6 if unit_cols(u)[1] == 64 else 2) for u in units}
        for u in units:
            c0, C = unit_cols(u)
            tt(BBs[u].t[0:C, :], LL[u].t[0:C, :], MMc(C, 0), ALU.mult, [LL[u], cst], [BBs[u]])
            tt(TTp[u][0].t[0:C, :], IIc(C), BBs[u].t[0:C, :], ALU.subtract, [cst, BBs[u]], [TTp[u][0]])
        for lev in range(1, 6):
            for u in units:
                if lev >= nlev[u]:
                    continue
                c0, C = unit_cols(u)
                To = TTp[u][tcur[u]]
                Tn = TTp[u][1 - tcur[u]]
                tt(BBs[u].t[0:C, :], LL[u].t[0:C, :], MMc(C, lev), ALU.mult, [LL[u], cst], [BBs[u]])
                ps = small()
                P.group("pe", [
                    lambda e, ps=ps, C=C, u=u, To=To: e.matmul(ps.t[0:C, 0:C], BBs[u].t[0:C, C:2 * C],
                                                             To.t[0:C, 0:C], start=True, stop=True),
                    lambda e, ps=ps, C=C, u=u, To=To: e.matmul(ps.t[0:C, C:2 * C], BBs[u].t[0:C, 0:C],
                                                             To.t[0:C, C:2 * C], start=True, stop=True)],
                    [BBs[u].b, To.b], [ps.b])
                act(PQ[u].t[0:C, :], ps.t[0:C, 0:2 * C], ACT.Copy, [ps], [PQ[u]])
                ps2 = small()
                P.group("pe", [
                    lambda e, ps2=ps2, C=C, u=u, To=To: e.matmul(ps2.t[0:C, 0:C], To.t[0:C, C:2 * C],
                                                               PQ[u].t[0:C, 0:C], start=True, stop=True),
                    lambda e, ps2=ps2, C=C, u=u, To=To: e.matmul(ps2.t[0:C, C:2 * C], To.t[0:C, 0:C],
                                                               PQ[u].t[0:C, C:2 * C], start=True, stop=True)],
                    [PQ[u].b, To.b], [ps2.b])
                tt(Tn.t[0:C, :], To.t[0:C, :], ps2.t[0:C, 0:2 * C], ALU.subtract, [To, ps2], [Tn])
                tcur[u] = 1 - tcur[u]
        for u in units:
            c0, C = unit_cols(u)
            act(Ttb[u].t[0:C, 0:C], TTp[u][tcur[u]].t[0:C, 0:C], ACT.Copy, [TTp[u][tcur[u]]], [Ttb[u]])
        for u in units:
            c0, C = unit_cols(u)
            rr["rot"] += 1
            ri = rr["rot"] % NR
            Tt = Ttb[u]
            if u < NCH:
                S32t, Sbft = S32[hd], Sbf[hd]
            else:
                s = u - NCH
                S32t = Tile(Ss32[sset].t[:, s, :], Ss32[sset].b)
                Sbft = Ssbf[sset][s]
            psK = small()
            mm(psK.t[0:C, :], k_.t[:, c0:c0 + C], Sbft.t[:, :], [k_, Sbft], [psK])
            psQ = small()
            mm(psQ.t[0:C, :], q_.t[:, c0:c0 + C], Sbft.t[:, :], [q_, Sbft], [psQ])
            stt(Yt[ri].t[0:C, :], psK.t[0:C, :], neG[u].t[0:C, hd:hd + 1], Vtm[u].t[0:C, :],
                ALU.mult, ALU.add, [psK, neG[u], Vtm[u]], [Yt[ri]])
            act(o1s[ri].t[0:C, :], psQ.t[0:C, :], ACT.Copy, [psQ, eG[u]], [o1s[ri]], scale=eG[u].t[0:C, hd:hd + 1])
            psU = small()
            mm(psU.t[0:C, :], Tt.t[0:C, 0:C], Yt[ri].t[0:C, :], [Tt, Yt[ri]], [psU])
            act(ut[ri].t[0:C, :], psU.t[0:C, :], ACT.Copy, [psU, bt[u]], [ut[ri]], scale=bt[u].t[0:C, hd:hd + 1])
            psO = small()
            mm(psO.t[0:C, :], pmT[u].t[0:C, 0:C], ut[ri].t[0:C, :], [pmT[u], ut[ri]], [psO])
            tt(ot[ri].t[0:C, :], psO.t[0:C, :], o1s[ri].t[0:C, :], ALU.add, [psO, o1s[ri]], [ot[ri]])
            psD = small()
            mm(psD.t[:, :], ke[u].t[0:C, :], ut[ri].t[0:C, :], [ke[u], ut[ri]], [psD])
            if u < NCH:
                stt(S32t.t[:, :], S32t.t[:, :], geb[u].t[:, hd:hd + 1], psD.t[:, :], ALU.mult, ALU.add,
                    [S32t, geb[u], psD], [S32t])
                act(Sbft.t[:, :], S32t.t[:, :], ACT.Copy, [S32t], [Sbft])
            else:
                s = u - NCH
                sn = Ssn[0][s]
                stt(sn.t[:, :], S32t.t, geb[u].t[:, hd:hd + 1], psD.t[:, :], ALU.mult, ALU.add,
                    [S32t, geb[u], psD], [sn])
                out_ticks.append(sdma(o_dl_s[l, st * SS + s, hd], sn.t[:, :], f"ods{s}", [sn], []))
            act(junk[ri].t[0:C, :], ot[ri].t[0:C, :], ACT.Square, [ot[ri]], [junk[ri], ssq[ri]],
                accum=ssq[ri].t[0:C, 0:1])
            act(rsq[ri].t[0:C, :], ssq[ri].t[0:C, :], ACT.Sqrt, [ssq[ri]], [rsq[ri]], scale=1.0 / 128,
                bias=epsc.t[0:C, 0:1])
            recip(rsq[ri].t[0:C, :], rsq[ri].t[0:C, :], [rsq[ri]], [rsq[ri]])
            ts(onb[ri].t[0:C, :], ot[ri].t[0:C, :], rsq[ri].t[0:C, 0:1], ALU.mult, [ot[ri], rsq[ri]], [onb[ri]])
            psT = small()
            pvT = psT.t[:, :].bitcast(BF16)
            tr(pvT[:, 0:C], onb[ri].t[0:C, :], idb.t[0:C, 0:C], [onb[ri], idb], [psT])
            stt(yh[:, NG + hd, c0:c0 + C], pvT[:, 0:C], parcol(cfg.p_dnw), z_.t[:, c0:c0 + C],
                ALU.mult, ALU.mult, [psT, par, z_], [b_yh[NG + hd]])

    onec = P.tile("onec", [128, 1])
    A_("dve", lambda e: e.memset(onec.t[:, :], 1.0), [], [onec])

    def conv_apply(Ut, W, wcol0, out_t, reads_extra):
        hw = W - 1
        so = hw + TP
        sw = hw + 4
        Us = Ut.t[:, so:so + SS * sw].rearrange("p (s w) -> p s w", s=SS)
        outs = out_t.t[:, TP:NT].rearrange("p (s w) -> p s w", s=SS)
        act(out_t.t[:, 0:TP], Ut.t[:, 0:TP], ACT.Copy, [Ut, par], [out_t], scale=parcol(wcol0))
        act(outs, Us[:, :, 0:4], ACT.Copy, [Ut, par], [out_t], scale=parcol(wcol0))
        for i in range(1, W):
            stt(out_t.t[:, 0:TP], Ut.t[:, i:i + TP], parcol(wcol0 + i), out_t.t[:, 0:TP], ALU.mult, ALU.add,
                [Ut, par, out_t], [out_t])
            stt(outs, Us[:, :, i:i + 4], parcol(wcol0 + i), outs, ALU.mult, ALU.add, [Ut, par, out_t], [out_t])

    def evac_halo(pb, Ut, W):
        hw = W - 1
        so = hw + TP
        sw = hw + 4
        Us = Ut.t[:, so:so + SS * sw].rearrange("p (s w) -> p s w", s=SS)
        act(Ut.t[:, hw:hw + TP], pb.t[:, 0:TP], ACT.Copy, [pb], [Ut])
        act(Us[:, :, hw:hw + 4], pb.t[:, TP:NT].rearrange("p (s w) -> p s w", s=SS), ACT.Copy, [pb], [Ut])
        return Us

    def run_pass(l, st):
        src = xin[st] if l == 0 else xs[st]
        sdma(xT[:, :, :], src, "xld", ([b_xs[st]] if l > 0 else []), b_x)
        sset = st % 2
        norm_to_h(cfg.p_n1)
        P.dma("pool", lambda e: e.dma_start(out=wab_sb.t[:, :, :], in_=w_ab[l]), "wab", [], [wab_sb.b])
        ab_proj()
        for g in range(NG):
            sl, wv = wload(w_A[l, g], KC, 384)
            pbs = []
            for i in range(3):
                pb = big_mm(wv, KC, i * 128, hT, b_h, sl)
                if i == 0:
                    act(pj[0].t[:, :], pb.t[:, 0:NT], ACT.Copy, [pb], [pj[0]])
                elif i == 1:
                    act(pj[1].t[:, :], pb.t[:, 0:NT], ACT.Copy, [pb], [pj[1]])
                else:
                    Ut = U[0]
                    hw = 2
                    so = hw + TP
                    sw = hw + 4
                    Us = Ut.t[:, so:so + SS * sw].rearrange("p (s w) -> p s w", s=SS)
                    tt(Ut.t[:, hw:hw + TP], pb.t[:, 0:TP], pj[1].t[:, 0:TP], ALU.mult, [pb, pj[1]], [Ut])
                    tt(Us[:, :, hw:hw + 4], pb.t[:, TP:NT].rearrange("p (s w) -> p s w", s=SS),
                       pj[1].t[:, TP:NT].rearrange("p (s w) -> p s w", s=SS), ALU.mult, [pb, pj[1]], [Ut])
            Ut = U[0]
            if st == 0:
                A_("dve", lambda e, Ut=Ut: e.memset(Ut.t[:, 0:2], 0.0), [], [Ut])
            else:
                act(Ut.t[:, 0:2], haloA.t[:, g, :], ACT.Copy, [haloA], [Ut])
            sdma(Us[:, :, 0:2], s_ca[l, :, g, st * SS:(st + 1) * SS, :], "sca", [], [Ut])
            act(haloA.t[:, g, :], Ut.t[:, TP:TP + 2], ACT.Copy, [Ut], [haloA])
            out_ticks.append(sdma(o_ca_s[l, :, g, st * SS:(st + 1) * SS, :], Us[:, :, 4:6], "ocas", [Ut], []))
            conv_apply(Ut, 3, cfg.p_caw + g * 3, acc, [])
            tt(cv[0].t[:, :], pj[0].t[:, :], acc.t[:, :], ALU.mult, [pj[0], acc], [cv[0]])
            act(sq[0].t[:, :], cv[0].t[:, :], ACT.Square, [cv[0]], [sq[0]])
            pb = ones_sum([sq[0]])
            rstd_from(pb, 1.0 / 128, cv[1])
            stt(yh[:, g, 0:NT], cv[0].t[:, :], parcol(cfg.p_canw + g), cv[1].t[:, :], ALU.mult, ALU.mult,
                [cv[0], par, cv[1]], [b_yh[g]])
        if st == NST - 1:
            out_ticks.append(sdma(o_ca_p[l], haloA.t[:, :, :], "ocap", [haloA], []))
        for hd in range(NH):
            p2 = hd % 2
            sset = hd % 2
            sst = Ss32[sset]
            sdma(sst.t[:, :, :], s_dl[l, st * SS:(st + 1) * SS, hd].rearrange("s k v -> k s v"), f"sdl{sset}",
                 [], [sst])
            for s in range(SS):
                act(Ssbf[sset][s].t[:, :], sst.t[:, s, :], ACT.Copy, [sst], [Ssbf[sset][s]])
            if st == 0:
                A_("dve", lambda e, hd=hd: e.memset(S32[hd].t[:, :], 0.0), [], [S32[hd]])
                A_("dve", lambda e, hd=hd: e.memset(Sbf[hd].t[:, :], 0.0), [], [Sbf[hd]])
            sl, wv = wload(w_B[l, hd], KC, 512)
            for i in range(3):
                pb = big_mm(wv, KC, i * 128, hT, b_h, sl)
                Ut = U[i]
                Us = evac_halo(pb, Ut, 4)
                if st == 0:
                    A_("dve", lambda e, Ut=Ut: e.memset(Ut.t[:, 0:3], 0.0), [], [Ut])
                else:
                    act(Ut.t[:, 0:3], haloB.t[:, hd, i, :], ACT.Copy, [haloB], [Ut])
                sdma(Us[:, :, 0:3], s_cq[l, :, hd, i, st * SS:(st + 1) * SS, :], f"scq{i}", [], [Ut])
                act(haloB.t[:, hd, i, :], Ut.t[:, TP:TP + 3], ACT.Copy, [Ut], [haloB])
                out_ticks.append(sdma(o_cq_s[l, :, hd, i, st * SS:(st + 1) * SS, :], Us[:, :, 4:7], f"ocqs{i}",
                                      [Ut], []))
            pb = big_mm(wv, KC, 3 * 128, hT, b_h, sl)
            act(zs[p2].t[:, :], pb.t[:, 0:NT], ACT.Silu, [pb], [zs[p2]])
            for i in range(3):
                conv_apply(U[i], 4, cfg.p_cbw + (hd * 3 + i) * 4, acc, [])
                if i == 2:
                    act(vb[p2].t[:, :], acc.t[:, :], ACT.Silu, [acc], [vb[p2]])
                else:
                    act(cv[i].t[:, :], acc.t[:, :], ACT.Silu, [acc], [cv[i]])
            for i in range(2):
                act(sq[i].t[:, :], cv[i].t[:, :], ACT.Square, [cv[i]], [sq[i]])
                pb = ones_sum([sq[i]])
                rstd_from(pb, 1.0, pj[i])
                if i == 0:
                    stt(qn[p2].t[:, :], cv[0].t[:, :], float(128 ** -0.5), pj[0].t[:, :], ALU.mult, ALU.mult,
                        [cv[0], pj[0]], [qn[p2]])
                else:
                    tt(kn[p2].t[:, :], cv[1].t[:, :], pj[1].t[:, :], ALU.mult, [cv[1], pj[1]], [kn[p2]])
            delta_head(hd, p2, st, l, hd % 2)
            if st == NST - 1:
                out_ticks.append(sdma(o_dl_p[l, hd], S32[hd].t[:, :], "odp", [S32[hd]], []))
        if st == NST - 1:
            out_ticks.append(sdma(o_cq_p[l], haloB.t[:, :, :, :], "ocqp", [haloB], []))
        for ot_ in range(cfg.NOT):
            sl, wv = wload(w_o[l, ot_], YC, cfg.CWO)
            for j in range(cfg.CWO // 128):
                oc = ot_ * (cfg.CWO // 128) + j
                pb = big_mm(wv, YC, j * 128, yh, b_yh[0:YC], sl)
                tt(xT[:, oc, :], pb.t[:, 0:NT], xT[:, oc, :], ALU.add, [pb, b_x[oc]], [b_x[oc]])
        norm_to_h(cfg.p_n2)
        for hg in range(cfg.NHG):
            for ut_ in range(cfg.NUT):
                sl, wv = wload(w_u[l, hg * cfg.NUT + ut_], KC, 512)
                for j in range(4):
                    hc = ut_ * 4 + j
                    pb = big_mm(wv, KC, j * 128, hT, b_h, sl)
                    act(tmpN.t[:, :], pb.t[:, 0:NT], ACT.Relu, [pb], [tmpN])
                    tt(yh[:, hc, 0:NT], tmpN.t[:, :], tmpN.t[:, :], ALU.mult, [tmpN], [b_yh[hc]])
            for ot_ in range(cfg.NOT):
                sl, wv = wload(w_d[l, hg, ot_], cfg.HK, cfg.CWO)
                for j in range(cfg.CWO // 128):
                    oc = ot_ * (cfg.CWO // 128) + j
                    pb = big_mm(wv, cfg.HK, j * 128, yh, b_yh[0:cfg.HK], sl)
                    tt(xT[:, oc, :], pb.t[:, 0:NT], xT[:, oc, :], ALU.add, [pb, b_x[oc]], [b_x[oc]])
        if l == L - 1:
            pb = big()
            for k in range(KC):
                s = sq[k % 2]
                act(s.t[:, :], xT[:, k, :], ACT.Square, [b_x[k]], [s])
                for (a, b) in TTS:
                    P.op("pe", lambda e, k=k, s=s, a=a, b=b, pb=pb: e.matmul(
                        pb.t[:, a:b], onesb.t[:, :], s.t[:, a:b], start=(k == 0), stop=(k == KC - 1)),
                        [onesb.b, s.b], [pb.b], signal=True)
            rstd_from(pb, 1.0 / D, rstdB)
            for k in range(KC):
                stt(xT[:, k, :], xT[:, k, :], fnw.t[:, k:k + 1], rstdB.t[:, :], ALU.mult, ALU.mult,
                    [b_x[k], fnw, rstdB], [b_x[k]])
            out_ticks.append(sdma(yout[st], xT[:, :, :], "xst", b_x, []))
        else:
            sdma(xs[st], xT[:, :, :], "xst", b_x, [b_xs[st]])

    for l in range(L):
        sdma(par.t[:, :], par_d[l], "par", [], [par])
        act(negA.t[:, :], par.t[:, cfg.p_alog:cfg.p_alog + NH], ACT.Exp, [par], [negA])
        ts(negA.t[:, :], negA.t[:, :], -1.0, ALU.mult, [negA], [negA])
        for st in range(NST):
            run_pass(l, st)
    P.wait_all("sp", out_ticks)
    P.finish()
    return nc


def prep_weights(cfg, inp):
    D, L, KC, NG, NH = cfg.D, cfg.L, cfg.KC, cfg.NG, cfg.NH
    DC = cfg.DC
    w_in = np.asarray(inp["w_in"], np.float32)

    def ktile(w):
        c = w.shape[-1]
        return np.ascontiguousarray(w.reshape(L, -1, 128, c).transpose(0, 2, 1, 3))
    oB, oC, oH, oQ, oZ, oa = 0, DC, 2 * DC, 3 * DC, 6 * DC, 7 * DC
    w_ab = ktile(w_in[:, :, oa:oa + 2 * NH])
    wA = np.empty((L, NG, 128, KC, 384), np.float32)
    for g in range(NG):
        cols = np.concatenate([np.arange(o + g * 128, o + (g + 1) * 128) for o in (oB, oC, oH)])
        wA[:, g] = ktile(w_in[:, :, cols])
    wB = np.empty((L, NH, 128, KC, 512), np.float32)
    for h in range(NH):
        cols = np.concatenate([np.arange(o + h * 128, o + (h + 1) * 128)
                               for o in (oQ, oQ + DC, oQ + 2 * DC, oZ)])
        wB[:, h] = ktile(w_in[:, :, cols])
    w_out = np.asarray(inp["w_out"], np.float32)
    wo = np.stack([ktile(w_out[:, :, t * cfg.CWO:(t + 1) * cfg.CWO]) for t in range(cfg.NOT)], 1)
    w_up = np.asarray(inp["w_up"], np.float32)
    wu = np.stack([ktile(w_up[:, :, t * 512:(t + 1) * 512]) for t in range(cfg.DFF // 512)], 1)
    w_dn = np.asarray(inp["w_down"], np.float32)
    wd = np.empty((L, cfg.NHG, cfg.NOT, 128, cfg.HK, cfg.CWO), np.float32)
    for hg in range(cfg.NHG):
        for t in range(cfg.NOT):
            wd[:, hg, t] = ktile(w_dn[:, hg * cfg.HGS:(hg + 1) * cfg.HGS, t * cfg.CWO:(t + 1) * cfg.CWO])
    par = np.zeros((L, 128, cfg.NPAR), np.float32)
    fm = lambda v: v.reshape(L, -1, 128).transpose(0, 2, 1)
    par[:, :, cfg.p_n1:cfg.p_n1 + KC] = fm(np.asarray(inp["norm_mix_w"]))
    par[:, :, cfg.p_n2:cfg.p_n2 + KC] = fm(np.asarray(inp["norm_ffn_w"]))
    caw = np.asarray(inp["conv_a_w"]).reshape(L, 3, NG, 128).transpose(0, 3, 2, 1)
    par[:, :, cfg.p_caw:cfg.p_caw + NG * 3] = caw.reshape(L, 128, NG * 3)
    par[:, :, cfg.p_canw:cfg.p_canw + NG] = fm(np.asarray(inp["conv_a_norm_w"]))
    cbw = np.asarray(inp["conv_qkv_w"]).reshape(L, 4, 3, NH, 128).transpose(0, 4, 3, 2, 1)
    par[:, :, cfg.p_cbw:cfg.p_cbw + NH * 12] = cbw.reshape(L, 128, NH * 12)
    par[:, :, cfg.p_dnw] = np.asarray(inp["dn_norm_w"])
    par[:, :, cfg.p_dtb:cfg.p_dtb + NH] = np.asarray(inp["dt_bias"])[:, None, :]
    par[:, :, cfg.p_alog:cfg.p_alog + NH] = np.asarray(inp["a_log"])[:, None, :]
    fnw = np.ascontiguousarray(np.asarray(inp["final_norm_w"], np.float32).reshape(-1, 128).T)
    return dict(w_ab=w_ab, w_A=wA, w_B=wB, w_o=wo, w_u=wu, w_d=wd, par=par, fnw=fnw,
                consts=make_consts(cfg))


def prep_core(cfg, inp, xp_seq, seq0):
    D, L, KC, NG, NH, NST, TP, SS, NT = cfg.D, cfg.L, cfg.KC, cfg.NG, cfg.NH, cfg.NST, cfg.TP, cfg.SS, cfg.NT
    NSEQ = cfg.NSEQ
    xsamp = np.asarray(inp["x_sample"], np.float32)[seq0:seq0 + NSEQ]
    xin = np.empty((NST, 128, KC, NT), np.float32)
    for st in range(NST):
        tok = np.concatenate([xp_seq[st * TP:(st + 1) * TP], xsamp[st * SS:(st + 1) * SS].reshape(-1, D)], 0)
        xin[st] = tok.T.reshape(KC, 128, NT).transpose(1, 0, 2)
    sca = np.asarray(inp["state_conv_a"], np.float32)[:, seq0:seq0 + NSEQ]
    s_ca = np.ascontiguousarray(sca.reshape(L, NSEQ, 2, NG, 128).transpose(0, 4, 3, 1, 2))
    scq = np.asarray(inp["state_conv_qkv"], np.float32)[:, seq0:seq0 + NSEQ]
    s_cq = np.ascontiguousarray(scq.reshape(L, NSEQ, 3, 3, NH, 128).transpose(0, 5, 4, 3, 1, 2))
    sdl = np.asarray(inp["state_delta"], np.float32)[:, seq0:seq0 + NSEQ]
    s_dl = np.ascontiguousarray(sdl.transpose(0, 1, 2, 4, 3))
    return dict(xin=xin, s_ca=s_ca, s_cq=s_cq, s_dl=s_dl)


_CACHE = {}


def kernel(**inputs):
    cfg = Cfg()
    NCORES = 8
    if "nc" not in _CACHE:
        _CACHE["nc"] = build_program(cfg)
    nc = _CACHE["nc"]
    wts = prep_weights(cfg, inputs)
    xp = np.asarray(inputs["x_prompt"], np.float32)
    B, S, D = xp.shape
    in_maps = []
    for c in range(NCORES):
        xseq = xp[c] if c < B else np.zeros((S, D), np.float32)
        m = dict(wts)
        m.update(prep_core(cfg, inputs, xseq, c * cfg.NSEQ))
        in_maps.append(m)
    res = run_bass_kernel_spmd(nc, in_maps, core_ids=list(range(NCORES)))
    return assemble(cfg, res.results, B)


def assemble(cfg, results, B):
    D, L, KC, NG, NH, NST, TP, SS, NT, NSEQ = (cfg.D, cfg.L, cfg.KC, cfg.NG, cfg.NH, cfg.NST, cfg.TP,
                                                 cfg.SS, cfg.NT, cfg.NSEQ)
    DC = cfg.DC
    ncores = len(results)
    y_p = np.empty((B, NST * TP, D), np.float32)
    y_s = np.empty((ncores * NSEQ, 4, D), np.float32)
    ca_p = np.empty((L, B, 2, DC), np.float32)
    cq_p = np.empty((L, B, 3, 3 * DC), np.float32)
    dl_p = np.empty((L, B, NH, 128, 128), np.float32)
    ca_s = np.empty((L, ncores * NSEQ, 2, DC), np.float32)
    cq_s = np.empty((L, ncores * NSEQ, 3, 3 * DC), np.float32)
    dl_s = np.empty((L, ncores * NSEQ, NH, 128, 128), np.float32)
    for c, r in enumerate(results):
        yo = np.asarray(r["yout"])
        tok = yo.transpose(0, 3, 2, 1).reshape(NST, NT, D)
        for st in range(NST):
            if c < B:
                y_p[c, st * TP:(st + 1) * TP] = tok[st, :TP]
            y_s[c * NSEQ + st * SS: c * NSEQ + (st + 1) * SS] = tok[st, TP:].reshape(SS, 4, D)
        sl = slice(c * NSEQ, (c + 1) * NSEQ)
        ca_s[:, sl] = np.asarray(r["o_ca_s"]).transpose(0, 3, 4, 2, 1).reshape(L, NSEQ, 2, DC)
        cq_s[:, sl] = np.asarray(r["o_cq_s"]).transpose(0, 4, 5, 3, 2, 1).reshape(L, NSEQ, 3, 3 * DC)
        dl_s[:, sl] = np.asarray(r["o_dl_s"]).transpose(0, 1, 2, 4, 3)
        if c < B:
            ca_p[:, c] = np.asarray(r["o_ca_p"]).transpose(0, 3, 2, 1).reshape(L, 2, DC)
            cq_p[:, c] = np.asarray(r["o_cq_p"]).transpose(0, 4, 3, 2, 1).reshape(L, 3, 3 * DC)
            dl_p[:, c] = np.asarray(r["o_dl_p"]).transpose(0, 1, 3, 2)
    return (y_p, y_s, ca_p, cq_p, dl_p, ca_s, cq_s, dl_s)
```

```python
import contextlib
import numpy as np
import concourse.bass as bass
import concourse.mybir as mybir
from concourse.bass_utils import run_bass_kernel_spmd

DT = mybir.dt
F32 = DT.float32
BF16 = DT.bfloat16
ACT = mybir.ActivationFunctionType
ALU = mybir.AluOpType
EPS = 1e-6

ENGS = ["pe", "act", "dve", "pool", "sp"]
EPOCH = 16000


class Buf:
    __slots__ = ("name", "w", "r")

    def __init__(self, name):
        self.name = name
        self.w = None
        self.r = []


class Tile:
    def __init__(self, t, b):
        self.t = t
        self.b = b


class Prog:
    def __init__(self, nc):
        self.nc = nc
        self.q = {e: [] for e in ENGS}
        self.cnt = {}
        self.seen = {e: {} for e in ENGS}
        self.ecnt = {e: 0 for e in ENGS}
        self.dma_last = {}
        self.stack = contextlib.ExitStack()
        self.nbuf = 0

    def sbuf(self, name, shape, dtype=F32):
        return self.stack.enter_context(self.nc.sbuf_tensor("sb_" + name, list(shape), dtype))

    def psum(self, name, shape, dtype=F32):
        return self.stack.enter_context(self.nc.psum_tensor("ps_" + name, list(shape), dtype))

    def buf(self, name=None):
        self.nbuf += 1
        return Buf(name or f"b{self.nbuf}")

    def tile(self, name, shape, dtype=F32):
        return Tile(self.sbuf(name, shape, dtype), self.buf(name))

    def _waits(self, eng, reads, writes):
        need = {}

        def add(t):
            if t is None:
                return
            k, v = t
            if need.get(k, 0) < v:
                need[k] = v
        for b in reads:
            add(b.w)
        for b in writes:
            add(b.w)
            for t in b.r:
                add(t)
        out = []
        seen = self.seen[eng]
        for k, v in need.items():
            if seen.get(k, 0) < v:
                seen[k] = v
                out.append((k, v))
        return out

    def op(self, eng, fn, reads=(), writes=(), signal=True):
        waits = self._waits(eng, reads, writes)
        tick = None
        if signal:
            self.ecnt[eng] += 1
            key = (eng, self.ecnt[eng] // EPOCH)
            self.cnt[key] = self.cnt.get(key, 0) + 1
            tick = (key, self.cnt[key])
        self.q[eng].append((waits, fn, tick, 1))
        if tick is not None:
            for b in reads:
                if len(b.r) > 24:
                    b.r = b.r[-24:] if False else b.r
                b.r.append(tick)
            for b in writes:
                b.w = tick
                b.r = []
        return tick

    def group(self, eng, fns, reads=(), writes=()):
        n = len(fns)
        for i, fn in enumerate(fns):
            if i == n - 1:
                return self.op(eng, fn, reads, writes, signal=True)
            if i == 0:
                self.op(eng, fn, reads, writes, signal=False)
            else:
                self.op(eng, fn, (), (), signal=False)

    def dma(self, eng, fn, semkey, reads=(), writes=()):
        key = ("dma", semkey)
        waits = self._waits(eng, reads, writes)
        prev = self.dma_last.get(key)
        if prev is not None and self.seen[eng].get(key, 0) < prev[1]:
            self.seen[eng][key] = prev[1]
            waits.append(prev)
        self.cnt[key] = self.cnt.get(key, 0) + 16
        tick = (key, self.cnt[key])
        self.dma_last[key] = tick
        self.q[eng].append((waits, fn, tick, 16))
        for b in reads:
            b.r.append(tick)
        for b in writes:
            b.w = tick
            b.r = []
        return tick

    def wait_all(self, eng, ticks):
        waits = []
        for t in ticks:
            if t is None:
                continue
            k, v = t
            if self.seen[eng].get(k, 0) < v:
                self.seen[eng][k] = v
                waits.append((k, v))
        self.q[eng].append((waits, None, None, 0))

    def finish(self):
        nc = self.nc
        sems = {}
        for i, k in enumerate(self.cnt):
            sems[k] = self.stack.enter_context(nc.semaphore(f"s{i}"))
        engobj = {"pe": "tensor", "act": "scalar", "dve": "vector", "pool": "gpsimd", "sp": "sync"}
        with nc.Block() as block:
            for e in ENGS:
                items = self.q[e]
                if not items:
                    continue

                def body(engine, items=items):
                    for waits, fn, tick, n in items:
                        for k, v in waits:
                            engine.wait_ge(sems[k], v)
                        if fn is None:
                            continue
                        ins = fn(engine)
                        if tick is not None:
                            ins.then_inc(sems[tick[0]], n)
                getattr(block, engobj[e])(body)
        self.stack.close()


class Cfg:
    def __init__(self, D=2048, L=4, NST=4, TP=512, SS=4):
        self.D = D
        self.L = L
        self.NST = NST
        self.TP = TP
        self.SS = SS
        self.KC = D // 128
        self.DC = D // 2
        self.NG = self.DC // 128
        self.NH = self.DC // 128
        self.DFF = 4 * D
        self.NTS = SS * 4
        self.NT = TP + self.NTS
        self.NCH = TP // 64
        self.NSEQ = NST * SS
        self.CWO = min(512, D)
        self.NOT = D // self.CWO
        self.HGS = min(2048, self.DFF)
        self.NHG = self.DFF // self.HGS
        self.HK = self.HGS // 128
        self.NUT = self.HGS // 512
        self.YC = D // 128
        o = 0
        self.p_n1 = o; o += self.KC
        self.p_n2 = o; o += self.KC
        self.p_caw = o; o += self.NG * 3
        self.p_canw = o; o += self.NG
        self.p_cbw = o; o += self.NH * 3 * 4
        self.p_dnw = o; o += 1
        self.p_dtb = o; o += self.NH
        self.p_alog = o; o += self.NH
        self.NPAR = o
        o = 0
        self.c_id = o; o += 128
        self.c_ones = o; o += 128
        self.c_trili = o; o += 65
        self.c_sgt = o; o += 64
        self.c_msu = o; o += 64
        self.c_miu = o; o += 64
        self.c_trili4 = o; o += 5
        self.c_mm64 = o; o += 6 * 128
        self.c_ii64 = o; o += 128
        self.c_mm4 = o; o += 2 * 8
        self.c_ii4 = o; o += 8
        self.NCONST = o
        self.WSLOT = 16 * 512


def make_consts(cfg):
    c = np.zeros((128, cfg.NCONST), np.float32)
    c[:, cfg.c_id:cfg.c_id + 128] = np.eye(128, dtype=np.float32)
    c[:, cfg.c_ones:cfg.c_ones + 128] = 1.0
    j = np.arange(64)[:, None]
    m = np.arange(64)[None, :]
    c[:64, cfg.c_trili:cfg.c_trili + 64] = (j <= m)
    c[:64, cfg.c_trili + 64] = 1.0
    c[:64, cfg.c_sgt:cfg.c_sgt + 64] = (j > m)
    c[:64, cfg.c_msu:cfg.c_msu + 64] = (m > j)
    c[:64, cfg.c_miu:cfg.c_miu + 64] = (m >= j)
    c[:4, cfg.c_trili4:cfg.c_trili4 + 4] = (j[:4] <= m[:, :4])
    c[:4, cfg.c_trili4 + 4] = 1.0
    mi = np.arange(64)[:, None]
    ci = np.arange(64)[None, :]
    for i in range(6):
        sz = 2 ** i
        M = ((mi // (2 * sz)) == (ci // (2 * sz))) & ((mi // sz) % 2 == 0) & ((ci // sz) % 2 == 1)
        M = M.astype(np.float32)
        c[:64, cfg.c_mm64 + i * 128: cfg.c_mm64 + i * 128 + 64] = M
        c[:64, cfg.c_mm64 + i * 128 + 64: cfg.c_mm64 + (i + 1) * 128] = M.T
        if i < 2:
            c[:4, cfg.c_mm4 + i * 8: cfg.c_mm4 + i * 8 + 4] = M[:4, :4]
            c[:4, cfg.c_mm4 + i * 8 + 4: cfg.c_mm4 + (i + 1) * 8] = M[:4, :4].T
    c[:64, cfg.c_ii64:cfg.c_ii64 + 64] = np.eye(64)
    c[:64, cfg.c_ii64 + 64:cfg.c_ii64 + 128] = np.eye(64)
    c[:4, cfg.c_ii4:cfg.c_ii4 + 4] = np.eye(4)
    c[:4, cfg.c_ii4 + 4:cfg.c_ii4 + 8] = np.eye(4)
    return c


def build_program(cfg):
    nc = bass.Bass("TRN2", target_bir_lowering=False)
    P = Prog(nc)
    D, L, NST, TP, SS, KC, NG, NH = cfg.D, cfg.L, cfg.NST, cfg.TP, cfg.SS, cfg.KC, cfg.NG, cfg.NH
    NT, NTS, NCH, NSEQ = cfg.NT, cfg.NTS, cfg.NCH, cfg.NSEQ
    YC = cfg.YC

    def din(name, shape):
        return nc.dram_tensor(name, list(shape), F32, kind="ExternalInput").ap()

    def dout(name, shape):
        return nc.dram_tensor(name, list(shape), F32, kind="ExternalOutput").ap()

    xin = din("xin", [NST, 128, KC, NT])
    w_ab = din("w_ab", [L, 128, KC, 2 * NH])
    w_A = din("w_A", [L, NG, 128, KC, 384])
    w_B = din("w_B", [L, NH, 128, KC, 512])
    w_o = din("w_o", [L, cfg.NOT, 128, YC, cfg.CWO])
    w_u = din("w_u", [L, cfg.NHG * cfg.NUT, 128, KC, 512])
    w_d = din("w_d", [L, cfg.NHG, cfg.NOT, 128, cfg.HK, cfg.CWO])
    par_d = din("par", [L, 128, cfg.NPAR])
    fnw_d = din("fnw", [128, KC])
    const_d = din("consts", [128, cfg.NCONST])
    s_ca = din("s_ca", [L, 128, NG, NSEQ, 2])
    s_cq = din("s_cq", [L, 128, NH, 3, NSEQ, 3])
    s_dl = din("s_dl", [L, NSEQ, NH, 128, 128])
    yout = dout("yout", [NST, 128, KC, NT])
    o_ca_p = dout("o_ca_p", [L, 128, NG, 2])
    o_cq_p = dout("o_cq_p", [L, 128, NH, 3, 3])
    o_dl_p = dout("o_dl_p", [L, NH, 128, 128])
    o_ca_s = dout("o_ca_s", [L, 128, NG, NSEQ, 2])
    o_cq_s = dout("o_cq_s", [L, 128, NH, 3, NSEQ, 3])
    o_dl_s = dout("o_dl_s", [L, NSEQ, NH, 128, 128])
    xs = nc.dram_tensor("xs", [NST, 128, KC, NT], F32).ap()
    b_xs = [P.buf(f"xs{i}") for i in range(NST)]
    out_ticks = []

    xT = P.sbuf("xT", [128, KC, NT]); b_x = [P.buf(f"x{k}") for k in range(KC)]
    hT = P.sbuf("hT", [128, KC, NT], BF16); b_h = [P.buf(f"h{k}") for k in range(KC)]
    NYH = max(YC, cfg.HK)
    yh = P.sbuf("yh", [128, NYH, NT], BF16); b_yh = [P.buf(f"yh{k}") for k in range(NYH)]
    NWS = 3
    wsl = [P.tile(f"ws{i}", [128, cfg.WSLOT], BF16) for i in range(NWS)]
    wab_sb = P.tile("wab", [128, KC, 2 * NH], BF16)
    cst = P.tile("cst", [128, cfg.NCONST])
    idb = P.tile("idb", [128, 128], BF16)
    onesb = P.tile("onesb", [128, 128], BF16)
    par = P.tile("par", [128, cfg.NPAR])
    fnw = P.tile("fnw", [128, KC])
    negA = P.tile("negA", [128, NH])
    sq = [P.tile(f"sq{i}", [128, NT], BF16) for i in range(2)]
    pj = [P.tile(f"pj{i}", [128, NT]) for i in range(2)]
    rstdB = pj[0]
    WU = TP + 3 + 7 * SS
    U = [P.tile(f"U{i}", [128, WU]) for i in range(3)]
    acc = P.tile("acc", [128, NT])
    tmpN = acc
    cvp = [[P.tile(f"cv{p}_{i}", [128, NT]) for i in range(2)] for p in range(2)]
    cv = cvp[0]
    sqp = [[P.tile(f"sqp{p}_{i}", [128, NT], BF16) for i in range(2)] for p in range(2)]
    zs = [P.tile(f"zs{i}", [128, NT]) for i in range(2)]
    qn = [P.tile(f"qn{i}", [128, NT], BF16) for i in range(2)]
    kn = [P.tile(f"kn{i}", [128, NT], BF16) for i in range(2)]
    vb = [P.tile(f"vb{i}", [128, NT], BF16) for i in range(2)]
    haloA = P.tile("haloA", [128, NG, 2])
    haloB = P.tile("haloB", [128, NH, 3, 3])
    S32 = [P.tile(f"S32_{h}", [128, 128]) for h in range(NH)]
    Sbf = [P.tile(f"Sbf_{h}", [128, 128], BF16) for h in range(NH)]
    Ss32 = [P.tile(f"Ss32_{i}", [128, SS, 128]) for i in range(2)]
    Ssbf = [[P.tile(f"Ssbf_{i}_{s}", [128, 128], BF16) for s in range(SS)] for i in range(2)]
    Ssn = [[P.tile(f"Ssn_{i}_{s}", [128, 128]) for s in range(SS)] for i in range(1)]
    NU = NCH + SS
    UC = [64 if u < NCH else 4 for u in range(NU)]
    gt = [P.tile(f"g{u}", [UC[u], NH]) for u in range(NU)]
    bt = [P.tile(f"bt{u}", [UC[u], NH]) for u in range(NU)]
    t1 = [P.tile(f"t1_{u}", [UC[u], NH]) for u in range(NU)]
    eG = [P.tile(f"eG{u}", [UC[u], NH]) for u in range(NU)]
    neG = [P.tile(f"neG{u}", [UC[u], NH]) for u in range(NU)]
    geb = [P.tile(f"geb{u}", [128, NH]) for u in range(NU)]
    _gm = [P.tile(f"gm{i}", [64, 65]) for i in range(2)]
    gm = [_gm[u % 2] for u in range(NU)]
    E = [P.tile(f"E{u}", [UC[u], UC[u] + 1]) for u in range(NU)]
    _Es = [P.tile(f"Es{i}", [64, 64]) for i in range(2)]
    _Ei = [P.tile(f"Ei{i}", [64, 64]) for i in range(2)]
    Es = [_Es[u % 2] for u in range(NU)]
    Ei = [_Ei[u % 2] for u in range(NU)]
    LL = [P.tile(f"LL{u}", [UC[u], 2 * UC[u]]) for u in range(NU)]
    IB = 4
    _BBs = [P.tile(f"BBs{i}", [64, 128]) for i in range(IB)] + [P.tile(f"BBs4_{i}", [4, 8]) for i in range(IB)]
    _PQ = [P.tile(f"PQ{i}", [64, 128]) for i in range(IB)] + [P.tile(f"PQ4_{i}", [4, 8]) for i in range(IB)]
    _TT = [[P.tile(f"TT{i}_{j}", [64, 128]) for j in range(2)] for i in range(IB)] + \
          [[P.tile(f"TT4_{i}_{j}", [4, 8]) for j in range(2)] for i in range(IB)]
    _slot = lambda u: (u % IB) if u < NCH else IB + ((u - NCH) % IB)
    BBs = [_BBs[_slot(u)] for u in range(NU)]
    PQ = [_PQ[_slot(u)] for u in range(NU)]
    TTp = [_TT[_slot(u)] for u in range(NU)]
    Ttb = [P.tile(f"Ttb{u}", [UC[u], UC[u]], BF16) for u in range(NU)]
    pmT = [P.tile(f"pmT{u}", [UC[u], UC[u]], BF16) for u in range(NU)]
    Vtm = [P.tile(f"Vtm{u}", [UC[u], 128], BF16) for u in range(NU)]
    ke = [P.tile(f"ke{u}", [UC[u], 128], BF16) for u in range(NU)]
    NR = 2
    Yt = [P.tile(f"Y{i}", [64, 128], BF16) for i in range(NR)]
    ut = [P.tile(f"u{i}", [64, 128], BF16) for i in range(NR)]
    o1s = [P.tile(f"o1s{i}", [64, 128]) for i in range(NR)]
    ot = [P.tile(f"o{i}", [64, 128]) for i in range(NR)]
    junk = o1s
    ssq = [P.tile(f"ssq{i}", [64, 1]) for i in range(NR)]
    rsq = [P.tile(f"rsq{i}", [64, 1]) for i in range(NR)]
    onb = [P.tile(f"onb{i}", [64, 128], BF16) for i in range(NR)]
    NBIG = 2
    pbig = [Tile(P.psum(f"pb{i}", [128, 1024]), P.buf(f"pb{i}")) for i in range(NBIG)]
    NSM = 4
    psm = [Tile(P.psum(f"psmb{i}", [128, 512])[:, 0:128], P.buf(f"psm{i}")) for i in range(NSM)]
    rr = {"big": 0, "sm": 0, "ws": 0, "rot": 0}

    def big():
        rr["big"] += 1
        return pbig[rr["big"] % NBIG]

    def small():
        rr["sm"] += 1
        return psm[rr["sm"] % NSM]

    def A_(eng, fn, reads=(), writes=()):
        return P.op(eng, fn, [t.b if isinstance(t, Tile) else t for t in reads],
                    [t.b if isinstance(t, Tile) else t for t in writes])

    def act(out, in_, func, reads, writes, scale=None, bias=None, accum=None):
        kw = {}
        if scale is not None:
            kw["scale"] = scale
        if bias is not None:
            kw["bias"] = bias
        if accum is not None:
            kw["accum_out"] = accum
        return A_("act", lambda e: e.activation(out=out, in_=in_, func=func, **kw), reads, writes)

    def tt(out, in0, in1, op, reads, writes):
        return A_("dve", lambda e: e.tensor_tensor(out=out, in0=in0, in1=in1, op=op), reads, writes)

    def stt(out, in0, scalar, in1, op0, op1, reads, writes):
        return A_("dve", lambda e: e.scalar_tensor_tensor(out=out, in0=in0, scalar=scalar, in1=in1,
                                                          op0=op0, op1=op1), reads, writes)

    def ts(out, in0, s1, op0, reads, writes, s2=None, op1=None):
        if op1 is None:
            return A_("dve", lambda e: e.tensor_scalar(out=out, in0=in0, scalar1=s1, scalar2=None, op0=op0),
                      reads, writes)
        return A_("dve", lambda e: e.tensor_scalar(out=out, in0=in0, scalar1=s1, scalar2=s2, op0=op0, op1=op1),
                  reads, writes)

    def recip(out, in_, reads, writes):
        return A_("dve", lambda e: e.reciprocal(out=out, in_=in_), reads, writes)

    def mm(out, lhsT, rhs, reads, writes, start=True, stop=True):
        return A_("pe", lambda e: e.matmul(out, lhsT, rhs, start=start, stop=stop), reads, writes)

    def tr(out, in_, ident, reads, writes):
        return A_("pe", lambda e: e.transpose(out, in_, ident), reads, writes)

    def sdma(out, in_, key, reads=(), writes=()):
        return P.dma("sp", lambda e: e.dma_start(out=out, in_=in_), key,
                     [t.b if isinstance(t, Tile) else t for t in reads],
                     [t.b if isinstance(t, Tile) else t for t in writes])

    def wload(src_ap, kch, cols):
        rr["ws"] += 1
        sl = wsl[rr["ws"] % NWS]
        i = rr["ws"] % NWS
        view = sl.t[:, 0:kch * cols].rearrange("p (k c) -> p k c", k=kch)
        P.dma("pool", lambda e: e.dma_start(out=view, in_=src_ap), f"w{i}", [], [sl.b])
        return sl, view

    def parcol(c0, n=1):
        return par.t[:, c0:c0 + n]

    TTS = [(0, min(512, NT))]
    if NT > 512:
        TTS.append((512, NT))

    def big_mm(wview, kch, col0, rhs_t, rhs_bufs, wslot):
        pb = big()
        fns = []
        for k in range(kch):
            for (a, b) in TTS:
                fns.append(lambda e, k=k, a=a, b=b: e.matmul(
                    pb.t[:, a:b], wview[:, k, col0:col0 + 128], rhs_t[:, k, a:b],
                    start=(k == 0), stop=(k == kch - 1)))
        P.group("pe", fns, [wslot.b] + list(rhs_bufs), [pb.b])
        return pb

    def ones_sum(src_tiles, nparts=128):
        pb = big()
        n = len(src_tiles)
        fns = []
        for i, s in enumerate(src_tiles):
            for (a, b) in TTS:
                fns.append(lambda e, i=i, s=s, a=a, b=b: e.matmul(
                    pb.t[:, a:b], onesb.t[:, :], s.t[:, a:b], start=(i == 0), stop=(i == n - 1)))
        P.group("pe", fns, [onesb.b] + [s.b for s in src_tiles], [pb.b])
        return pb

    def rstd_from(pb, scale, out_tile):
        act(tmpN.t[:, :], pb.t[:, 0:NT], ACT.Sqrt, [pb], [tmpN], scale=scale, bias=epsc.t[:, 0:1])
        recip(out_tile.t[:, :], tmpN.t[:, :], [tmpN], [out_tile])

    epsc = P.tile("epsc", [128, 1])
    A_("dve", lambda e: e.memset(epsc.t[:, :], EPS), [], [epsc])
    sdma(cst.t[:, :], const_d, "c0", [], [cst])
    sdma(fnw.t[:, :], fnw_d, "c1", [], [fnw])
    A_("dve", lambda e: e.tensor_copy(out=idb.t[:, :], in_=cst.t[:, cfg.c_id:cfg.c_id + 128]), [cst], [idb])
    A_("dve", lambda e: e.tensor_copy(out=onesb.t[:, :], in_=cst.t[:, cfg.c_ones:cfg.c_ones + 128]), [cst], [onesb])
    TRILI = lambda C: (cst.t[0:C, cfg.c_trili:cfg.c_trili + 65] if C == 64
                       else cst.t[0:C, cfg.c_trili4:cfg.c_trili4 + C + 1])
    SGT = lambda C: cst.t[0:C, cfg.c_sgt:cfg.c_sgt + C]
    MSU = lambda C: cst.t[0:C, cfg.c_msu:cfg.c_msu + C]
    MIU = lambda C: cst.t[0:C, cfg.c_miu:cfg.c_miu + C]
    ONESF = lambda C: cst.t[0:C, cfg.c_ones:cfg.c_ones + 128]

    def norm_to_h(pcol):
        srcs = []
        pb = big()
        fns = []
        reads = [onesb.b]
        for k in range(KC):
            s = sq[k % 2]
            act(s.t[:, :], xT[:, k, :], ACT.Square, [b_x[k]], [s])
            for (a, b) in TTS:
                P.op("pe", lambda e, k=k, s=s, a=a, b=b: e.matmul(
                    pb.t[:, a:b], onesb.t[:, :], s.t[:, a:b], start=(k == 0), stop=(k == KC - 1)),
                    [onesb.b, s.b], [pb.b], signal=True)
        rstd_from(pb, 1.0 / D, rstdB)
        for k in range(KC):
            stt(hT[:, k, :], xT[:, k, :], parcol(pcol + k), rstdB.t[:, :], ALU.mult, ALU.mult,
                [b_x[k], par, rstdB], [b_h[k]])

    def unit_cols(u):
        if u < NCH:
            return u * 64, 64
        return TP + (u - NCH) * 4, 4

    def ab_proj():
        pss = []
        for u in range(NU):
            c0, C = unit_cols(u)
            ps = small()
            fns = [lambda e, k=k, ps=ps, c0=c0, C=C: e.matmul(
                ps.t[0:C, 0:2 * NH], hT[:, k, c0:c0 + C], wab_sb.t[:, k, :],
                start=(k == 0), stop=(k == KC - 1)) for k in range(KC)]
            P.group("pe", fns, [wab_sb.b] + b_h, [ps.b])
            tt(t1[u].t[0:C, :], ps.t[0:C, 0:NH], parcol(cfg.p_dtb, NH)[0:C, :], ALU.add, [ps, par], [t1[u]])
            act(bt[u].t[0:C, :], ps.t[0:C, NH:2 * NH], ACT.Sigmoid, [ps], [bt[u]])
        for u in range(NU):
            c0, C = unit_cols(u)
            act(t1[u].t[0:C, :], t1[u].t[0:C, :], ACT.Exp, [t1[u]], [t1[u]])
        for u in range(NU):
            c0, C = unit_cols(u)
            act(t1[u].t[0:C, :], t1[u].t[0:C, :], ACT.Ln, [t1[u]], [t1[u]], bias=onec.t[0:C, 0:1])
            tt(gt[u].t[0:C, :], t1[u].t[0:C, :], negA.t[0:C, :], ALU.mult, [t1[u], negA], [gt[u]])
        for u in range(NU):
            c0, C = unit_cols(u)
            ps = small()
            mm(ps.t[0:C, 0:NH], TRILI(C)[:, 0:C], gt[u].t[0:C, :], [cst, gt[u]], [ps])
            act(eG[u].t[0:C, :], ps.t[0:C, 0:NH], ACT.Exp, [ps], [eG[u]])
            ts(neG[u].t[0:C, :], eG[u].t[0:C, :], -1.0, ALU.mult, [eG[u]], [neG[u]])
            ps2 = small()
            mm(ps2.t[:, 0:NH], ONESF(C), gt[u].t[0:C, :], [cst, gt[u]], [ps2])
            act(geb[u].t[:, :], ps2.t[:, 0:NH], ACT.Exp, [ps2], [geb[u]])

    def delta_head(hd, par2, st, l, sset):
        q_, k_, v_, z_ = qn[par2], kn[par2], vb[par2], zs[par2]
        units = list(range(NU))
        kkq = {}
        for u in units:
            c0, C = unit_cols(u)
            ts(gm[u].t[0:C, 0:C + 1], TRILI(C), gt[u].t[0:C, hd:hd + 1], ALU.mult, [cst, gt[u]], [gm[u]])
            ps = small()
            mm(ps.t[0:C, 0:C + 1], SGT(C), gm[u].t[0:C, 0:C + 1], [cst, gm[u]], [ps])
            act(E[u].t[0:C, 0:C + 1], ps.t[0:C, 0:C + 1], ACT.Exp, [ps], [E[u]])
        for u in units:
            c0, C = unit_cols(u)
            tt(Es[u].t[0:C, 0:C], E[u].t[0:C, 0:C], MSU(C), ALU.mult, [E[u], cst], [Es[u]])
            tt(Ei[u].t[0:C, 0:C], E[u].t[0:C, 0:C], MIU(C), ALU.mult, [E[u], cst], [Ei[u]])
            ps = small()
            P.group("pe", [
                lambda e, ps=ps, c0=c0, C=C: e.matmul(ps.t[0:C, 0:C], k_.t[:, c0:c0 + C], k_.t[:, c0:c0 + C],
                                                      start=True, stop=True),
                lambda e, ps=ps, c0=c0, C=C: e.matmul(ps.t[0:C, 64:64 + C], k_.t[:, c0:c0 + C], q_.t[:, c0:c0 + C],
                                                      start=True, stop=True)],
                [k_.b, q_.b], [ps.b])
            stt(LL[u].t[0:C, 0:C], ps.t[0:C, 0:C], bt[u].t[0:C, hd:hd + 1], Es[u].t[0:C, 0:C],
                ALU.mult, ALU.mult, [ps, bt[u], Es[u]], [LL[u]])
            tt(pmT[u].t[0:C, 0:C], ps.t[0:C, 64:64 + C], Ei[u].t[0:C, 0:C], ALU.mult, [ps, Ei[u]], [pmT[u]])
        for u in units:
            c0, C = unit_cols(u)
            ps = small()
            tr(ps.t[0:C, 0:C], LL[u].t[0:C, 0:C], cst.t[0:C, cfg.c_id:cfg.c_id + C], [LL[u], cst], [ps])
            act(LL[u].t[0:C, C:2 * C], ps.t[0:C, 0:C], ACT.Copy, [ps], [LL[u]])
            ps2 = small()
            pv2 = ps2.t[:, :].bitcast(BF16)
            tr(pv2[0:C, 0:128], v_.t[:, c0:c0 + C], idb.t[:, :], [v_, idb], [ps2])
            act(Vtm[u].t[0:C, :], pv2[0:C, 0:128], ACT.Copy, [ps2], [Vtm[u]])
            ps3 = small()
            pv3 = ps3.t[:, :].bitcast(BF16)
            tr(pv3[0:C, 0:128], k_.t[:, c0:c0 + C], idb.t[:, :], [k_, idb], [ps3])
            act(ke[u].t[0:C, :], pv3[0:C, 0:128], ACT.Copy, [ps3, E[u]], [ke[u]], scale=E[u].t[0:C, C:C + 1])
        def MMc(C, i):
            if C == 64:
                return cst.t[0:64, cfg.c_mm64 + i * 128: cfg.c_mm64 + (i + 1) * 128]
            return cst.t[0:4, cfg.c_mm4 + i * 8: cfg.c_mm4 + (i + 1) * 8]

        def IIc(C):
            if C == 64:
                return cst.t[0:64, cfg.c_ii64:cfg.c_ii64 + 128]
            return cst.t[0:4, cfg.c_ii4:cfg.c_ii4 + 8]
        tcur = {u: 0 for u in units}
        nlev = {u: (6 if unit_cols(u)[1] == 64 else 2) for u in units}
        pr = [u for u in units if u < NCH]
        sm_ = [u for u in units if u >= NCH]
        pbatches = [pr[i:i + IB] for i in range(0, len(pr), IB)]
        sbatches = [sm_[i:i + IB] for i in range(0, len(sm_), IB)]

        def inv_gen(batch):
            for u in batch:
                c0, C = unit_cols(u)
                tt(BBs[u].t[0:C, 0:2 * C], LL[u].t[0:C, :], MMc(C, 0), ALU.mult, [LL[u], cst], [BBs[u]])
                tt(TTp[u][0].t[0:C, 0:2 * C], IIc(C), BBs[u].t[0:C, 0:2 * C], ALU.subtract, [cst, BBs[u]],
                   [TTp[u][0]])
                tcur[u] = 0
            yield
            for lev in range(1, 6):
                act_units = [u for u in batch if lev < nlev[u]]
                if not act_units:
                    continue
                for u in act_units:
                    c0, C = unit_cols(u)
                    To = TTp[u][tcur[u]]
                    tt(BBs[u].t[0:C, 0:2 * C], LL[u].t[0:C, :], MMc(C, lev), ALU.mult, [LL[u], cst], [BBs[u]])
                    ps = small()
                    P.group("pe", [
                        lambda e, ps=ps, C=C, u=u, To=To: e.matmul(ps.t[0:C, 0:C], BBs[u].t[0:C, C:2 * C],
                                                                 To.t[0:C, 0:C], start=True, stop=True),
                        lambda e, ps=ps, C=C, u=u, To=To: e.matmul(ps.t[0:C, C:2 * C], BBs[u].t[0:C, 0:C],
                                                                 To.t[0:C, C:2 * C], start=True, stop=True)],
                        [BBs[u].b, To.b], [ps.b])
                    act(PQ[u].t[0:C, 0:2 * C], ps.t[0:C, 0:2 * C], ACT.Copy, [ps], [PQ[u]])
                yield
                for u in act_units:
                    c0, C = unit_cols(u)
                    To = TTp[u][tcur[u]]
                    Tn = TTp[u][1 - tcur[u]]
                    ps2 = small()
                    P.group("pe", [
                        lambda e, ps2=ps2, C=C, u=u, To=To: e.matmul(ps2.t[0:C, 0:C], To.t[0:C, C:2 * C],
                                                                   PQ[u].t[0:C, 0:C], start=True, stop=True),
                        lambda e, ps2=ps2, C=C, u=u, To=To: e.matmul(ps2.t[0:C, C:2 * C], To.t[0:C, 0:C],
                                                                   PQ[u].t[0:C, C:2 * C], start=True, stop=True)],
                        [PQ[u].b, To.b], [ps2.b])
                    tt(Tn.t[0:C, 0:2 * C], To.t[0:C, 0:2 * C], ps2.t[0:C, 0:2 * C], ALU.subtract, [To, ps2], [Tn])
                    tcur[u] = 1 - tcur[u]
                yield
            for u in batch:
                c0, C = unit_cols(u)
                act(Ttb[u].t[0:C, 0:C], TTp[u][tcur[u]].t[0:C, 0:C], ACT.Copy, [TTp[u][tcur[u]]], [Ttb[u]])
            yield

        def p2_unit(u, ri):
            c0, C = unit_cols(u)
            Tt = Ttb[u]
            if u < NCH:
                S32t, Sbft = S32[hd], Sbf[hd]
            else:
                s = u - NCH
                S32t = Tile(Ss32[sset].t[:, s, :], Ss32[sset].b)
                Sbft = Ssbf[sset][s]
            psK = small()
            mm(psK.t[0:C, :], k_.t[:, c0:c0 + C], Sbft.t[:, :], [k_, Sbft], [psK])
            psQ = small()
            mm(psQ.t[0:C, :], q_.t[:, c0:c0 + C], Sbft.t[:, :], [q_, Sbft], [psQ])
            stt(Yt[ri].t[0:C, :], psK.t[0:C, :], neG[u].t[0:C, hd:hd + 1], Vtm[u].t[0:C, :],
                ALU.mult, ALU.add, [psK, neG[u], Vtm[u]], [Yt[ri]])
            act(o1s[ri].t[0:C, :], psQ.t[0:C, :], ACT.Copy, [psQ, eG[u]], [o1s[ri]], scale=eG[u].t[0:C, hd:hd + 1])
            yield
            psU = small()
            mm(psU.t[0:C, :], Tt.t[0:C, 0:C], Yt[ri].t[0:C, :], [Tt, Yt[ri]], [psU])
            act(ut[ri].t[0:C, :], psU.t[0:C, :], ACT.Copy, [psU, bt[u]], [ut[ri]], scale=bt[u].t[0:C, hd:hd + 1])
            yield
            psD = small()
            mm(psD.t[:, :], ke[u].t[0:C, :], ut[ri].t[0:C, :], [ke[u], ut[ri]], [psD])
            if u < NCH:
                stt(S32t.t[:, :], S32t.t[:, :], geb[u].t[:, hd:hd + 1], psD.t[:, :], ALU.mult, ALU.add,
                    [S32t, geb[u], psD], [S32t])
                act(Sbft.t[:, :], S32t.t[:, :], ACT.Copy, [S32t], [Sbft])
            else:
                s = u - NCH
                sn = Ssn[0][s]
                stt(sn.t[:, :], S32t.t, geb[u].t[:, hd:hd + 1], psD.t[:, :], ALU.mult, ALU.add,
                    [S32t, geb[u], psD], [sn])
                out_ticks.append(sdma(o_dl_s[l, st * SS + s, hd], sn.t[:, :], f"ods{s}", [sn], []))
            psO = small()
            mm(psO.t[0:C, :], pmT[u].t[0:C, 0:C], ut[ri].t[0:C, :], [pmT[u], ut[ri]], [psO])
            tt(ot[ri].t[0:C, :], psO.t[0:C, :], o1s[ri].t[0:C, :], ALU.add, [psO, o1s[ri]], [ot[ri]])
            act(junk[ri].t[0:C, :], ot[ri].t[0:C, :], ACT.Square, [ot[ri]], [junk[ri], ssq[ri]],
                accum=ssq[ri].t[0:C, 0:1])
            act(rsq[ri].t[0:C, :], ssq[ri].t[0:C, :], ACT.Sqrt, [ssq[ri]], [rsq[ri]], scale=1.0 / 128,
                bias=epsc.t[0:C, 0:1])
            recip(rsq[ri].t[0:C, :], rsq[ri].t[0:C, :], [rsq[ri]], [rsq[ri]])
            ts(onb[ri].t[0:C, :], ot[ri].t[0:C, :], rsq[ri].t[0:C, 0:1], ALU.mult, [ot[ri], rsq[ri]], [onb[ri]])
            yield
            psT = small()
            pvT = psT.t[:, :].bitcast(BF16)
            tr(pvT[:, 0:C], onb[ri].t[0:C, :], idb.t[0:C, 0:C], [onb[ri], idb], [psT])
            stt(yh[:, NG + hd, c0:c0 + C], pvT[:, 0:C], parcol(cfg.p_dnw), z_.t[:, c0:c0 + C],
                ALU.mult, ALU.mult, [psT, par, z_], [b_yh[NG + hd]])
            yield

        def p2_chain(us, ri0, alt=True):
            for i, u in enumerate(us):
                yield from p2_unit(u, (ri0 + i) % NR if alt else ri0)

        def run(*gens):
            gens = list(gens)
            while gens:
                for g in list(gens):
                    try:
                        next(g)
                    except StopIteration:
                        gens.remove(g)

        run(inv_gen(pbatches[0]))
        for i in range(1, len(pbatches)):
            run(inv_gen(pbatches[i]), p2_chain(pbatches[i - 1], 0))
        lastp = p2_chain(pbatches[-1], 0)
        if sbatches:
            run(inv_gen(sbatches[0]), lastp)
            for sb in sbatches[1:]:
                run(inv_gen(sb))
        else:
            run(lastp)
        for i in range(0, len(sm_), 2):
            pair = sm_[i:i + 2]
            run(*[p2_chain([u], j, alt=False) for j, u in enumerate(pair)])

    onec = P.tile("onec", [128, 1])
    A_("dve", lambda e: e.memset(onec.t[:, :], 1.0), [], [onec])

    def conv_apply(Ut, W, wcol0, out_t, reads_extra):
        hw = W - 1
        so = hw + TP
        sw = hw + 4
        Us = Ut.t[:, so:so + SS * sw].rearrange("p (s w) -> p s w", s=SS)
        outs = out_t.t[:, TP:NT].rearrange("p (s w) -> p s w", s=SS)
        act(out_t.t[:, 0:TP], Ut.t[:, 0:TP], ACT.Copy, [Ut, par], [out_t], scale=parcol(wcol0))
        act(outs, Us[:, :, 0:4], ACT.Copy, [Ut, par], [out_t], scale=parcol(wcol0))
        for i in range(1, W):
            stt(out_t.t[:, 0:TP], Ut.t[:, i:i + TP], parcol(wcol0 + i), out_t.t[:, 0:TP], ALU.mult, ALU.add,
                [Ut, par, out_t], [out_t])
            stt(outs, Us[:, :, i:i + 4], parcol(wcol0 + i), outs, ALU.mult, ALU.add, [Ut, par, out_t], [out_t])

    def evac_halo(pb, Ut, W):
        hw = W - 1
        so = hw + TP
        sw = hw + 4
        Us = Ut.t[:, so:so + SS * sw].rearrange("p (s w) -> p s w", s=SS)
        act(Ut.t[:, hw:hw + TP], pb.t[:, 0:TP], ACT.Copy, [pb], [Ut])
        act(Us[:, :, hw:hw + 4], pb.t[:, TP:NT].rearrange("p (s w) -> p s w", s=SS), ACT.Copy, [pb], [Ut])
        return Us

    def run_pass(l, st):
        src = xin[st] if l == 0 else xs[st]
        sdma(xT[:, :, :], src, "xld", ([b_xs[st]] if l > 0 else []), b_x)
        sset = st % 2
        norm_to_h(cfg.p_n1)
        P.dma("pool", lambda e: e.dma_start(out=wab_sb.t[:, :, :], in_=w_ab[l]), "wab", [], [wab_sb.b])
        ab_proj()
        def grpA_part1(g):
            p = g % 2
            sl, wv = wload(w_A[l, g], KC, 384)
            Ut = U[0]
            hw = 2
            so = hw + TP
            sw = hw + 4
            Us = Ut.t[:, so:so + SS * sw].rearrange("p (s w) -> p s w", s=SS)
            for i in range(3):
                pb = big_mm(wv, KC, i * 128, hT, b_h, sl)
                if i == 0:
                    act(pj[0].t[:, :], pb.t[:, 0:NT], ACT.Copy, [pb], [pj[0]])
                elif i == 1:
                    act(pj[1].t[:, :], pb.t[:, 0:NT], ACT.Copy, [pb], [pj[1]])
                else:
                    tt(Ut.t[:, hw:hw + TP], pb.t[:, 0:TP], pj[1].t[:, 0:TP], ALU.mult, [pb, pj[1]], [Ut])
                    tt(Us[:, :, hw:hw + 4], pb.t[:, TP:NT].rearrange("p (s w) -> p s w", s=SS),
                       pj[1].t[:, TP:NT].rearrange("p (s w) -> p s w", s=SS), ALU.mult, [pb, pj[1]], [Ut])
            if st == 0:
                A_("dve", lambda e, Ut=Ut: e.memset(Ut.t[:, 0:2], 0.0), [], [Ut])
            else:
                act(Ut.t[:, 0:2], haloA.t[:, g, :], ACT.Copy, [haloA], [Ut])
            sdma(Us[:, :, 0:2], s_ca[l, :, g, st * SS:(st + 1) * SS, :], "sca", [], [Ut])
            act(haloA.t[:, g, :], Ut.t[:, TP:TP + 2], ACT.Copy, [Ut], [haloA])
            out_ticks.append(sdma(o_ca_s[l, :, g, st * SS:(st + 1) * SS, :], Us[:, :, 4:6], "ocas", [Ut], []))
            conv_apply(Ut, 3, cfg.p_caw + g * 3, acc, [])
            tt(cvp[p][0].t[:, :], pj[0].t[:, :], acc.t[:, :], ALU.mult, [pj[0], acc], [cvp[p][0]])
            act(sqp[p][0].t[:, :], cvp[p][0].t[:, :], ACT.Square, [cvp[p][0]], [sqp[p][0]])

        def grpA_part2(g):
            p = g % 2
            pb = ones_sum([sqp[p][0]])
            rstd_from(pb, 1.0 / 128, cvp[p][1])
            stt(yh[:, g, 0:NT], cvp[p][0].t[:, :], parcol(cfg.p_canw + g), cvp[p][1].t[:, :], ALU.mult, ALU.mult,
                [cvp[p][0], par, cvp[p][1]], [b_yh[g]])

        grpA_part1(0)
        for g in range(NG):
            if g + 1 < NG:
                grpA_part1(g + 1)
            grpA_part2(g)
        if st == NST - 1:
            out_ticks.append(sdma(o_ca_p[l], haloA.t[:, :, :], "ocap", [haloA], []))
        def headB_part1(hd):
            p2 = hd % 2
            sset = hd % 2
            sst = Ss32[sset]
            sdma(sst.t[:, :, :], s_dl[l, st * SS:(st + 1) * SS, hd].rearrange("s k v -> k s v"), f"sdl{sset}",
                 [], [sst])
            for s_ in range(SS):
                act(Ssbf[sset][s_].t[:, :], sst.t[:, s_, :], ACT.Copy, [sst], [Ssbf[sset][s_]])
            if st == 0:
                A_("dve", lambda e, hd=hd: e.memset(S32[hd].t[:, :], 0.0), [], [S32[hd]])
                A_("dve", lambda e, hd=hd: e.memset(Sbf[hd].t[:, :], 0.0), [], [Sbf[hd]])
            sl, wv = wload(w_B[l, hd], KC, 512)
            for i in range(3):
                pb = big_mm(wv, KC, i * 128, hT, b_h, sl)
                Ut = U[i]
                Us = evac_halo(pb, Ut, 4)
                if st == 0:
                    A_("dve", lambda e, Ut=Ut: e.memset(Ut.t[:, 0:3], 0.0), [], [Ut])
                else:
                    act(Ut.t[:, 0:3], haloB.t[:, hd, i, :], ACT.Copy, [haloB], [Ut])
                sdma(Us[:, :, 0:3], s_cq[l, :, hd, i, st * SS:(st + 1) * SS, :], f"scq{i}", [], [Ut])
                act(haloB.t[:, hd, i, :], Ut.t[:, TP:TP + 3], ACT.Copy, [Ut], [haloB])
                out_ticks.append(sdma(o_cq_s[l, :, hd, i, st * SS:(st + 1) * SS, :], Us[:, :, 4:7], f"ocqs{i}",
                                      [Ut], []))
            pb = big_mm(wv, KC, 3 * 128, hT, b_h, sl)
            act(zs[p2].t[:, :], pb.t[:, 0:NT], ACT.Silu, [pb], [zs[p2]])
            for i in range(3):
                conv_apply(U[i], 4, cfg.p_cbw + (hd * 3 + i) * 4, acc, [])
                if i == 2:
                    act(vb[p2].t[:, :], acc.t[:, :], ACT.Silu, [acc], [vb[p2]])
                else:
                    act(cvp[p2][i].t[:, :], acc.t[:, :], ACT.Silu, [acc], [cvp[p2][i]])
            for i in range(2):
                act(sqp[p2][i].t[:, :], cvp[p2][i].t[:, :], ACT.Square, [cvp[p2][i]], [sqp[p2][i]])

        def headB_part2(hd):
            p2 = hd % 2
            for i in range(2):
                pb = ones_sum([sqp[p2][i]])
                rstd_from(pb, 1.0, pj[i])
                if i == 0:
                    stt(qn[p2].t[:, :], cvp[p2][0].t[:, :], float(128 ** -0.5), pj[0].t[:, :], ALU.mult, ALU.mult,
                        [cvp[p2][0], pj[0]], [qn[p2]])
                else:
                    tt(kn[p2].t[:, :], cvp[p2][1].t[:, :], pj[1].t[:, :], ALU.mult, [cvp[p2][1], pj[1]], [kn[p2]])
            delta_head(hd, p2, st, l, hd % 2)
            if st == NST - 1:
                out_ticks.append(sdma(o_dl_p[l, hd], S32[hd].t[:, :], "odp", [S32[hd]], []))

        headB_part1(0)
        for hd in range(NH):
            if hd + 1 < NH:
                headB_part1(hd + 1)
            headB_part2(hd)
        if st == NST - 1:
            out_ticks.append(sdma(o_cq_p[l], haloB.t[:, :, :, :], "ocqp", [haloB], []))
        for ot_ in range(cfg.NOT):
            sl, wv = wload(w_o[l, ot_], YC, cfg.CWO)
            for j in range(cfg.CWO // 128):
                oc = ot_ * (cfg.CWO // 128) + j
                pb = big_mm(wv, YC, j * 128, yh, b_yh[0:YC], sl)
                tt(xT[:, oc, :], pb.t[:, 0:NT], xT[:, oc, :], ALU.add, [pb, b_x[oc]], [b_x[oc]])
        norm_to_h(cfg.p_n2)
        for hg in range(cfg.NHG):
            for ut_ in range(cfg.NUT):
                sl, wv = wload(w_u[l, hg * cfg.NUT + ut_], KC, 512)
                for j in range(4):
                    hc = ut_ * 4 + j
                    pb = big_mm(wv, KC, j * 128, hT, b_h, sl)
                    act(tmpN.t[:, :], pb.t[:, 0:NT], ACT.Relu, [pb], [tmpN])
                    tt(yh[:, hc, 0:NT], tmpN.t[:, :], tmpN.t[:, :], ALU.mult, [tmpN], [b_yh[hc]])
            for ot_ in range(cfg.NOT):
                sl, wv = wload(w_d[l, hg, ot_], cfg.HK, cfg.CWO)
                for j in range(cfg.CWO // 128):
                    oc = ot_ * (cfg.CWO // 128) + j
                    pb = big_mm(wv, cfg.HK, j * 128, yh, b_yh[0:cfg.HK], sl)
                    tt(xT[:, oc, :], pb.t[:, 0:NT], xT[:, oc, :], ALU.add, [pb, b_x[oc]], [b_x[oc]])
        if l == L - 1:
            pb = big()
            for k in range(KC):
                s = sq[k % 2]
                act(s.t[:, :], xT[:, k, :], ACT.Square, [b_x[k]], [s])
                for (a, b) in TTS:
                    P.op("pe", lambda e, k=k, s=s, a=a, b=b, pb=pb: e.matmul(
                        pb.t[:, a:b], onesb.t[:, :], s.t[:, a:b], start=(k == 0), stop=(k == KC - 1)),
                        [onesb.b, s.b], [pb.b], signal=True)
            rstd_from(pb, 1.0 / D, rstdB)
            for k in range(KC):
                stt(xT[:, k, :], xT[:, k, :], fnw.t[:, k:k + 1], rstdB.t[:, :], ALU.mult, ALU.mult,
                    [b_x[k], fnw, rstdB], [b_x[k]])
            out_ticks.append(sdma(yout[st], xT[:, :, :], "xst", b_x, []))
        else:
            sdma(xs[st], xT[:, :, :], "xst", b_x, [b_xs[st]])

    for l in range(L):
        sdma(par.t[:, :], par_d[l], "par", [], [par])
        act(negA.t[:, :], par.t[:, cfg.p_alog:cfg.p_alog + NH], ACT.Exp, [par], [negA])
        ts(negA.t[:, :], negA.t[:, :], -1.0, ALU.mult, [negA], [negA])
        for st in range(NST):
            run_pass(l, st)
    P.wait_all("sp", out_ticks)
    P.finish()
    return nc


def prep_weights(cfg, inp):
    D, L, KC, NG, NH = cfg.D, cfg.L, cfg.KC, cfg.NG, cfg.NH
    DC = cfg.DC
    w_in = np.asarray(inp["w_in"], np.float32)

    def ktile(w):
        c = w.shape[-1]
        return np.ascontiguousarray(w.reshape(L, -1, 128, c).transpose(0, 2, 1, 3))
    oB, oC, oH, oQ, oZ, oa = 0, DC, 2 * DC, 3 * DC, 6 * DC, 7 * DC
    w_ab = ktile(w_in[:, :, oa:oa + 2 * NH])
    wA = np.empty((L, NG, 128, KC, 384), np.float32)
    for g in range(NG):
        cols = np.concatenate([np.arange(o + g * 128, o + (g + 1) * 128) for o in (oB, oC, oH)])
        wA[:, g] = ktile(w_in[:, :, cols])
    wB = np.empty((L, NH, 128, KC, 512), np.float32)
    for h in range(NH):
        cols = np.concatenate([np.arange(o + h * 128, o + (h + 1) * 128)
                               for o in (oQ, oQ + DC, oQ + 2 * DC, oZ)])
        wB[:, h] = ktile(w_in[:, :, cols])
    w_out = np.asarray(inp["w_out"], np.float32)
    wo = np.stack([ktile(w_out[:, :, t * cfg.CWO:(t + 1) * cfg.CWO]) for t in range(cfg.NOT)], 1)
    w_up = np.asarray(inp["w_up"], np.float32)
    wu = np.stack([ktile(w_up[:, :, t * 512:(t + 1) * 512]) for t in range(cfg.DFF // 512)], 1)
    w_dn = np.asarray(inp["w_down"], np.float32)
    wd = np.empty((L, cfg.NHG, cfg.NOT, 128, cfg.HK, cfg.CWO), np.float32)
    for hg in range(cfg.NHG):
        for t in range(cfg.NOT):
            wd[:, hg, t] = ktile(w_dn[:, hg * cfg.HGS:(hg + 1) * cfg.HGS, t * cfg.CWO:(t + 1) * cfg.CWO])
    par = np.zeros((L, 128, cfg.NPAR), np.float32)
    fm = lambda v: v.reshape(L, -1, 128).transpose(0, 2, 1)
    par[:, :, cfg.p_n1:cfg.p_n1 + KC] = fm(np.asarray(inp["norm_mix_w"]))
    par[:, :, cfg.p_n2:cfg.p_n2 + KC] = fm(np.asarray(inp["norm_ffn_w"]))
    caw = np.asarray(inp["conv_a_w"]).reshape(L, 3, NG, 128).transpose(0, 3, 2, 1)
    par[:, :, cfg.p_caw:cfg.p_caw + NG * 3] = caw.reshape(L, 128, NG * 3)
    par[:, :, cfg.p_canw:cfg.p_canw + NG] = fm(np.asarray(inp["conv_a_norm_w"]))
    cbw = np.asarray(inp["conv_qkv_w"]).reshape(L, 4, 3, NH, 128).transpose(0, 4, 3, 2, 1)
    par[:, :, cfg.p_cbw:cfg.p_cbw + NH * 12] = cbw.reshape(L, 128, NH * 12)
    par[:, :, cfg.p_dnw] = np.asarray(inp["dn_norm_w"])
    par[:, :, cfg.p_dtb:cfg.p_dtb + NH] = np.asarray(inp["dt_bias"])[:, None, :]
    par[:, :, cfg.p_alog:cfg.p_alog + NH] = np.asarray(inp["a_log"])[:, None, :]
    fnw = np.ascontiguousarray(np.asarray(inp["final_norm_w"], np.float32).reshape(-1, 128).T)
    return dict(w_ab=w_ab, w_A=wA, w_B=wB, w_o=wo, w_u=wu, w_d=wd, par=par, fnw=fnw,
                consts=make_consts(cfg))


def prep_core(cfg, inp, xp_seq, seq0):
    D, L, KC, NG, NH, NST, TP, SS, NT = cfg.D, cfg.L, cfg.KC, cfg.NG, cfg.NH, cfg.NST, cfg.TP, cfg.SS, cfg.NT
    NSEQ = cfg.NSEQ
    xsamp = np.asarray(inp["x_sample"], np.float32)[seq0:seq0 + NSEQ]
    xin = np.empty((NST, 128, KC, NT), np.float32)
    for st in range(NST):
        tok = np.concatenate([xp_seq[st * TP:(st + 1) * TP], xsamp[st * SS:(st + 1) * SS].reshape(-1, D)], 0)
        xin[st] = tok.T.reshape(KC, 128, NT).transpose(1, 0, 2)
    sca = np.asarray(inp["state_conv_a"], np.float32)[:, seq0:seq0 + NSEQ]
    s_ca = np.ascontiguousarray(sca.reshape(L, NSEQ, 2, NG, 128).transpose(0, 4, 3, 1, 2))
    scq = np.asarray(inp["state_conv_qkv"], np.float32)[:, seq0:seq0 + NSEQ]
    s_cq = np.ascontiguousarray(scq.reshape(L, NSEQ, 3, 3, NH, 128).transpose(0, 5, 4, 3, 1, 2))
    sdl = np.asarray(inp["state_delta"], np.float32)[:, seq0:seq0 + NSEQ]
    s_dl = np.ascontiguousarray(sdl.transpose(0, 1, 2, 4, 3))
    return dict(xin=xin, s_ca=s_ca, s_cq=s_cq, s_dl=s_dl)


_CACHE = {}


def kernel(**inputs):
    cfg = Cfg()
    NCORES = 8
    if "nc" not in _CACHE:
        _CACHE["nc"] = build_program(cfg)
    nc = _CACHE["nc"]
    wts = prep_weights(cfg, inputs)
    xp = np.asarray(inputs["x_prompt"], np.float32)
    B, S, D = xp.shape
    in_maps = []
    for c in range(NCORES):
        xseq = xp[c] if c < B else np.zeros((S, D), np.float32)
        m = dict(wts)
        m.update(prep_core(cfg, inputs, xseq, c * cfg.NSEQ))
        in_maps.append(m)
    res = run_bass_kernel_spmd(nc, in_maps, core_ids=list(range(NCORES)))
    return assemble(cfg, res.results, B)


def assemble(cfg, results, B):
    D, L, KC, NG, NH, NST, TP, SS, NT, NSEQ = (cfg.D, cfg.L, cfg.KC, cfg.NG, cfg.NH, cfg.NST, cfg.TP,
                                                 cfg.SS, cfg.NT, cfg.NSEQ)
    DC = cfg.DC
    ncores = len(results)
    y_p = np.empty((B, NST * TP, D), np.float32)
    y_s = np.empty((ncores * NSEQ, 4, D), np.float32)
    ca_p = np.empty((L, B, 2, DC), np.float32)
    cq_p = np.empty((L, B, 3, 3 * DC), np.float32)
    dl_p = np.empty((L, B, NH, 128, 128), np.float32)
    ca_s = np.empty((L, ncores * NSEQ, 2, DC), np.float32)
    cq_s = np.empty((L, ncores * NSEQ, 3, 3 * DC), np.float32)
    dl_s = np.empty((L, ncores * NSEQ, NH, 128, 128), np.float32)
    for c, r in enumerate(results):
        yo = np.asarray(r["yout"])
        tok = yo.transpose(0, 3, 2, 1).reshape(NST, NT, D)
        for st in range(NST):
            if c < B:
                y_p[c, st * TP:(st + 1) * TP] = tok[st, :TP]
            y_s[c * NSEQ + st * SS: c * NSEQ + (st + 1) * SS] = tok[st, TP:].reshape(SS, 4, D)
        sl = slice(c * NSEQ, (c + 1) * NSEQ)
        ca_s[:, sl] = np.asarray(r["o_ca_s"]).transpose(0, 3, 4, 2, 1).reshape(L, NSEQ, 2, DC)
        cq_s[:, sl] = np.asarray(r["o_cq_s"]).transpose(0, 4, 5, 3, 2, 1).reshape(L, NSEQ, 3, 3 * DC)
        dl_s[:, sl] = np.asarray(r["o_dl_s"]).transpose(0, 1, 2, 4, 3)
        if c < B:
            ca_p[:, c] = np.asarray(r["o_ca_p"]).transpose(0, 3, 2, 1).reshape(L, 2, DC)
            cq_p[:, c] = np.asarray(r["o_cq_p"]).transpose(0, 4, 3, 2, 1).reshape(L, 3, 3 * DC)
            dl_p[:, c] = np.asarray(r["o_dl_p"]).transpose(0, 1, 3, 2)
    return (y_p, y_s, ca_p, cq_p, dl_p, ca_s, cq_s, dl_s)
```

```python
import contextlib
import numpy as np
import concourse.bass as bass
import concourse.mybir as mybir
from concourse.bass_utils import run_bass_kernel_spmd

DT = mybir.dt
F32 = DT.float32
BF16 = DT.bfloat16
ACT = mybir.ActivationFunctionType
ALU = mybir.AluOpType
EPS = 1e-6

ENGS = ["pe", "act", "dve", "pool", "sp"]
EPOCH = 16000


class Buf:
    __slots__ = ("name", "w", "r")

    def __init__(self, name):
        self.name = name
        self.w = None
        self.r = []


class Tile:
    def __init__(self, t, b):
        self.t = t
        self.b = b


class Prog:
    def __init__(self, nc):
        self.nc = nc
        self.q = {e: [] for e in ENGS}
        self.cnt = {}
        self.seen = {e: {} for e in ENGS}
        self.ecnt = {e: 0 for e in ENGS}
        self.dma_last = {}
        self.stack = contextlib.ExitStack()
        self.nbuf = 0

    def sbuf(self, name, shape, dtype=F32):
        return self.stack.enter_context(self.nc.sbuf_tensor("sb_" + name, list(shape), dtype))

    def psum(self, name, shape, dtype=F32):
        return self.stack.enter_context(self.nc.psum_tensor("ps_" + name, list(shape), dtype))

    def buf(self, name=None):
        self.nbuf += 1
        return Buf(name or f"b{self.nbuf}")

    def tile(self, name, shape, dtype=F32):
        return Tile(self.sbuf(name, shape, dtype), self.buf(name))

    def _waits(self, eng, reads, writes):
        need = {}

        def add(t):
            if t is None:
                return
            k, v = t
            if need.get(k, 0) < v:
                need[k] = v
        for b in reads:
            add(b.w)
        for b in writes:
            add(b.w)
            for t in b.r:
                add(t)
        out = []
        seen = self.seen[eng]
        for k, v in need.items():
            if seen.get(k, 0) < v:
                seen[k] = v
                out.append((k, v))
        return out

    def op(self, eng, fn, reads=(), writes=(), signal=True):
        waits = self._waits(eng, reads, writes)
        tick = None
        if signal:
            self.ecnt[eng] += 1
            key = (eng, self.ecnt[eng] // EPOCH)
            self.cnt[key] = self.cnt.get(key, 0) + 1
            tick = (key, self.cnt[key])
        self.q[eng].append((waits, fn, tick, 1))
        if tick is not None:
            for b in reads:
                if len(b.r) > 24:
                    b.r = b.r[-24:] if False else b.r
                b.r.append(tick)
            for b in writes:
                b.w = tick
                b.r = []
        return tick

    def group(self, eng, fns, reads=(), writes=()):
        n = len(fns)
        for i, fn in enumerate(fns):
            if i == n - 1:
                return self.op(eng, fn, reads, writes, signal=True)
            if i == 0:
                self.op(eng, fn, reads, writes, signal=False)
            else:
                self.op(eng, fn, (), (), signal=False)

    def dma(self, eng, fn, semkey, reads=(), writes=(), inc=16):
        key = ("dma", semkey)
        waits = self._waits(eng, reads, writes)
        prev = self.dma_last.get(key)
        if prev is not None and self.seen[eng].get(key, 0) < prev[1]:
            self.seen[eng][key] = prev[1]
            waits.append(prev)
        self.cnt[key] = self.cnt.get(key, 0) + inc
        tick = (key, self.cnt[key])
        self.dma_last[key] = tick
        self.q[eng].append((waits, fn, tick, inc))
        for b in reads:
            b.r.append(tick)
        for b in writes:
            b.w = tick
            b.r = []
        return tick

    def wait_all(self, eng, ticks):
        waits = []
        for t in ticks:
            if t is None:
                continue
            k, v = t
            if self.seen[eng].get(k, 0) < v:
                self.seen[eng][k] = v
                waits.append((k, v))
        self.q[eng].append((waits, None, None, 0))

    def finish(self):
        nc = self.nc
        sems = {}
        for i, k in enumerate(self.cnt):
            sems[k] = self.stack.enter_context(nc.semaphore(f"s{i}"))
        engobj = {"pe": "tensor", "act": "scalar", "dve": "vector", "pool": "gpsimd", "sp": "sync"}
        with nc.Block() as block:
            for e in ENGS:
                items = self.q[e]
                if not items:
                    continue

                def body(engine, items=items):
                    for waits, fn, tick, n in items:
                        for k, v in waits:
                            engine.wait_ge(sems[k], v)
                        if fn is None:
                            continue
                        ins = fn(engine)
                        if tick is not None:
                            ins.then_inc(sems[tick[0]], n)
                getattr(block, engobj[e])(body)
        self.stack.close()


class Cfg:
    def __init__(self, D=2048, L=4, NST=4, TP=512, SS=4, PIPE=False):
        self.D = D
        self.L = L
        self.PIPE = PIPE
        self.GROUPS = [[0, 1], [2, 3], [4, 5], [6, 7]]
        self.LS = L + 1 if PIPE else L
        self.NST = NST
        self.TP = TP
        self.SS = SS
        self.KC = D // 128
        self.DC = D // 2
        self.NG = self.DC // 128
        self.NH = self.DC // 128
        self.DFF = 4 * D
        self.NTS = SS * 4
        self.NT = TP + self.NTS
        self.NCH = TP // 64
        self.NSEQ = NST * SS
        self.CWO = min(512, D)
        self.NOT = D // self.CWO
        self.HGS = min(2048, self.DFF)
        self.NHG = self.DFF // self.HGS
        self.HK = self.HGS // 128
        self.NUT = self.HGS // 512
        self.YC = D // 128
        o = 0
        self.p_n1 = o; o += self.KC
        self.p_n2 = o; o += self.KC
        self.p_caw = o; o += self.NG * 3
        self.p_canw = o; o += self.NG
        self.p_cbw = o; o += self.NH * 3 * 4
        self.p_dnw = o; o += 1
        self.p_dtb = o; o += self.NH
        self.p_alog = o; o += self.NH
        self.NPAR = o
        o = 0
        self.c_id = o; o += 128
        self.c_ones = o; o += 128
        self.c_trili = o; o += 65
        self.c_sgt = o; o += 64
        self.c_msu = o; o += 64
        self.c_miu = o; o += 64
        self.c_trili4 = o; o += 5
        self.c_mm64 = o; o += 6 * 128
        self.c_ii64 = o; o += 128
        self.c_mm4 = o; o += 2 * 8
        self.c_ii4 = o; o += 8
        self.NCONST = o
        self.WSLOT = 16 * 512


def make_consts(cfg):
    c = np.zeros((128, cfg.NCONST), np.float32)
    c[:, cfg.c_id:cfg.c_id + 128] = np.eye(128, dtype=np.float32)
    c[:, cfg.c_ones:cfg.c_ones + 128] = 1.0
    j = np.arange(64)[:, None]
    m = np.arange(64)[None, :]
    c[:64, cfg.c_trili:cfg.c_trili + 64] = (j <= m)
    c[:64, cfg.c_trili + 64] = 1.0
    c[:64, cfg.c_sgt:cfg.c_sgt + 64] = (j > m)
    c[:64, cfg.c_msu:cfg.c_msu + 64] = (m > j)
    c[:64, cfg.c_miu:cfg.c_miu + 64] = (m >= j)
    c[:4, cfg.c_trili4:cfg.c_trili4 + 4] = (j[:4] <= m[:, :4])
    c[:4, cfg.c_trili4 + 4] = 1.0
    mi = np.arange(64)[:, None]
    ci = np.arange(64)[None, :]
    for i in range(6):
        sz = 2 ** i
        M = ((mi // (2 * sz)) == (ci // (2 * sz))) & ((mi // sz) % 2 == 0) & ((ci // sz) % 2 == 1)
        M = M.astype(np.float32)
        c[:64, cfg.c_mm64 + i * 128: cfg.c_mm64 + i * 128 + 64] = M
        c[:64, cfg.c_mm64 + i * 128 + 64: cfg.c_mm64 + (i + 1) * 128] = M.T
        if i < 2:
            c[:4, cfg.c_mm4 + i * 8: cfg.c_mm4 + i * 8 + 4] = M[:4, :4]
            c[:4, cfg.c_mm4 + i * 8 + 4: cfg.c_mm4 + (i + 1) * 8] = M[:4, :4].T
    c[:64, cfg.c_ii64:cfg.c_ii64 + 64] = np.eye(64)
    c[:64, cfg.c_ii64 + 64:cfg.c_ii64 + 128] = np.eye(64)
    c[:4, cfg.c_ii4:cfg.c_ii4 + 4] = np.eye(4)
    c[:4, cfg.c_ii4 + 4:cfg.c_ii4 + 8] = np.eye(4)
    return c


def build_program(cfg):
    nc = bass.Bass("TRN2", target_bir_lowering=False)
    P = Prog(nc)
    D, L, NST, TP, SS, KC, NG, NH = cfg.D, cfg.LS, cfg.NST, cfg.TP, cfg.SS, cfg.KC, cfg.NG, cfg.NH
    NT, NTS, NCH, NSEQ = cfg.NT, cfg.NTS, cfg.NCH, cfg.NSEQ
    YC = cfg.YC
    PIPE = cfg.PIPE

    def din(name, shape):
        return nc.dram_tensor(name, list(shape), F32, kind="ExternalInput").ap()

    def dout(name, shape):
        return nc.dram_tensor(name, list(shape), F32, kind="ExternalOutput").ap()

    xin = din("xin", [NST, 128, KC, NT])
    w_ab = din("w_ab", [L, 128, KC, 2 * NH])
    w_A = din("w_A", [L, NG, 128, KC, 384])
    w_B = din("w_B", [L, NH, 128, KC, 512])
    w_o = din("w_o", [L, cfg.NOT, 128, YC, cfg.CWO])
    w_u = din("w_u", [L, cfg.NHG * cfg.NUT, 128, KC, 512])
    w_d = din("w_d", [L, cfg.NHG, cfg.NOT, 128, cfg.HK, cfg.CWO])
    par_d = din("par", [L, 128, cfg.NPAR])
    fnw_d = din("fnw", [128, KC])
    const_d = din("consts", [128, cfg.NCONST])
    s_ca = din("s_ca", [L, 128, NG, NSEQ, 2])
    s_cq = din("s_cq", [L, 128, NH, 3, NSEQ, 3])
    s_dl = din("s_dl", [L, NSEQ, NH, 128, 128])
    yout = dout("yout", [NST, 128, KC, NT])
    o_ca_p = dout("o_ca_p", [L, 128, NG, 2])
    o_cq_p = dout("o_cq_p", [L, 128, NH, 3, 3])
    o_dl_p = dout("o_dl_p", [L, NH, 128, 128])
    o_ca_s = dout("o_ca_s", [L, 128, NG, NSEQ, 2])
    o_cq_s = dout("o_cq_s", [L, 128, NH, 3, NSEQ, 3])
    o_dl_s = dout("o_dl_s", [L, NSEQ, NH, 128, 128])
    xs = nc.dram_tensor("xs", [NST, 128, KC, NT], F32).ap()
    XF = NH * 128 + NG * 2 + NH * 9
    if PIPE:
        mask_d = din("mask", [128, 1])
        xch_src_t = nc.dram_tensor("xch_src", [128, XF], F32)
        xch_dst_t = nc.dram_tensor("xch_dst", [2 * 128, XF], F32)
        xch_src = xch_src_t.ap()
        xch_dst = xch_dst_t.ap()
        b_xsrc = P.buf("xsrc")
        b_xdst = P.buf("xdst")
    b_xs = [P.buf(f"xs{i}") for i in range(NST)]
    out_ticks = []

    xT = P.sbuf("xT", [128, KC, NT]); b_x = [P.buf(f"x{k}") for k in range(KC)]
    hT = P.sbuf("hT", [128, KC, NT], BF16); b_h = [P.buf(f"h{k}") for k in range(KC)]
    NYH = max(YC, cfg.HK)
    yh = P.sbuf("yh", [128, NYH, NT], BF16); b_yh = [P.buf(f"yh{k}") for k in range(NYH)]
    NWS = 3
    wsl = [P.tile(f"ws{i}", [128, cfg.WSLOT], BF16) for i in range(NWS)]
    wab_sb = P.tile("wab", [128, KC, 2 * NH], BF16)
    cst = P.tile("cst", [128, cfg.NCONST])
    idb = P.tile("idb", [128, 128], BF16)
    onesb = P.tile("onesb", [128, 128], BF16)
    par = P.tile("par", [128, cfg.NPAR])
    fnw = P.tile("fnw", [128, KC])
    negA = P.tile("negA", [128, NH])
    pj = [P.tile(f"pj{i}", [128, NT]) for i in range(2)]
    rstdB = pj[0]
    WU = TP + 3 + 7 * SS
    U = [P.tile(f"U{i}", [128, WU]) for i in range(3)]
    acc = P.tile("acc", [128, NT])
    tmpN = acc
    cvp = [[P.tile(f"cv{p}_{i}", [128, NT]) for i in range(2)] for p in range(2)]
    cv = cvp[0]
    sqp = [[P.tile(f"sqp{p}_{i}", [128, NT], BF16) for i in range(2)] for p in range(2)]
    sq = [sqp[0][1], sqp[1][1]]
    zs = [P.tile(f"zs{i}", [128, NT]) for i in range(2)]
    qn = [P.tile(f"qn{i}", [128, NT], BF16) for i in range(2)]
    kn = [P.tile(f"kn{i}", [128, NT], BF16) for i in range(2)]
    vb = [P.tile(f"vb{i}", [128, NT], BF16) for i in range(2)]
    haloA = P.tile("haloA", [128, NG, 2])
    haloB = P.tile("haloB", [128, NH, 3, 3])
    S32all = P.sbuf("S32all", [128, NH, 128])
    S32 = [Tile(S32all[:, h, :], P.buf(f"S32_{h}")) for h in range(NH)]
    maskt = P.tile("maskt", [128, 1])
    Sbf = [P.tile(f"Sbf_{h}", [128, 128], BF16) for h in range(NH)]
    Ss32 = [P.tile(f"Ss32_{i}", [128, SS, 128]) for i in range(1)] * 2
    Ssbf = [[P.tile(f"Ssbf_{i}_{s}", [128, 128], BF16) for s in range(SS)] for i in range(1)] * 2
    _Ssn = [P.tile(f"Ssn_{s}", [128, 128]) for s in range(2)]
    Ssn = [[_Ssn[s % 2] for s in range(SS)]]
    NU = NCH + SS
    UC = [64 if u < NCH else 4 for u in range(NU)]
    gt = [P.tile(f"g{u}", [UC[u], NH]) for u in range(NU)]
    bt = [P.tile(f"bt{u}", [UC[u], NH]) for u in range(NU)]
    t1 = [P.tile(f"t1_{u}", [UC[u], NH]) for u in range(NU)]
    eG = [P.tile(f"eG{u}", [UC[u], NH]) for u in range(NU)]
    neG = [P.tile(f"neG{u}", [UC[u], NH]) for u in range(NU)]
    geb = [P.tile(f"geb{u}", [128, NH]) for u in range(NU)]
    _gm = [P.tile(f"gm{i}", [64, 65]) for i in range(2)]
    gm = [_gm[u % 2] for u in range(NU)]
    E = [P.tile(f"E{u}", [UC[u], UC[u] + 1]) for u in range(NU)]
    _Es = [P.tile(f"Es{i}", [64, 64]) for i in range(2)]
    _Ei = [P.tile(f"Ei{i}", [64, 64]) for i in range(2)]
    Es = [_Es[u % 2] for u in range(NU)]
    Ei = [_Ei[u % 2] for u in range(NU)]
    LL = [P.tile(f"LL{u}", [UC[u], 2 * UC[u]]) for u in range(NU)]
    IB = 4
    _BBs = [P.tile(f"BBs{i}", [64, 128]) for i in range(IB)] + [P.tile(f"BBs4_{i}", [4, 8]) for i in range(IB)]
    _PQ = [P.tile(f"PQ{i}", [64, 128]) for i in range(IB)] + [P.tile(f"PQ4_{i}", [4, 8]) for i in range(IB)]
    _TT = [[P.tile(f"TT{i}_{j}", [64, 128]) for j in range(2)] for i in range(IB)] + \
          [[P.tile(f"TT4_{i}_{j}", [4, 8]) for j in range(2)] for i in range(IB)]
    _slot = lambda u: (u % IB) if u < NCH else IB + ((u - NCH) % IB)
    BBs = [_BBs[_slot(u)] for u in range(NU)]
    PQ = [_PQ[_slot(u)] for u in range(NU)]
    TTp = [_TT[_slot(u)] for u in range(NU)]
    Ttb = [P.tile(f"Ttb{u}", [UC[u], UC[u]], BF16) for u in range(NU)]
    pmT = [P.tile(f"pmT{u}", [UC[u], UC[u]], BF16) for u in range(NU)]
    Vtm = [P.tile(f"Vtm{u}", [UC[u], 128], BF16) for u in range(NU)]
    ke = [P.tile(f"ke{u}", [UC[u], 128], BF16) for u in range(NU)]
    NR = 2
    Yt = [P.tile(f"Y{i}", [64, 128], BF16) for i in range(NR)]
    ut = [P.tile(f"u{i}", [64, 128], BF16) for i in range(NR)]
    o1s = [P.tile(f"o1s{i}", [64, 128]) for i in range(NR)]
    ot = [P.tile(f"o{i}", [64, 128]) for i in range(NR)]
    junk = o1s
    ssq = [P.tile(f"ssq{i}", [64, 1]) for i in range(NR)]
    rsq = [P.tile(f"rsq{i}", [64, 1]) for i in range(NR)]
    onb = [P.tile(f"onb{i}", [64, 128], BF16) for i in range(NR)]
    NBIG = 2
    pbig = [Tile(P.psum(f"pb{i}", [128, 1024]), P.buf(f"pb{i}")) for i in range(NBIG)]
    NSM = 4
    psm = [Tile(P.psum(f"psmb{i}", [128, 512])[:, 0:128], P.buf(f"psm{i}")) for i in range(NSM)]
    rr = {"big": 0, "sm": 0, "ws": 0, "rot": 0}

    def big():
        rr["big"] += 1
        return pbig[rr["big"] % NBIG]

    def small():
        rr["sm"] += 1
        return psm[rr["sm"] % NSM]

    def A_(eng, fn, reads=(), writes=()):
        return P.op(eng, fn, [t.b if isinstance(t, Tile) else t for t in reads],
                    [t.b if isinstance(t, Tile) else t for t in writes])

    def act(out, in_, func, reads, writes, scale=None, bias=None, accum=None):
        kw = {}
        if scale is not None:
            kw["scale"] = scale
        if bias is not None:
            kw["bias"] = bias
        if accum is not None:
            kw["accum_out"] = accum
        return A_("act", lambda e: e.activation(out=out, in_=in_, func=func, **kw), reads, writes)

    def tt(out, in0, in1, op, reads, writes):
        return A_("dve", lambda e: e.tensor_tensor(out=out, in0=in0, in1=in1, op=op), reads, writes)

    def stt(out, in0, scalar, in1, op0, op1, reads, writes):
        return A_("dve", lambda e: e.scalar_tensor_tensor(out=out, in0=in0, scalar=scalar, in1=in1,
                                                          op0=op0, op1=op1), reads, writes)

    def ts(out, in0, s1, op0, reads, writes, s2=None, op1=None):
        if op1 is None:
            return A_("dve", lambda e: e.tensor_scalar(out=out, in0=in0, scalar1=s1, scalar2=None, op0=op0),
                      reads, writes)
        return A_("dve", lambda e: e.tensor_scalar(out=out, in0=in0, scalar1=s1, scalar2=s2, op0=op0, op1=op1),
                  reads, writes)

    def recip(out, in_, reads, writes):
        return A_("dve", lambda e: e.reciprocal(out=out, in_=in_), reads, writes)

    def mm(out, lhsT, rhs, reads, writes, start=True, stop=True):
        return A_("pe", lambda e: e.matmul(out, lhsT, rhs, start=start, stop=stop), reads, writes)

    def tr(out, in_, ident, reads, writes):
        return A_("pe", lambda e: e.transpose(out, in_, ident), reads, writes)

    def sdma(out, in_, key, reads=(), writes=()):
        return P.dma("sp", lambda e: e.dma_start(out=out, in_=in_), key,
                     [t.b if isinstance(t, Tile) else t for t in reads],
                     [t.b if isinstance(t, Tile) else t for t in writes])

    def wload(src_ap, kch, cols):
        rr["ws"] += 1
        sl = wsl[rr["ws"] % NWS]
        i = rr["ws"] % NWS
        view = sl.t[:, 0:kch * cols].rearrange("p (k c) -> p k c", k=kch)
        P.dma("pool", lambda e: e.dma_start(out=view, in_=src_ap), f"w{i}", [], [sl.b])
        return sl, view

    def parcol(c0, n=1):
        return par.t[:, c0:c0 + n]

    TTS = [(0, min(512, NT))]
    if NT > 512:
        TTS.append((512, NT))

    def big_mm(wview, kch, col0, rhs_t, rhs_bufs, wslot):
        pb = big()
        fns = []
        for k in range(kch):
            for (a, b) in TTS:
                fns.append(lambda e, k=k, a=a, b=b: e.matmul(
                    pb.t[:, a:b], wview[:, k, col0:col0 + 128], rhs_t[:, k, a:b],
                    start=(k == 0), stop=(k == kch - 1)))
        P.group("pe", fns, [wslot.b] + list(rhs_bufs), [pb.b])
        return pb

    def ones_sum(src_tiles, nparts=128):
        pb = big()
        n = len(src_tiles)
        fns = []
        for i, s in enumerate(src_tiles):
            for (a, b) in TTS:
                fns.append(lambda e, i=i, s=s, a=a, b=b: e.matmul(
                    pb.t[:, a:b], onesb.t[:, :], s.t[:, a:b], start=(i == 0), stop=(i == n - 1)))
        P.group("pe", fns, [onesb.b] + [s.b for s in src_tiles], [pb.b])
        return pb

    def rstd_from(pb, scale, out_tile):
        act(tmpN.t[:, :], pb.t[:, 0:NT], ACT.Sqrt, [pb], [tmpN], scale=scale, bias=epsc.t[:, 0:1])
        recip(out_tile.t[:, :], tmpN.t[:, :], [tmpN], [out_tile])

    epsc = P.tile("epsc", [128, 1])
    A_("dve", lambda e: e.memset(epsc.t[:, :], EPS), [], [epsc])
    sdma(cst.t[:, :], const_d, "c0", [], [cst])
    sdma(fnw.t[:, :], fnw_d, "c1", [], [fnw])
    if PIPE:
        sdma(maskt.t[:, :], mask_d, "c2", [], [maskt])
    A_("dve", lambda e: e.tensor_copy(out=idb.t[:, :], in_=cst.t[:, cfg.c_id:cfg.c_id + 128]), [cst], [idb])
    A_("dve", lambda e: e.tensor_copy(out=onesb.t[:, :], in_=cst.t[:, cfg.c_ones:cfg.c_ones + 128]), [cst], [onesb])
    TRILI = lambda C: (cst.t[0:C, cfg.c_trili:cfg.c_trili + 65] if C == 64
                       else cst.t[0:C, cfg.c_trili4:cfg.c_trili4 + C + 1])
    SGT = lambda C: cst.t[0:C, cfg.c_sgt:cfg.c_sgt + C]
    MSU = lambda C: cst.t[0:C, cfg.c_msu:cfg.c_msu + C]
    MIU = lambda C: cst.t[0:C, cfg.c_miu:cfg.c_miu + C]
    ONESF = lambda C: cst.t[0:C, cfg.c_ones:cfg.c_ones + 128]

    def norm_to_h(pcol):
        srcs = []
        pb = big()
        fns = []
        reads = [onesb.b]
        for k in range(KC):
            s = sq[k % 2]
            act(s.t[:, :], xT[:, k, :], ACT.Square, [b_x[k]], [s])
            for (a, b) in TTS:
                P.op("pe", lambda e, k=k, s=s, a=a, b=b: e.matmul(
                    pb.t[:, a:b], onesb.t[:, :], s.t[:, a:b], start=(k == 0), stop=(k == KC - 1)),
                    [onesb.b, s.b], [pb.b], signal=True)
        rstd_from(pb, 1.0 / D, rstdB)
        for k in range(KC):
            stt(hT[:, k, :], xT[:, k, :], parcol(pcol + k), rstdB.t[:, :], ALU.mult, ALU.mult,
                [b_x[k], par, rstdB], [b_h[k]])

    def unit_cols(u):
        if u < NCH:
            return u * 64, 64
        return TP + (u - NCH) * 4, 4

    def ab_proj():
        pss = []
        for u in range(NU):
            c0, C = unit_cols(u)
            ps = small()
            fns = [lambda e, k=k, ps=ps, c0=c0, C=C: e.matmul(
                ps.t[0:C, 0:2 * NH], hT[:, k, c0:c0 + C], wab_sb.t[:, k, :],
                start=(k == 0), stop=(k == KC - 1)) for k in range(KC)]
            P.group("pe", fns, [wab_sb.b] + b_h, [ps.b])
            tt(t1[u].t[0:C, :], ps.t[0:C, 0:NH], parcol(cfg.p_dtb, NH)[0:C, :], ALU.add, [ps, par], [t1[u]])
            act(bt[u].t[0:C, :], ps.t[0:C, NH:2 * NH], ACT.Sigmoid, [ps], [bt[u]])
        for u in range(NU):
            c0, C = unit_cols(u)
            act(t1[u].t[0:C, :], t1[u].t[0:C, :], ACT.Exp, [t1[u]], [t1[u]])
        for u in range(NU):
            c0, C = unit_cols(u)
            act(t1[u].t[0:C, :], t1[u].t[0:C, :], ACT.Ln, [t1[u]], [t1[u]], bias=onec.t[0:C, 0:1])
            tt(gt[u].t[0:C, :], t1[u].t[0:C, :], negA.t[0:C, :], ALU.mult, [t1[u], negA], [gt[u]])
        for u in range(NU):
            c0, C = unit_cols(u)
            ps = small()
            mm(ps.t[0:C, 0:NH], TRILI(C)[:, 0:C], gt[u].t[0:C, :], [cst, gt[u]], [ps])
            act(eG[u].t[0:C, :], ps.t[0:C, 0:NH], ACT.Exp, [ps], [eG[u]])
            ts(neG[u].t[0:C, :], eG[u].t[0:C, :], -1.0, ALU.mult, [eG[u]], [neG[u]])
            ps2 = small()
            mm(ps2.t[:, 0:NH], ONESF(C), gt[u].t[0:C, :], [cst, gt[u]], [ps2])
            act(geb[u].t[:, :], ps2.t[:, 0:NH], ACT.Exp, [ps2], [geb[u]])

    def delta_head(hd, par2, st, l, sset):
        q_, k_, v_, z_ = qn[par2], kn[par2], vb[par2], zs[par2]
        units = list(range(NU))
        kkq = {}
        for u in units:
            c0, C = unit_cols(u)
            ts(gm[u].t[0:C, 0:C + 1], TRILI(C), gt[u].t[0:C, hd:hd + 1], ALU.mult, [cst, gt[u]], [gm[u]])
            ps = small()
            mm(ps.t[0:C, 0:C + 1], SGT(C), gm[u].t[0:C, 0:C + 1], [cst, gm[u]], [ps])
            act(E[u].t[0:C, 0:C + 1], ps.t[0:C, 0:C + 1], ACT.Exp, [ps], [E[u]])
        for u in units:
            c0, C = unit_cols(u)
            tt(Es[u].t[0:C, 0:C], E[u].t[0:C, 0:C], MSU(C), ALU.mult, [E[u], cst], [Es[u]])
            tt(Ei[u].t[0:C, 0:C], E[u].t[0:C, 0:C], MIU(C), ALU.mult, [E[u], cst], [Ei[u]])
            ps = small()
            P.group("pe", [
                lambda e, ps=ps, c0=c0, C=C: e.matmul(ps.t[0:C, 0:C], k_.t[:, c0:c0 + C], k_.t[:, c0:c0 + C],
                                                      start=True, stop=True),
                lambda e, ps=ps, c0=c0, C=C: e.matmul(ps.t[0:C, 64:64 + C], k_.t[:, c0:c0 + C], q_.t[:, c0:c0 + C],
                                                      start=True, stop=True)],
                [k_.b, q_.b], [ps.b])
            stt(LL[u].t[0:C, 0:C], ps.t[0:C, 0:C], bt[u].t[0:C, hd:hd + 1], Es[u].t[0:C, 0:C],
                ALU.mult, ALU.mult, [ps, bt[u], Es[u]], [LL[u]])
            tt(pmT[u].t[0:C, 0:C], ps.t[0:C, 64:64 + C], Ei[u].t[0:C, 0:C], ALU.mult, [ps, Ei[u]], [pmT[u]])
        for u in units:
            c0, C = unit_cols(u)
            ps = small()
            tr(ps.t[0:C, 0:C], LL[u].t[0:C, 0:C], cst.t[0:C, cfg.c_id:cfg.c_id + C], [LL[u], cst], [ps])
            act(LL[u].t[0:C, C:2 * C], ps.t[0:C, 0:C], ACT.Copy, [ps], [LL[u]])
            ps2 = small()
            pv2 = ps2.t[:, :].bitcast(BF16)
            tr(pv2[0:C, 0:128], v_.t[:, c0:c0 + C], idb.t[:, :], [v_, idb], [ps2])
            act(Vtm[u].t[0:C, :], pv2[0:C, 0:128], ACT.Copy, [ps2], [Vtm[u]])
            ps3 = small()
            pv3 = ps3.t[:, :].bitcast(BF16)
            tr(pv3[0:C, 0:128], k_.t[:, c0:c0 + C], idb.t[:, :], [k_, idb], [ps3])
            act(ke[u].t[0:C, :], pv3[0:C, 0:128], ACT.Copy, [ps3, E[u]], [ke[u]], scale=E[u].t[0:C, C:C + 1])
        def MMc(C, i):
            if C == 64:
                return cst.t[0:64, cfg.c_mm64 + i * 128: cfg.c_mm64 + (i + 1) * 128]
            return cst.t[0:4, cfg.c_mm4 + i * 8: cfg.c_mm4 + (i + 1) * 8]

        def IIc(C):
            if C == 64:
                return cst.t[0:64, cfg.c_ii64:cfg.c_ii64 + 128]
            return cst.t[0:4, cfg.c_ii4:cfg.c_ii4 + 8]
        tcur = {u: 0 for u in units}
        nlev = {u: (6 if unit_cols(u)[1] == 64 else 2) for u in units}
        pr = [u for u in units if u < NCH]
        sm_ = [u for u in units if u >= NCH]
        pbatches = [pr[i:i + IB] for i in range(0, len(pr), IB)]
        sbatches = [sm_[i:i + IB] for i in range(0, len(sm_), IB)]

        def inv_gen(batch):
            for u in batch:
                c0, C = unit_cols(u)
                tt(BBs[u].t[0:C, 0:2 * C], LL[u].t[0:C, :], MMc(C, 0), ALU.mult, [LL[u], cst], [BBs[u]])
                tt(TTp[u][0].t[0:C, 0:2 * C], IIc(C), BBs[u].t[0:C, 0:2 * C], ALU.subtract, [cst, BBs[u]],
                   [TTp[u][0]])
                tcur[u] = 0
            yield
            for lev in range(1, 6):
                act_units = [u for u in batch if lev < nlev[u]]
                if not act_units:
                    continue
                for u in act_units:
                    c0, C = unit_cols(u)
                    To = TTp[u][tcur[u]]
                    tt(BBs[u].t[0:C, 0:2 * C], LL[u].t[0:C, :], MMc(C, lev), ALU.mult, [LL[u], cst], [BBs[u]])
                    ps = small()
                    P.group("pe", [
                        lambda e, ps=ps, C=C, u=u, To=To: e.matmul(ps.t[0:C, 0:C], BBs[u].t[0:C, C:2 * C],
                                                                 To.t[0:C, 0:C], start=True, stop=True),
                        lambda e, ps=ps, C=C, u=u, To=To: e.matmul(ps.t[0:C, C:2 * C], BBs[u].t[0:C, 0:C],
                                                                 To.t[0:C, C:2 * C], start=True, stop=True)],
                        [BBs[u].b, To.b], [ps.b])
                    act(PQ[u].t[0:C, 0:2 * C], ps.t[0:C, 0:2 * C], ACT.Copy, [ps], [PQ[u]])
                yield
                for u in act_units:
                    c0, C = unit_cols(u)
                    To = TTp[u][tcur[u]]
                    Tn = TTp[u][1 - tcur[u]]
                    ps2 = small()
                    P.group("pe", [
                        lambda e, ps2=ps2, C=C, u=u, To=To: e.matmul(ps2.t[0:C, 0:C], To.t[0:C, C:2 * C],
                                                                   PQ[u].t[0:C, 0:C], start=True, stop=True),
                        lambda e, ps2=ps2, C=C, u=u, To=To: e.matmul(ps2.t[0:C, C:2 * C], To.t[0:C, 0:C],
                                                                   PQ[u].t[0:C, C:2 * C], start=True, stop=True)],
                        [PQ[u].b, To.b], [ps2.b])
                    tt(Tn.t[0:C, 0:2 * C], To.t[0:C, 0:2 * C], ps2.t[0:C, 0:2 * C], ALU.subtract, [To, ps2], [Tn])
                    tcur[u] = 1 - tcur[u]
                yield
            for u in batch:
                c0, C = unit_cols(u)
                act(Ttb[u].t[0:C, 0:C], TTp[u][tcur[u]].t[0:C, 0:C], ACT.Copy, [TTp[u][tcur[u]]], [Ttb[u]])
            yield

        def p2_unit(u, ri):
            c0, C = unit_cols(u)
            Tt = Ttb[u]
            if u < NCH:
                S32t, Sbft = S32[hd], Sbf[hd]
            else:
                s = u - NCH
                S32t = Tile(Ss32[sset].t[:, s, :], Ss32[sset].b)
                Sbft = Ssbf[sset][s]
            psK = small()
            mm(psK.t[0:C, :], k_.t[:, c0:c0 + C], Sbft.t[:, :], [k_, Sbft], [psK])
            psQ = small()
            mm(psQ.t[0:C, :], q_.t[:, c0:c0 + C], Sbft.t[:, :], [q_, Sbft], [psQ])
            stt(Yt[ri].t[0:C, :], psK.t[0:C, :], neG[u].t[0:C, hd:hd + 1], Vtm[u].t[0:C, :],
                ALU.mult, ALU.add, [psK, neG[u], Vtm[u]], [Yt[ri]])
            act(o1s[ri].t[0:C, :], psQ.t[0:C, :], ACT.Copy, [psQ, eG[u]], [o1s[ri]], scale=eG[u].t[0:C, hd:hd + 1])
            yield
            psU = small()
            mm(psU.t[0:C, :], Tt.t[0:C, 0:C], Yt[ri].t[0:C, :], [Tt, Yt[ri]], [psU])
            act(ut[ri].t[0:C, :], psU.t[0:C, :], ACT.Copy, [psU, bt[u]], [ut[ri]], scale=bt[u].t[0:C, hd:hd + 1])
            yield
            psD = small()
            mm(psD.t[:, :], ke[u].t[0:C, :], ut[ri].t[0:C, :], [ke[u], ut[ri]], [psD])
            if u < NCH:
                stt(S32t.t[:, :], S32t.t[:, :], geb[u].t[:, hd:hd + 1], psD.t[:, :], ALU.mult, ALU.add,
                    [S32t, geb[u], psD], [S32t])
                act(Sbft.t[:, :], S32t.t[:, :], ACT.Copy, [S32t], [Sbft])
            else:
                s = u - NCH
                sn = Ssn[0][s]
                stt(sn.t[:, :], S32t.t, geb[u].t[:, hd:hd + 1], psD.t[:, :], ALU.mult, ALU.add,
                    [S32t, geb[u], psD], [sn])
                out_ticks.append(sdma(o_dl_s[l, st * SS + s, hd], sn.t[:, :], f"ods{s % 2}", [sn], []))
            psO = small()
            mm(psO.t[0:C, :], pmT[u].t[0:C, 0:C], ut[ri].t[0:C, :], [pmT[u], ut[ri]], [psO])
            tt(ot[ri].t[0:C, :], psO.t[0:C, :], o1s[ri].t[0:C, :], ALU.add, [psO, o1s[ri]], [ot[ri]])
            act(junk[ri].t[0:C, :], ot[ri].t[0:C, :], ACT.Square, [ot[ri]], [junk[ri], ssq[ri]],
                accum=ssq[ri].t[0:C, 0:1])
            act(rsq[ri].t[0:C, :], ssq[ri].t[0:C, :], ACT.Sqrt, [ssq[ri]], [rsq[ri]], scale=1.0 / 128,
                bias=epsc.t[0:C, 0:1])
            recip(rsq[ri].t[0:C, :], rsq[ri].t[0:C, :], [rsq[ri]], [rsq[ri]])
            ts(onb[ri].t[0:C, :], ot[ri].t[0:C, :], rsq[ri].t[0:C, 0:1], ALU.mult, [ot[ri], rsq[ri]], [onb[ri]])
            yield
            psT = small()
            pvT = psT.t[:, :].bitcast(BF16)
            tr(pvT[:, 0:C], onb[ri].t[0:C, :], idb.t[0:C, 0:C], [onb[ri], idb], [psT])
            stt(yh[:, NG + hd, c0:c0 + C], pvT[:, 0:C], parcol(cfg.p_dnw), z_.t[:, c0:c0 + C],
                ALU.mult, ALU.mult, [psT, par, z_], [b_yh[NG + hd]])
            yield

        def p2_chain(us, ri0, alt=True):
            for i, u in enumerate(us):
                yield from p2_unit(u, (ri0 + i) % NR if alt else ri0)

        def run(*gens):
            gens = list(gens)
            while gens:
                for g in list(gens):
                    try:
                        next(g)
                    except StopIteration:
                        gens.remove(g)

        run(inv_gen(pbatches[0]))
        for i in range(1, len(pbatches)):
            run(inv_gen(pbatches[i]), p2_chain(pbatches[i - 1], 0))
        lastp = p2_chain(pbatches[-1], 0)
        if sbatches:
            run(inv_gen(sbatches[0]), lastp)
            for sb in sbatches[1:]:
                run(inv_gen(sb))
        else:
            run(lastp)
        for i in range(0, len(sm_), 2):
            pair = sm_[i:i + 2]
            run(*[p2_chain([u], j, alt=False) for j, u in enumerate(pair)])

    onec = P.tile("onec", [128, 1])
    A_("dve", lambda e: e.memset(onec.t[:, :], 1.0), [], [onec])

    def conv_apply(Ut, W, wcol0, out_t, reads_extra):
        hw = W - 1
        so = hw + TP
        sw = hw + 4
        Us = Ut.t[:, so:so + SS * sw].rearrange("p (s w) -> p s w", s=SS)
        outs = out_t.t[:, TP:NT].rearrange("p (s w) -> p s w", s=SS)
        act(out_t.t[:, 0:TP], Ut.t[:, 0:TP], ACT.Copy, [Ut, par], [out_t], scale=parcol(wcol0))
        act(outs, Us[:, :, 0:4], ACT.Copy, [Ut, par], [out_t], scale=parcol(wcol0))
        for i in range(1, W):
            stt(out_t.t[:, 0:TP], Ut.t[:, i:i + TP], parcol(wcol0 + i), out_t.t[:, 0:TP], ALU.mult, ALU.add,
                [Ut, par, out_t], [out_t])
            stt(outs, Us[:, :, i:i + 4], parcol(wcol0 + i), outs, ALU.mult, ALU.add, [Ut, par, out_t], [out_t])

    def evac_halo(pb, Ut, W):
        hw = W - 1
        so = hw + TP
        sw = hw + 4
        Us = Ut.t[:, so:so + SS * sw].rearrange("p (s w) -> p s w", s=SS)
        act(Ut.t[:, hw:hw + TP], pb.t[:, 0:TP], ACT.Copy, [pb], [Ut])
        act(Us[:, :, hw:hw + 4], pb.t[:, TP:NT].rearrange("p (s w) -> p s w", s=SS), ACT.Copy, [pb], [Ut])
        return Us

    def run_pass(l, st):
        src = xin[st] if l == 0 else xs[st]
        sdma(xT[:, :, :], src, "xld", ([b_xs[st]] if l > 0 else []), b_x)
        fresh = (st == 0 and (not PIPE or l == 0))
        if PIPE and st == 0 and l > 0:
            r0 = xch_dst[0:128, :]
            sdma(S32all[:, :, :], r0[:, 0:NH * 128].rearrange("p (h v) -> p h v", h=NH), "xl0", [b_xdst],
                 [t.b for t in S32])
            sdma(haloA.t[:, :, :], r0[:, NH * 128:NH * 128 + NG * 2].rearrange("p (g t) -> p g t", g=NG), "xl1",
                 [b_xdst], [haloA])
            sdma(haloB.t[:, :, :, :], r0[:, NH * 128 + NG * 2:XF].rearrange("p (h w t) -> p h w t", h=NH, w=3),
                 "xl2", [b_xdst], [haloB])
            ts(S32all[:, :, :], S32all[:, :, :], maskt.t[:, 0:1], ALU.mult, [t.b for t in S32] + [maskt],
               [t.b for t in S32])
            ts(haloA.t[:, :, :], haloA.t[:, :, :], maskt.t[:, 0:1], ALU.mult, [haloA, maskt], [haloA])
            ts(haloB.t[:, :, :, :], haloB.t[:, :, :, :], maskt.t[:, 0:1], ALU.mult, [haloB, maskt], [haloB])
        norm_to_h(cfg.p_n1)
        P.dma("pool", lambda e: e.dma_start(out=wab_sb.t[:, :, :], in_=w_ab[l]), "wab", [], [wab_sb.b])
        ab_proj()
        def grpA_part1(g):
            p = g % 2
            sl, wv = wload(w_A[l, g], KC, 384)
            Ut = U[0]
            hw = 2
            so = hw + TP
            sw = hw + 4
            Us = Ut.t[:, so:so + SS * sw].rearrange("p (s w) -> p s w", s=SS)
            for i in range(3):
                pb = big_mm(wv, KC, i * 128, hT, b_h, sl)
                if i == 0:
                    act(pj[0].t[:, :], pb.t[:, 0:NT], ACT.Copy, [pb], [pj[0]])
                elif i == 1:
                    act(pj[1].t[:, :], pb.t[:, 0:NT], ACT.Copy, [pb], [pj[1]])
                else:
                    tt(Ut.t[:, hw:hw + TP], pb.t[:, 0:TP], pj[1].t[:, 0:TP], ALU.mult, [pb, pj[1]], [Ut])
                    tt(Us[:, :, hw:hw + 4], pb.t[:, TP:NT].rearrange("p (s w) -> p s w", s=SS),
                       pj[1].t[:, TP:NT].rearrange("p (s w) -> p s w", s=SS), ALU.mult, [pb, pj[1]], [Ut])
            if fresh:
                A_("dve", lambda e, Ut=Ut: e.memset(Ut.t[:, 0:2], 0.0), [], [Ut])
            else:
                act(Ut.t[:, 0:2], haloA.t[:, g, :], ACT.Copy, [haloA], [Ut])
            sdma(Us[:, :, 0:2], s_ca[l, :, g, st * SS:(st + 1) * SS, :], "sca", [], [Ut])
            act(haloA.t[:, g, :], Ut.t[:, TP:TP + 2], ACT.Copy, [Ut], [haloA])
            out_ticks.append(sdma(o_ca_s[l, :, g, st * SS:(st + 1) * SS, :], Us[:, :, 4:6], "ocas", [Ut], []))
            conv_apply(Ut, 3, cfg.p_caw + g * 3, acc, [])
            tt(cvp[p][0].t[:, :], pj[0].t[:, :], acc.t[:, :], ALU.mult, [pj[0], acc], [cvp[p][0]])
            act(sqp[p][0].t[:, :], cvp[p][0].t[:, :], ACT.Square, [cvp[p][0]], [sqp[p][0]])

        def grpA_part2(g):
            p = g % 2
            pb = ones_sum([sqp[p][0]])
            rstd_from(pb, 1.0 / 128, cvp[p][1])
            stt(yh[:, g, 0:NT], cvp[p][0].t[:, :], parcol(cfg.p_canw + g), cvp[p][1].t[:, :], ALU.mult, ALU.mult,
                [cvp[p][0], par, cvp[p][1]], [b_yh[g]])

        grpA_part1(0)
        for g in range(NG):
            if g + 1 < NG:
                grpA_part1(g + 1)
            grpA_part2(g)
        if st == NST - 1:
            out_ticks.append(sdma(o_ca_p[l], haloA.t[:, :, :], "ocap", [haloA], []))
        def headB_part1(hd):
            p2 = hd % 2
            if fresh:
                A_("dve", lambda e, hd=hd: e.memset(S32[hd].t[:, :], 0.0), [], [S32[hd]])
                A_("dve", lambda e, hd=hd: e.memset(Sbf[hd].t[:, :], 0.0), [], [Sbf[hd]])
            elif st == 0:
                act(Sbf[hd].t[:, :], S32[hd].t[:, :], ACT.Copy, [S32[hd]], [Sbf[hd]])
            sl, wv = wload(w_B[l, hd], KC, 512)
            for i in range(3):
                pb = big_mm(wv, KC, i * 128, hT, b_h, sl)
                Ut = U[i]
                Us = evac_halo(pb, Ut, 4)
                if fresh:
                    A_("dve", lambda e, Ut=Ut: e.memset(Ut.t[:, 0:3], 0.0), [], [Ut])
                else:
                    act(Ut.t[:, 0:3], haloB.t[:, hd, i, :], ACT.Copy, [haloB], [Ut])
                sdma(Us[:, :, 0:3], s_cq[l, :, hd, i, st * SS:(st + 1) * SS, :], f"scq{i}", [], [Ut])
                act(haloB.t[:, hd, i, :], Ut.t[:, TP:TP + 3], ACT.Copy, [Ut], [haloB])
                out_ticks.append(sdma(o_cq_s[l, :, hd, i, st * SS:(st + 1) * SS, :], Us[:, :, 4:7], f"ocqs{i}",
                                      [Ut], []))
            pb = big_mm(wv, KC, 3 * 128, hT, b_h, sl)
            act(zs[p2].t[:, :], pb.t[:, 0:NT], ACT.Silu, [pb], [zs[p2]])
            for i in range(3):
                conv_apply(U[i], 4, cfg.p_cbw + (hd * 3 + i) * 4, acc, [])
                if i == 2:
                    act(vb[p2].t[:, :], acc.t[:, :], ACT.Silu, [acc], [vb[p2]])
                else:
                    act(cvp[p2][i].t[:, :], acc.t[:, :], ACT.Silu, [acc], [cvp[p2][i]])
            for i in range(2):
                act(sqp[p2][i].t[:, :], cvp[p2][i].t[:, :], ACT.Square, [cvp[p2][i]], [sqp[p2][i]])

        def headB_part2(hd):
            p2 = hd % 2
            sset = hd % 2
            sst = Ss32[sset]
            sdma(sst.t[:, :, :], s_dl[l, st * SS:(st + 1) * SS, hd].rearrange("s k v -> k s v"), "sdl",
                 [], [sst])
            for s_ in range(SS):
                act(Ssbf[sset][s_].t[:, :], sst.t[:, s_, :], ACT.Copy, [sst], [Ssbf[sset][s_]])
            for i in range(2):
                pb = ones_sum([sqp[p2][i]])
                rstd_from(pb, 1.0, pj[i])
                if i == 0:
                    stt(qn[p2].t[:, :], cvp[p2][0].t[:, :], float(128 ** -0.5), pj[0].t[:, :], ALU.mult, ALU.mult,
                        [cvp[p2][0], pj[0]], [qn[p2]])
                else:
                    tt(kn[p2].t[:, :], cvp[p2][1].t[:, :], pj[1].t[:, :], ALU.mult, [cvp[p2][1], pj[1]], [kn[p2]])
            delta_head(hd, p2, st, l, hd % 2)
            if st == NST - 1:
                out_ticks.append(sdma(o_dl_p[l, hd], S32[hd].t[:, :], "odp", [S32[hd]], []))

        headB_part1(0)
        for hd in range(NH):
            if hd + 1 < NH:
                headB_part1(hd + 1)
            headB_part2(hd)
        if st == NST - 1:
            out_ticks.append(sdma(o_cq_p[l], haloB.t[:, :, :, :], "ocqp", [haloB], []))
            if PIPE and l < L - 1:
                sdma(xch_src[:, 0:NH * 128].rearrange("p (h v) -> p h v", h=NH), S32all[:, :, :], "xs0",
                     [t.b for t in S32], [b_xsrc])
                sdma(xch_src[:, NH * 128:NH * 128 + NG * 2].rearrange("p (g t) -> p g t", g=NG), haloA.t[:, :, :],
                     "xs1", [haloA], [b_xsrc])
                sdma(xch_src[:, NH * 128 + NG * 2:XF].rearrange("p (h w t) -> p h w t", h=NH, w=3),
                     haloB.t[:, :, :, :], "xs2", [haloB], [b_xsrc])
                P.dma("pool", lambda e: e.collective_compute(
                    "AllGather", ALU.bypass, replica_groups=cfg.GROUPS,
                    ins=[xch_src_t.ap().opt()], outs=[xch_dst_t.ap().opt()]), "cc", [b_xsrc], [b_xdst], inc=1)
        for ot_ in range(cfg.NOT):
            sl, wv = wload(w_o[l, ot_], YC, cfg.CWO)
            for j in range(cfg.CWO // 128):
                oc = ot_ * (cfg.CWO // 128) + j
                pb = big_mm(wv, YC, j * 128, yh, b_yh[0:YC], sl)
                tt(xT[:, oc, :], pb.t[:, 0:NT], xT[:, oc, :], ALU.add, [pb, b_x[oc]], [b_x[oc]])
        norm_to_h(cfg.p_n2)
        for hg in range(cfg.NHG):
            for ut_ in range(cfg.NUT):
                sl, wv = wload(w_u[l, hg * cfg.NUT + ut_], KC, 512)
                for j in range(4):
                    hc = ut_ * 4 + j
                    pb = big_mm(wv, KC, j * 128, hT, b_h, sl)
                    act(tmpN.t[:, :], pb.t[:, 0:NT], ACT.Relu, [pb], [tmpN])
                    tt(yh[:, hc, 0:NT], tmpN.t[:, :], tmpN.t[:, :], ALU.mult, [tmpN], [b_yh[hc]])
            for ot_ in range(cfg.NOT):
                sl, wv = wload(w_d[l, hg, ot_], cfg.HK, cfg.CWO)
                for j in range(cfg.CWO // 128):
                    oc = ot_ * (cfg.CWO // 128) + j
                    pb = big_mm(wv, cfg.HK, j * 128, yh, b_yh[0:cfg.HK], sl)
                    tt(xT[:, oc, :], pb.t[:, 0:NT], xT[:, oc, :], ALU.add, [pb, b_x[oc]], [b_x[oc]])
        if l == L - 1:
            pb = big()
            for k in range(KC):
                s = sq[k % 2]
                act(s.t[:, :], xT[:, k, :], ACT.Square, [b_x[k]], [s])
                for (a, b) in TTS:
                    P.op("pe", lambda e, k=k, s=s, a=a, b=b, pb=pb: e.matmul(
                        pb.t[:, a:b], onesb.t[:, :], s.t[:, a:b], start=(k == 0), stop=(k == KC - 1)),
                        [onesb.b, s.b], [pb.b], signal=True)
            rstd_from(pb, 1.0 / D, rstdB)
            for k in range(KC):
                stt(xT[:, k, :], xT[:, k, :], fnw.t[:, k:k + 1], rstdB.t[:, :], ALU.mult, ALU.mult,
                    [b_x[k], fnw, rstdB], [b_x[k]])
            out_ticks.append(sdma(yout[st], xT[:, :, :], "xst", b_x, []))
        else:
            sdma(xs[st], xT[:, :, :], "xst", b_x, [b_xs[st]])

    for l in range(L):
        sdma(par.t[:, :], par_d[l], "par", [], [par])
        act(negA.t[:, :], par.t[:, cfg.p_alog:cfg.p_alog + NH], ACT.Exp, [par], [negA])
        ts(negA.t[:, :], negA.t[:, :], -1.0, ALU.mult, [negA], [negA])
        for st in range(NST):
            run_pass(l, st)
    P.wait_all("sp", out_ticks)
    P.finish()
    return nc


def prep_weights(cfg, inp):
    D, L, KC, NG, NH = cfg.D, cfg.L, cfg.KC, cfg.NG, cfg.NH
    DC = cfg.DC
    w_in = np.asarray(inp["w_in"], np.float32)

    def ktile(w):
        c = w.shape[-1]
        return np.ascontiguousarray(w.reshape(L, -1, 128, c).transpose(0, 2, 1, 3))
    oB, oC, oH, oQ, oZ, oa = 0, DC, 2 * DC, 3 * DC, 6 * DC, 7 * DC
    w_ab = ktile(w_in[:, :, oa:oa + 2 * NH])
    wA = np.empty((L, NG, 128, KC, 384), np.float32)
    for g in range(NG):
        cols = np.concatenate([np.arange(o + g * 128, o + (g + 1) * 128) for o in (oB, oC, oH)])
        wA[:, g] = ktile(w_in[:, :, cols])
    wB = np.empty((L, NH, 128, KC, 512), np.float32)
    for h in range(NH):
        cols = np.concatenate([np.arange(o + h * 128, o + (h + 1) * 128)
                               for o in (oQ, oQ + DC, oQ + 2 * DC, oZ)])
        wB[:, h] = ktile(w_in[:, :, cols])
    w_out = np.asarray(inp["w_out"], np.float32)
    wo = np.stack([ktile(w_out[:, :, t * cfg.CWO:(t + 1) * cfg.CWO]) for t in range(cfg.NOT)], 1)
    w_up = np.asarray(inp["w_up"], np.float32)
    wu = np.stack([ktile(w_up[:, :, t * 512:(t + 1) * 512]) for t in range(cfg.DFF // 512)], 1)
    w_dn = np.asarray(inp["w_down"], np.float32)
    wd = np.empty((L, cfg.NHG, cfg.NOT, 128, cfg.HK, cfg.CWO), np.float32)
    for hg in range(cfg.NHG):
        for t in range(cfg.NOT):
            wd[:, hg, t] = ktile(w_dn[:, hg * cfg.HGS:(hg + 1) * cfg.HGS, t * cfg.CWO:(t + 1) * cfg.CWO])
    par = np.zeros((L, 128, cfg.NPAR), np.float32)
    fm = lambda v: v.reshape(L, -1, 128).transpose(0, 2, 1)
    par[:, :, cfg.p_n1:cfg.p_n1 + KC] = fm(np.asarray(inp["norm_mix_w"]))
    par[:, :, cfg.p_n2:cfg.p_n2 + KC] = fm(np.asarray(inp["norm_ffn_w"]))
    caw = np.asarray(inp["conv_a_w"]).reshape(L, 3, NG, 128).transpose(0, 3, 2, 1)
    par[:, :, cfg.p_caw:cfg.p_caw + NG * 3] = caw.reshape(L, 128, NG * 3)
    par[:, :, cfg.p_canw:cfg.p_canw + NG] = fm(np.asarray(inp["conv_a_norm_w"]))
    cbw = np.asarray(inp["conv_qkv_w"]).reshape(L, 4, 3, NH, 128).transpose(0, 4, 3, 2, 1)
    par[:, :, cfg.p_cbw:cfg.p_cbw + NH * 12] = cbw.reshape(L, 128, NH * 12)
    par[:, :, cfg.p_dnw] = np.asarray(inp["dn_norm_w"])
    par[:, :, cfg.p_dtb:cfg.p_dtb + NH] = np.asarray(inp["dt_bias"])[:, None, :]
    par[:, :, cfg.p_alog:cfg.p_alog + NH] = np.asarray(inp["a_log"])[:, None, :]
    fnw = np.ascontiguousarray(np.asarray(inp["final_norm_w"], np.float32).reshape(-1, 128).T)
    return dict(w_ab=w_ab, w_A=wA, w_B=wB, w_o=wo, w_u=wu, w_d=wd, par=par, fnw=fnw,
                consts=make_consts(cfg))


def prep_core(cfg, inp, xp_seq, seq0):
    D, L, KC, NG, NH, NST, TP, SS, NT = cfg.D, cfg.L, cfg.KC, cfg.NG, cfg.NH, cfg.NST, cfg.TP, cfg.SS, cfg.NT
    NSEQ = cfg.NSEQ
    xsamp = np.asarray(inp["x_sample"], np.float32)[seq0:seq0 + NSEQ]
    xin = np.empty((NST, 128, KC, NT), np.float32)
    for st in range(NST):
        tok = np.concatenate([xp_seq[st * TP:(st + 1) * TP], xsamp[st * SS:(st + 1) * SS].reshape(-1, D)], 0)
        xin[st] = tok.T.reshape(KC, 128, NT).transpose(1, 0, 2)
    sca = np.asarray(inp["state_conv_a"], np.float32)[:, seq0:seq0 + NSEQ]
    s_ca = np.ascontiguousarray(sca.reshape(L, NSEQ, 2, NG, 128).transpose(0, 4, 3, 1, 2))
    scq = np.asarray(inp["state_conv_qkv"], np.float32)[:, seq0:seq0 + NSEQ]
    s_cq = np.ascontiguousarray(scq.reshape(L, NSEQ, 3, 3, NH, 128).transpose(0, 5, 4, 3, 1, 2))
    sdl = np.asarray(inp["state_delta"], np.float32)[:, seq0:seq0 + NSEQ]
    s_dl = np.ascontiguousarray(sdl.transpose(0, 1, 2, 4, 3))
    return dict(xin=xin, s_ca=s_ca, s_cq=s_cq, s_dl=s_dl)


_CACHE = {}
LAYERED = ("w_ab", "w_A", "w_B", "w_o", "w_u", "w_d", "par", "s_ca", "s_cq", "s_dl")


def _role_shift(arr, role):
    z = np.zeros_like(arr[:1])
    return np.concatenate([arr, z], 0) if role == 0 else np.concatenate([z, arr], 0)


def kernel(**inputs):
    cfg = Cfg(NST=2, TP=512, SS=8, PIPE=True)
    NCORES = 8
    if "nc" not in _CACHE:
        _CACHE["nc"] = build_program(cfg)
    nc = _CACHE["nc"]
    wts = prep_weights(cfg, inputs)
    xp = np.asarray(inputs["x_prompt"], np.float32)
    B, S, D = xp.shape
    half = cfg.NST * cfg.TP
    shared = []
    for role in range(2):
        d = {}
        for k, v in wts.items():
            d[k] = _role_shift(v, role) if k in LAYERED else v
        d["mask"] = np.full((128, 1), float(role), np.float32)
        shared.append(d)
    in_maps = []
    for c in range(NCORES):
        b, role = c // 2, c % 2
        m = dict(shared[role])
        pc = prep_core(cfg, inputs, xp[b, role * half:(role + 1) * half], c * cfg.NSEQ)
        for k, v in pc.items():
            m[k] = _role_shift(v, role) if k in LAYERED else v
        in_maps.append(m)
    res = run_bass_kernel_spmd(nc, in_maps, core_ids=list(range(NCORES)))
    return assemble(cfg, res.results, B)


def assemble(cfg, results, B):
    D, L, KC, NG, NH, NST, TP, SS, NT, NSEQ = (cfg.D, cfg.L, cfg.KC, cfg.NG, cfg.NH, cfg.NST, cfg.TP,
                                                 cfg.SS, cfg.NT, cfg.NSEQ)
    DC = cfg.DC
    ncores = len(results)
    half = NST * TP
    y_p = np.empty((B, 2 * half, D), np.float32)
    y_s = np.empty((ncores * NSEQ, 4, D), np.float32)
    ca_p = np.empty((L, B, 2, DC), np.float32)
    cq_p = np.empty((L, B, 3, 3 * DC), np.float32)
    dl_p = np.empty((L, B, NH, 128, 128), np.float32)
    ca_s = np.empty((L, ncores * NSEQ, 2, DC), np.float32)
    cq_s = np.empty((L, ncores * NSEQ, 3, 3 * DC), np.float32)
    dl_s = np.empty((L, ncores * NSEQ, NH, 128, 128), np.float32)
    for c, r in enumerate(results):
        b, role = c // 2, c % 2
        st0 = slice(role, role + L)
        yo = np.asarray(r["yout"])
        tok = yo.transpose(0, 3, 2, 1).reshape(NST, NT, D)
        for st in range(NST):
            y_p[b, role * half + st * TP: role * half + (st + 1) * TP] = tok[st, :TP]
            y_s[c * NSEQ + st * SS: c * NSEQ + (st + 1) * SS] = tok[st, TP:].reshape(SS, 4, D)
        sl = slice(c * NSEQ, (c + 1) * NSEQ)
        ca_s[:, sl] = np.asarray(r["o_ca_s"])[st0].transpose(0, 3, 4, 2, 1).reshape(L, NSEQ, 2, DC)
        cq_s[:, sl] = np.asarray(r["o_cq_s"])[st0].transpose(0, 4, 5, 3, 2, 1).reshape(L, NSEQ, 3, 3 * DC)
        dl_s[:, sl] = np.asarray(r["o_dl_s"])[st0].transpose(0, 1, 2, 4, 3)
        if role == 1:
            ca_p[:, b] = np.asarray(r["o_ca_p"])[st0].transpose(0, 3, 2, 1).reshape(L, 2, DC)
            cq_p[:, b] = np.asarray(r["o_cq_p"])[st0].transpose(0, 4, 3, 2, 1).reshape(L, 3, 3 * DC)
            dl_p[:, b] = np.asarray(r["o_dl_p"])[st0].transpose(0, 1, 3, 2)
    return (y_p, y_s, ca_p, cq_p, dl_p, ca_s, cq_s, dl_s)
```

```python
import contextlib
import numpy as np
import concourse.bass as bass
import concourse.mybir as mybir
from concourse.bass_utils import run_bass_kernel_spmd

DT = mybir.dt
F32 = DT.float32
BF16 = DT.bfloat16
ACT = mybir.ActivationFunctionType
ALU = mybir.AluOpType
EPS = 1e-6

ENGS = ["pe", "act", "dve", "pool", "sp"]
EPOCH = 16000


class Buf:
    __slots__ = ("name", "w", "r")

    def __init__(self, name):
        self.name = name
        self.w = None
        self.r = []


class Tile:
    def __init__(self, t, b):
        self.t = t
        self.b = b


class Prog:
    def __init__(self, nc):
        self.nc = nc
        self.q = {e: [] for e in ENGS}
        self.cnt = {}
        self.seen = {e: {} for e in ENGS}
        self.ecnt = {e: 0 for e in ENGS}
        self.dma_last = {}
        self.stack = contextlib.ExitStack()
        self.nbuf = 0

    def sbuf(self, name, shape, dtype=F32):
        return self.stack.enter_context(self.nc.sbuf_tensor("sb_" + name, list(shape), dtype))

    def psum(self, name, shape, dtype=F32):
        return self.stack.enter_context(self.nc.psum_tensor("ps_" + name, list(shape), dtype))

    def buf(self, name=None):
        self.nbuf += 1
        return Buf(name or f"b{self.nbuf}")

    def tile(self, name, shape, dtype=F32):
        return Tile(self.sbuf(name, shape, dtype), self.buf(name))

    def _waits(self, eng, reads, writes):
        need = {}

        def add(t):
            if t is None:
                return
            k, v = t
            if need.get(k, 0) < v:
                need[k] = v
        for b in reads:
            add(b.w)
        for b in writes:
            add(b.w)
            for t in b.r:
                add(t)
        out = []
        seen = self.seen[eng]
        for k, v in need.items():
            if seen.get(k, 0) < v:
                seen[k] = v
                out.append((k, v))
        return out

    def op(self, eng, fn, reads=(), writes=(), signal=True):
        waits = self._waits(eng, reads, writes)
        tick = None
        if signal:
            self.ecnt[eng] += 1
            key = (eng, self.ecnt[eng] // EPOCH)
            self.cnt[key] = self.cnt.get(key, 0) + 1
            tick = (key, self.cnt[key])
        self.q[eng].append((waits, fn, tick, 1))
        if tick is not None:
            for b in reads:
                if len(b.r) > 24:
                    b.r = b.r[-24:] if False else b.r
                b.r.append(tick)
            for b in writes:
                b.w = tick
                b.r = []
        return tick

    def group(self, eng, fns, reads=(), writes=()):
        n = len(fns)
        for i, fn in enumerate(fns):
            if i == n - 1:
                return self.op(eng, fn, reads, writes, signal=True)
            if i == 0:
                self.op(eng, fn, reads, writes, signal=False)
            else:
                self.op(eng, fn, (), (), signal=False)

    def dma(self, eng, fn, semkey, reads=(), writes=(), inc=16):
        key = ("dma", semkey)
        waits = self._waits(eng, reads, writes)
        prev = self.dma_last.get(key)
        if prev is not None and self.seen[eng].get(key, 0) < prev[1]:
            self.seen[eng][key] = prev[1]
            waits.append(prev)
        self.cnt[key] = self.cnt.get(key, 0) + inc
        tick = (key, self.cnt[key])
        self.dma_last[key] = tick
        self.q[eng].append((waits, fn, tick, inc))
        for b in reads:
            b.r.append(tick)
        for b in writes:
            b.w = tick
            b.r = []
        return tick

    def wait_all(self, eng, ticks):
        waits = []
        for t in ticks:
            if t is None:
                continue
            k, v = t
            if self.seen[eng].get(k, 0) < v:
                self.seen[eng][k] = v
                waits.append((k, v))
        self.q[eng].append((waits, None, None, 0))

    def finish(self):
        nc = self.nc
        sems = {}
        for i, k in enumerate(self.cnt):
            sems[k] = self.stack.enter_context(nc.semaphore(f"s{i}"))
        engobj = {"pe": "tensor", "act": "scalar", "dve": "vector", "pool": "gpsimd", "sp": "sync"}
        with nc.Block() as block:
            for e in ENGS:
                items = self.q[e]
                if not items:
                    continue

                def body(engine, items=items):
                    for waits, fn, tick, n in items:
                        for k, v in waits:
                            engine.wait_ge(sems[k], v)
                        if fn is None:
                            continue
                        ins = fn(engine)
                        if tick is not None:
                            ins.then_inc(sems[tick[0]], n)
                getattr(block, engobj[e])(body)
        self.stack.close()


class Cfg:
    def __init__(self, D=2048, L=4, NST=4, TP=512, SS=4, PIPE=False):
        self.D = D
        self.L = L
        self.PIPE = PIPE
        self.GROUPS = [[0, 1], [2, 3], [4, 5], [6, 7]]
        self.LS = L + 1 if PIPE else L
        self.NST = NST
        self.TP = TP
        self.SS = SS
        self.KC = D // 128
        self.DC = D // 2
        self.NG = self.DC // 128
        self.NH = self.DC // 128
        self.DFF = 4 * D
        self.NTS = SS * 4
        self.NT = TP + self.NTS
        self.NCH = TP // 64
        self.NSEQ = NST * SS
        self.CWO = min(512, D)
        self.NOT = D // self.CWO
        self.HGS = min(2048, self.DFF)
        self.NHG = self.DFF // self.HGS
        self.HK = self.HGS // 128
        self.NUT = self.HGS // 512
        self.YC = D // 128
        o = 0
        self.p_n1 = o; o += self.KC
        self.p_n2 = o; o += self.KC
        self.p_caw = o; o += self.NG * 3
        self.p_canw = o; o += self.NG
        self.p_cbw = o; o += self.NH * 3 * 4
        self.p_dnw = o; o += 1
        self.p_dtb = o; o += self.NH
        self.p_alog = o; o += self.NH
        self.NPAR = o
        o = 0
        self.c_id = o; o += 128
        self.c_ones = o; o += 128
        self.c_trili = o; o += 65
        self.c_sgt = o; o += 64
        self.c_msu = o; o += 64
        self.c_miu = o; o += 64
        CS = self.CS = 4 * SS
        self.c_b_trili = o; o += CS + 1
        self.c_b_sgt = o; o += CS
        self.c_b_msu = o; o += CS
        self.c_b_miu = o; o += CS
        self.c_mm64 = o; o += 6 * 128
        self.c_ii64 = o; o += 128
        self.c_b_mm = o; o += 2 * 2 * CS
        self.c_b_ii = o; o += 2 * CS
        self.c_selb = o; o += SS * CS
        self.c_selt = o; o += SS
        self.NCONST = o
        self.WSLOT = 16 * 512


def make_consts(cfg):
    c = np.zeros((128, cfg.NCONST), np.float32)
    c[:, cfg.c_id:cfg.c_id + 128] = np.eye(128, dtype=np.float32)
    c[:, cfg.c_ones:cfg.c_ones + 128] = 1.0
    j = np.arange(64)[:, None]
    m = np.arange(64)[None, :]
    c[:64, cfg.c_trili:cfg.c_trili + 64] = (j <= m)
    c[:64, cfg.c_trili + 64] = 1.0
    c[:64, cfg.c_sgt:cfg.c_sgt + 64] = (j > m)
    c[:64, cfg.c_msu:cfg.c_msu + 64] = (m > j)
    c[:64, cfg.c_miu:cfg.c_miu + 64] = (m >= j)
    CS, SS = cfg.CS, cfg.SS
    jb = np.arange(CS)[:, None]
    mb = np.arange(CS)[None, :]
    same = (jb // 4) == (mb // 4)
    c[:CS, cfg.c_b_trili:cfg.c_b_trili + CS] = same & (jb <= mb)
    c[:CS, cfg.c_b_trili + CS] = 1.0
    c[:CS, cfg.c_b_sgt:cfg.c_b_sgt + CS] = same & (jb > mb)
    c[:CS, cfg.c_b_msu:cfg.c_b_msu + CS] = same & (mb > jb)
    c[:CS, cfg.c_b_miu:cfg.c_b_miu + CS] = same & (mb >= jb)
    sel = (np.arange(CS)[None, :] // 4) == np.arange(SS)[:, None]
    c[:, cfg.c_selb:cfg.c_selb + SS * CS] = sel.reshape(1, -1)
    c[:CS, cfg.c_selt:cfg.c_selt + SS] = sel.T
    mi = np.arange(64)[:, None]
    ci = np.arange(64)[None, :]
    for i in range(6):
        sz = 2 ** i
        M = ((mi // (2 * sz)) == (ci // (2 * sz))) & ((mi // sz) % 2 == 0) & ((ci // sz) % 2 == 1)
        M = M.astype(np.float32)
        c[:64, cfg.c_mm64 + i * 128: cfg.c_mm64 + i * 128 + 64] = M
        c[:64, cfg.c_mm64 + i * 128 + 64: cfg.c_mm64 + (i + 1) * 128] = M.T
        if i < 2:
            c[:CS, cfg.c_b_mm + i * 2 * CS: cfg.c_b_mm + i * 2 * CS + CS] = M[:CS, :CS]
            c[:CS, cfg.c_b_mm + i * 2 * CS + CS: cfg.c_b_mm + (i + 1) * 2 * CS] = M[:CS, :CS].T
    c[:64, cfg.c_ii64:cfg.c_ii64 + 64] = np.eye(64)
    c[:64, cfg.c_ii64 + 64:cfg.c_ii64 + 128] = np.eye(64)
    c[:CS, cfg.c_b_ii:cfg.c_b_ii + CS] = np.eye(CS)
    c[:CS, cfg.c_b_ii + CS:cfg.c_b_ii + 2 * CS] = np.eye(CS)
    return c


def build_program(cfg):
    nc = bass.Bass("TRN2", target_bir_lowering=False)
    P = Prog(nc)
    D, L, NST, TP, SS, KC, NG, NH = cfg.D, cfg.LS, cfg.NST, cfg.TP, cfg.SS, cfg.KC, cfg.NG, cfg.NH
    NT, NTS, NCH, NSEQ = cfg.NT, cfg.NTS, cfg.NCH, cfg.NSEQ
    YC = cfg.YC
    PIPE = cfg.PIPE

    def din(name, shape):
        return nc.dram_tensor(name, list(shape), F32, kind="ExternalInput").ap()

    def dout(name, shape):
        return nc.dram_tensor(name, list(shape), F32, kind="ExternalOutput").ap()

    xin = din("xin", [NST, 128, KC, NT])
    w_ab = din("w_ab", [L, 128, KC, 2 * NH])
    w_A = din("w_A", [L, NG, 128, KC, 384])
    w_B = din("w_B", [L, NH, 128, KC, 512])
    w_o = din("w_o", [L, cfg.NOT, 128, YC, cfg.CWO])
    w_u = din("w_u", [L, cfg.NHG * cfg.NUT, 128, KC, 512])
    w_d = din("w_d", [L, cfg.NHG, cfg.NOT, 128, cfg.HK, cfg.CWO])
    par_d = din("par", [L, 128, cfg.NPAR])
    fnw_d = din("fnw", [128, KC])
    const_d = din("consts", [128, cfg.NCONST])
    s_ca = din("s_ca", [L, 128, NG, NSEQ, 2])
    s_cq = din("s_cq", [L, 128, NH, 3, NSEQ, 3])
    s_dl = din("s_dl", [L, NSEQ, NH, 128, 128])
    yout = dout("yout", [NST, 128, KC, NT])
    o_ca_p = dout("o_ca_p", [L, 128, NG, 2])
    o_cq_p = dout("o_cq_p", [L, 128, NH, 3, 3])
    o_dl_p = dout("o_dl_p", [L, NH, 128, 128])
    o_ca_s = dout("o_ca_s", [L, 128, NG, NSEQ, 2])
    o_cq_s = dout("o_cq_s", [L, 128, NH, 3, NSEQ, 3])
    o_dl_s = dout("o_dl_s", [L, NSEQ, NH, 128, 128])
    xs = nc.dram_tensor("xs", [NST, 128, KC, NT], F32).ap()
    XF = NH * 128 + NG * 2 + NH * 9
    if PIPE:
        mask_d = din("mask", [128, 1])
        xch_src_t = nc.dram_tensor("xch_src", [128, XF], F32)
        xch_dst_t = nc.dram_tensor("xch_dst", [2 * 128, XF], F32)
        xch_src = xch_src_t.ap()
        xch_dst = xch_dst_t.ap()
        b_xsrc = P.buf("xsrc")
        b_xdst = P.buf("xdst")
    b_xs = [P.buf(f"xs{i}") for i in range(NST)]
    out_ticks = []

    xT = P.sbuf("xT", [128, KC, NT]); b_x = [P.buf(f"x{k}") for k in range(KC)]
    hT = P.sbuf("hT", [128, KC, NT], BF16); b_h = [P.buf(f"h{k}") for k in range(KC)]
    NYH = max(YC, cfg.HK)
    yh = P.sbuf("yh", [128, NYH, NT], BF16); b_yh = [P.buf(f"yh{k}") for k in range(NYH)]
    NWS = 3
    wsl = [P.tile(f"ws{i}", [128, cfg.WSLOT], BF16) for i in range(NWS)]
    wab_sb = P.tile("wab", [128, KC, 2 * NH], BF16)
    cst = P.tile("cst", [128, cfg.NCONST])
    idb = P.tile("idb", [128, 128], BF16)
    onesb = P.tile("onesb", [128, 128], BF16)
    par = P.tile("par", [128, cfg.NPAR])
    fnw = P.tile("fnw", [128, KC])
    negA = P.tile("negA", [128, NH])
    pj = [P.tile(f"pj{i}", [128, NT]) for i in range(2)]
    rstdB = pj[0]
    WU = TP + 3 + 7 * SS
    U = [P.tile(f"U{i}", [128, WU]) for i in range(3)]
    acc = P.tile("acc", [128, NT])
    tmpN = acc
    cvp = [[P.tile(f"cv{p}_{i}", [128, NT]) for i in range(2)] for p in range(2)]
    cv = cvp[0]
    sqp = [[P.tile(f"sqp{p}_{i}", [128, NT], BF16) for i in range(2)] for p in range(2)]
    sq = [sqp[0][1], sqp[1][1]]
    zs = [P.tile(f"zs{i}", [128, NT], BF16) for i in range(2)]
    qn = [P.tile(f"qn{i}", [128, NT], BF16) for i in range(2)]
    kn = [P.tile(f"kn{i}", [128, NT], BF16) for i in range(2)]
    vb = [P.tile(f"vb{i}", [128, NT], BF16) for i in range(2)]
    haloA = P.tile("haloA", [128, NG, 2])
    haloB = P.tile("haloB", [128, NH, 3, 3])
    S32all = P.sbuf("S32all", [128, NH, 128])
    S32 = [Tile(S32all[:, h, :], P.buf(f"S32_{h}")) for h in range(NH)]
    maskt = P.tile("maskt", [128, 1])
    Sbf = [P.tile(f"Sbf_{h}", [128, 128], BF16) for h in range(NH)]
    Ss32 = [P.tile(f"Ss32_{i}", [128, SS, 128]) for i in range(1)] * 2
    Ssbf = [[P.tile(f"Ssbf_{i}_{s}", [128, 128], BF16) for s in range(SS)] for i in range(1)] * 2
    _Ssn = [P.tile(f"Ssn_{s}", [128, 128]) for s in range(2)]
    Ssn = [[_Ssn[s % 2] for s in range(SS)]]
    CS = cfg.CS
    NU = NCH + 1
    UC = [64 if u < NCH else CS for u in range(NU)]
    gt = [P.tile(f"g{u}", [UC[u], NH]) for u in range(NU)]
    bt = [P.tile(f"bt{u}", [UC[u], NH]) for u in range(NU)]
    t1 = [P.tile(f"t1_{u}", [UC[u], NH]) for u in range(NU)]
    eG = [P.tile(f"eG{u}", [UC[u], NH]) for u in range(NU)]
    neG = [P.tile(f"neG{u}", [UC[u], NH]) for u in range(NU)]
    geb = [P.tile(f"geb{u}", [128, NH if u < NCH else SS * NH]) for u in range(NU)]
    gsel = P.tile("gsel", [CS, SS, NH])
    kx = P.tile("kx", [128, SS, CS], BF16)
    qx = P.tile("qx", [128, SS, CS], BF16)
    kex = P.tile("kex", [CS, SS, 128], BF16)
    _gm = [P.tile(f"gm{i}", [64, 65]) for i in range(2)]
    gm = [_gm[u % 2] for u in range(NU)]
    E = [P.tile(f"E{u}", [UC[u], UC[u] + 1]) for u in range(NU)]
    _Es = [P.tile(f"Es{i}", [64, 64]) for i in range(2)]
    _Ei = [P.tile(f"Ei{i}", [64, 64]) for i in range(2)]
    Es = [_Es[u % 2] for u in range(NU)]
    Ei = [_Ei[u % 2] for u in range(NU)]
    LL = [P.tile(f"LL{u}", [UC[u], 2 * UC[u]]) for u in range(NU)]
    IB = 4
    _BBs = [P.tile(f"BBs{i}", [64, 128]) for i in range(IB)] + [P.tile("BBsS", [CS, 2 * CS])]
    _PQ = [P.tile(f"PQ{i}", [64, 128]) for i in range(IB)] + [P.tile("PQS", [CS, 2 * CS])]
    _TT = [[P.tile(f"TT{i}_{j}", [64, 128]) for j in range(2)] for i in range(IB)] + \
          [[P.tile(f"TTS_{j}", [CS, 2 * CS]) for j in range(2)]]
    _slot = lambda u: (u % IB) if u < NCH else IB
    BBs = [_BBs[_slot(u)] for u in range(NU)]
    PQ = [_PQ[_slot(u)] for u in range(NU)]
    TTp = [_TT[_slot(u)] for u in range(NU)]
    Ttb = [P.tile(f"Ttb{u}", [UC[u], UC[u]], BF16) for u in range(NU)]
    pmT = [P.tile(f"pmT{u}", [UC[u], UC[u]], BF16) for u in range(NU)]
    Vtm = [P.tile(f"Vtm{u}", [UC[u], 128], BF16) for u in range(NU)]
    ke = [P.tile(f"ke{u}", [UC[u], 128], BF16) for u in range(NU)]
    NR = 3
    Yt = [P.tile(f"Y{i}", [64, 128], BF16) for i in range(NR)]
    ut = [P.tile(f"u{i}", [64, 128], BF16) for i in range(NR)]
    o1s = [P.tile(f"o1s{i}", [64, 128]) for i in range(NR)]
    ot = [P.tile(f"o{i}", [64, 128]) for i in range(NR)]
    junk = o1s
    ssq = [P.tile(f"ssq{i}", [64, 1]) for i in range(NR)]
    rsq = [P.tile(f"rsq{i}", [64, 1]) for i in range(NR)]
    onb = [P.tile(f"onb{i}", [64, 128], BF16) for i in range(NR)]
    NBIG = 2
    pbig = [Tile(P.psum(f"pb{i}", [128, 1024]), P.buf(f"pb{i}")) for i in range(NBIG)]
    NSM = 4
    psm = [Tile(P.psum(f"psmb{i}", [128, 512])[:, 0:128], P.buf(f"psm{i}")) for i in range(NSM)]
    rr = {"big": 0, "sm": 0, "ws": 0, "rot": 0}

    def big():
        rr["big"] += 1
        return pbig[rr["big"] % NBIG]

    def small():
        rr["sm"] += 1
        return psm[rr["sm"] % NSM]

    def A_(eng, fn, reads=(), writes=()):
        return P.op(eng, fn, [t.b if isinstance(t, Tile) else t for t in reads],
                    [t.b if isinstance(t, Tile) else t for t in writes])

    def act(out, in_, func, reads, writes, scale=None, bias=None, accum=None):
        kw = {}
        if scale is not None:
            kw["scale"] = scale
        if bias is not None:
            kw["bias"] = bias
        if accum is not None:
            kw["accum_out"] = accum
        return A_("act", lambda e: e.activation(out=out, in_=in_, func=func, **kw), reads, writes)

    def tt(out, in0, in1, op, reads, writes):
        return A_("dve", lambda e: e.tensor_tensor(out=out, in0=in0, in1=in1, op=op), reads, writes)

    def stt(out, in0, scalar, in1, op0, op1, reads, writes):
        return A_("dve", lambda e: e.scalar_tensor_tensor(out=out, in0=in0, scalar=scalar, in1=in1,
                                                          op0=op0, op1=op1), reads, writes)

    def ts(out, in0, s1, op0, reads, writes, s2=None, op1=None):
        if op1 is None:
            return A_("dve", lambda e: e.tensor_scalar(out=out, in0=in0, scalar1=s1, scalar2=None, op0=op0),
                      reads, writes)
        return A_("dve", lambda e: e.tensor_scalar(out=out, in0=in0, scalar1=s1, scalar2=s2, op0=op0, op1=op1),
                  reads, writes)

    def recip(out, in_, reads, writes):
        return A_("dve", lambda e: e.reciprocal(out=out, in_=in_), reads, writes)

    def mm(out, lhsT, rhs, reads, writes, start=True, stop=True):
        return A_("pe", lambda e: e.matmul(out, lhsT, rhs, start=start, stop=stop), reads, writes)

    def tr(out, in_, ident, reads, writes):
        return A_("pe", lambda e: e.transpose(out, in_, ident), reads, writes)

    def sdma(out, in_, key, reads=(), writes=()):
        return P.dma("sp", lambda e: e.dma_start(out=out, in_=in_), key,
                     [t.b if isinstance(t, Tile) else t for t in reads],
                     [t.b if isinstance(t, Tile) else t for t in writes])

    def wload(src_ap, kch, cols):
        rr["ws"] += 1
        sl = wsl[rr["ws"] % NWS]
        i = rr["ws"] % NWS
        view = sl.t[:, 0:kch * cols].rearrange("p (k c) -> p k c", k=kch)
        P.dma("pool", lambda e: e.dma_start(out=view, in_=src_ap), f"w{i}", [], [sl.b])
        return sl, view

    def parcol(c0, n=1):
        return par.t[:, c0:c0 + n]

    TTS = [(0, min(512, NT))]
    if NT > 512:
        TTS.append((512, NT))

    def big_mm(wview, kch, col0, rhs_t, rhs_bufs, wslot):
        pb = big()
        fns = []
        for k in range(kch):
            for (a, b) in TTS:
                fns.append(lambda e, k=k, a=a, b=b: e.matmul(
                    pb.t[:, a:b], wview[:, k, col0:col0 + 128], rhs_t[:, k, a:b],
                    start=(k == 0), stop=(k == kch - 1)))
        P.group("pe", fns, [wslot.b] + list(rhs_bufs), [pb.b])
        return pb

    def ones_sum(src_tiles, nparts=128):
        pb = big()
        n = len(src_tiles)
        fns = []
        for i, s in enumerate(src_tiles):
            for (a, b) in TTS:
                fns.append(lambda e, i=i, s=s, a=a, b=b: e.matmul(
                    pb.t[:, a:b], onesb.t[:, :], s.t[:, a:b], start=(i == 0), stop=(i == n - 1)))
        P.group("pe", fns, [onesb.b] + [s.b for s in src_tiles], [pb.b])
        return pb

    def rstd_from(pb, scale, out_tile):
        act(tmpN.t[:, :], pb.t[:, 0:NT], ACT.Sqrt, [pb], [tmpN], scale=scale, bias=epsc.t[:, 0:1])
        recip(out_tile.t[:, :], tmpN.t[:, :], [tmpN], [out_tile])

    epsc = P.tile("epsc", [128, 1])
    A_("dve", lambda e: e.memset(epsc.t[:, :], EPS), [], [epsc])
    sdma(cst.t[:, :], const_d, "c0", [], [cst])
    sdma(fnw.t[:, :], fnw_d, "c1", [], [fnw])
    if PIPE:
        sdma(maskt.t[:, :], mask_d, "c2", [], [maskt])
    A_("dve", lambda e: e.tensor_copy(out=idb.t[:, :], in_=cst.t[:, cfg.c_id:cfg.c_id + 128]), [cst], [idb])
    A_("dve", lambda e: e.tensor_copy(out=onesb.t[:, :], in_=cst.t[:, cfg.c_ones:cfg.c_ones + 128]), [cst], [onesb])
    TRILI = lambda C: (cst.t[0:C, cfg.c_trili:cfg.c_trili + 65] if C == 64
                       else cst.t[0:C, cfg.c_b_trili:cfg.c_b_trili + C + 1])
    SGT = lambda C: cst.t[0:C, cfg.c_sgt:cfg.c_sgt + C] if C == 64 else cst.t[0:C, cfg.c_b_sgt:cfg.c_b_sgt + C]
    MSU = lambda C: cst.t[0:C, cfg.c_msu:cfg.c_msu + C] if C == 64 else cst.t[0:C, cfg.c_b_msu:cfg.c_b_msu + C]
    MIU = lambda C: cst.t[0:C, cfg.c_miu:cfg.c_miu + C] if C == 64 else cst.t[0:C, cfg.c_b_miu:cfg.c_b_miu + C]
    ONESF = lambda C: cst.t[0:C, cfg.c_ones:cfg.c_ones + 128]

    def norm_to_h(pcol):
        srcs = []
        pb = big()
        fns = []
        reads = [onesb.b]
        for k in range(KC):
            s = sq[k % 2]
            act(s.t[:, :], xT[:, k, :], ACT.Square, [b_x[k]], [s])
            for (a, b) in TTS:
                P.op("pe", lambda e, k=k, s=s, a=a, b=b: e.matmul(
                    pb.t[:, a:b], onesb.t[:, :], s.t[:, a:b], start=(k == 0), stop=(k == KC - 1)),
                    [onesb.b, s.b], [pb.b], signal=True)
        rstd_from(pb, 1.0 / D, rstdB)
        for k in range(KC):
            stt(hT[:, k, :], xT[:, k, :], parcol(pcol + k), rstdB.t[:, :], ALU.mult, ALU.mult,
                [b_x[k], par, rstdB], [b_h[k]])

    def unit_cols(u):
        if u < NCH:
            return u * 64, 64
        return TP, CS

    def ab_proj():
        pss = []
        for u in range(NU):
            c0, C = unit_cols(u)
            ps = small()
            fns = [lambda e, k=k, ps=ps, c0=c0, C=C: e.matmul(
                ps.t[0:C, 0:2 * NH], hT[:, k, c0:c0 + C], wab_sb.t[:, k, :],
                start=(k == 0), stop=(k == KC - 1)) for k in range(KC)]
            P.group("pe", fns, [wab_sb.b] + b_h, [ps.b])
            tt(t1[u].t[0:C, :], ps.t[0:C, 0:NH], parcol(cfg.p_dtb, NH)[0:C, :], ALU.add, [ps, par], [t1[u]])
            act(bt[u].t[0:C, :], ps.t[0:C, NH:2 * NH], ACT.Sigmoid, [ps], [bt[u]])
        for u in range(NU):
            c0, C = unit_cols(u)
            act(t1[u].t[0:C, :], t1[u].t[0:C, :], ACT.Exp, [t1[u]], [t1[u]])
        for u in range(NU):
            c0, C = unit_cols(u)
            act(t1[u].t[0:C, :], t1[u].t[0:C, :], ACT.Ln, [t1[u], cst], [t1[u]], bias=cst.t[0:C, cfg.c_ones:cfg.c_ones + 1])
            tt(gt[u].t[0:C, :], t1[u].t[0:C, :], negA.t[0:C, :], ALU.mult, [t1[u], negA], [gt[u]])
        for u in range(NU):
            c0, C = unit_cols(u)
            ps = small()
            mm(ps.t[0:C, 0:NH], TRILI(C)[:, 0:C], gt[u].t[0:C, :], [cst, gt[u]], [ps])
            act(eG[u].t[0:C, :], ps.t[0:C, 0:NH], ACT.Exp, [ps], [eG[u]])
            ts(neG[u].t[0:C, :], eG[u].t[0:C, :], -1.0, ALU.mult, [eG[u]], [neG[u]])
            ps2 = small()
            if u < NCH:
                mm(ps2.t[:, 0:NH], ONESF(C), gt[u].t[0:C, :], [cst, gt[u]], [ps2])
                act(geb[u].t[:, :], ps2.t[:, 0:NH], ACT.Exp, [ps2], [geb[u]])
            else:
                tt(gsel.t[:, :, :], cst.t[0:C, cfg.c_selt:cfg.c_selt + SS].unsqueeze(2).broadcast_to([C, SS, NH]),
                   gt[u].t[0:C, :].unsqueeze(1).broadcast_to([C, SS, NH]), ALU.mult, [cst, gt[u]], [gsel])
                mm(ps2.t[:, 0:SS * NH], ONESF(C), gsel.t[:, :, :].rearrange("c s h -> c (s h)"), [cst, gsel], [ps2])
                act(geb[u].t[:, :], ps2.t[:, 0:SS * NH], ACT.Exp, [ps2], [geb[u]])

    def delta_head(hd, par2, st, l, sset):
        q_, k_, v_, z_ = qn[par2], kn[par2], vb[par2], zs[par2]
        units = list(range(NU))
        kkq = {}
        for u in units:
            c0, C = unit_cols(u)
            ts(gm[u].t[0:C, 0:C + 1], TRILI(C), gt[u].t[0:C, hd:hd + 1], ALU.mult, [cst, gt[u]], [gm[u]])
            ps = small()
            mm(ps.t[0:C, 0:C + 1], SGT(C), gm[u].t[0:C, 0:C + 1], [cst, gm[u]], [ps])
            act(E[u].t[0:C, 0:C + 1], ps.t[0:C, 0:C + 1], ACT.Exp, [ps], [E[u]])
        for u in units:
            c0, C = unit_cols(u)
            tt(Es[u].t[0:C, 0:C], E[u].t[0:C, 0:C], MSU(C), ALU.mult, [E[u], cst], [Es[u]])
            tt(Ei[u].t[0:C, 0:C], E[u].t[0:C, 0:C], MIU(C), ALU.mult, [E[u], cst], [Ei[u]])
            ps = small()
            P.group("pe", [
                lambda e, ps=ps, c0=c0, C=C: e.matmul(ps.t[0:C, 0:C], k_.t[:, c0:c0 + C], k_.t[:, c0:c0 + C],
                                                      start=True, stop=True),
                lambda e, ps=ps, c0=c0, C=C: e.matmul(ps.t[0:C, 64:64 + C], k_.t[:, c0:c0 + C], q_.t[:, c0:c0 + C],
                                                      start=True, stop=True)],
                [k_.b, q_.b], [ps.b])
            stt(LL[u].t[0:C, 0:C], ps.t[0:C, 0:C], bt[u].t[0:C, hd:hd + 1], Es[u].t[0:C, 0:C],
                ALU.mult, ALU.mult, [ps, bt[u], Es[u]], [LL[u]])
            tt(pmT[u].t[0:C, 0:C], ps.t[0:C, 64:64 + C], Ei[u].t[0:C, 0:C], ALU.mult, [ps, Ei[u]], [pmT[u]])
        for u in units:
            c0, C = unit_cols(u)
            ps = small()
            tr(ps.t[0:C, 0:C], LL[u].t[0:C, 0:C], cst.t[0:C, cfg.c_id:cfg.c_id + C], [LL[u], cst], [ps])
            act(LL[u].t[0:C, C:2 * C], ps.t[0:C, 0:C], ACT.Copy, [ps], [LL[u]])
            ps2 = small()
            pv2 = ps2.t[:, :].bitcast(BF16)
            tr(pv2[0:C, 0:128], v_.t[:, c0:c0 + C], idb.t[:, :], [v_, idb], [ps2])
            act(Vtm[u].t[0:C, :], pv2[0:C, 0:128], ACT.Copy, [ps2], [Vtm[u]])
            ps3 = small()
            pv3 = ps3.t[:, :].bitcast(BF16)
            tr(pv3[0:C, 0:128], k_.t[:, c0:c0 + C], idb.t[:, :], [k_, idb], [ps3])
            act(ke[u].t[0:C, :], pv3[0:C, 0:128], ACT.Copy, [ps3, E[u]], [ke[u]], scale=E[u].t[0:C, C:C + 1])
        def MMc(C, i):
            if C == 64:
                return cst.t[0:64, cfg.c_mm64 + i * 128: cfg.c_mm64 + (i + 1) * 128]
            return cst.t[0:C, cfg.c_b_mm + i * 2 * C: cfg.c_b_mm + (i + 1) * 2 * C]

        def IIc(C):
            if C == 64:
                return cst.t[0:64, cfg.c_ii64:cfg.c_ii64 + 128]
            return cst.t[0:C, cfg.c_b_ii:cfg.c_b_ii + 2 * C]
        tcur = {u: 0 for u in units}
        nlev = {u: (6 if u < NCH else 2) for u in units}
        pr = [u for u in units if u < NCH]
        sm_ = [u for u in units if u >= NCH]
        pbatches = [pr[i:i + IB] for i in range(0, len(pr), IB)]
        sbatches = [sm_[i:i + IB] for i in range(0, len(sm_), IB)]

        def inv_gen(batch):
            for u in batch:
                c0, C = unit_cols(u)
                tt(BBs[u].t[0:C, 0:2 * C], LL[u].t[0:C, :], MMc(C, 0), ALU.mult, [LL[u], cst], [BBs[u]])
                tt(TTp[u][0].t[0:C, 0:2 * C], IIc(C), BBs[u].t[0:C, 0:2 * C], ALU.subtract, [cst, BBs[u]],
                   [TTp[u][0]])
                tcur[u] = 0
            yield
            for lev in range(1, 6):
                act_units = [u for u in batch if lev < nlev[u]]
                if not act_units:
                    continue
                for u in act_units:
                    c0, C = unit_cols(u)
                    To = TTp[u][tcur[u]]
                    tt(BBs[u].t[0:C, 0:2 * C], LL[u].t[0:C, :], MMc(C, lev), ALU.mult, [LL[u], cst], [BBs[u]])
                    ps = small()
                    P.group("pe", [
                        lambda e, ps=ps, C=C, u=u, To=To: e.matmul(ps.t[0:C, 0:C], BBs[u].t[0:C, C:2 * C],
                                                                 To.t[0:C, 0:C], start=True, stop=True),
                        lambda e, ps=ps, C=C, u=u, To=To: e.matmul(ps.t[0:C, C:2 * C], BBs[u].t[0:C, 0:C],
                                                                 To.t[0:C, C:2 * C], start=True, stop=True)],
                        [BBs[u].b, To.b], [ps.b])
                    act(PQ[u].t[0:C, 0:2 * C], ps.t[0:C, 0:2 * C], ACT.Copy, [ps], [PQ[u]])
                yield
                for u in act_units:
                    c0, C = unit_cols(u)
                    To = TTp[u][tcur[u]]
                    Tn = TTp[u][1 - tcur[u]]
                    ps2 = small()
                    P.group("pe", [
                        lambda e, ps2=ps2, C=C, u=u, To=To: e.matmul(ps2.t[0:C, 0:C], To.t[0:C, C:2 * C],
                                                                   PQ[u].t[0:C, 0:C], start=True, stop=True),
                        lambda e, ps2=ps2, C=C, u=u, To=To: e.matmul(ps2.t[0:C, C:2 * C], To.t[0:C, 0:C],
                                                                   PQ[u].t[0:C, C:2 * C], start=True, stop=True)],
                        [PQ[u].b, To.b], [ps2.b])
                    tt(Tn.t[0:C, 0:2 * C], To.t[0:C, 0:2 * C], ps2.t[0:C, 0:2 * C], ALU.subtract, [To, ps2], [Tn])
                    tcur[u] = 1 - tcur[u]
                yield
            for u in batch:
                c0, C = unit_cols(u)
                act(Ttb[u].t[0:C, 0:C], TTp[u][tcur[u]].t[0:C, 0:C], ACT.Copy, [TTp[u][tcur[u]]], [Ttb[u]])
            yield

        def p2_unit(u, ri):
            c0, C = unit_cols(u)
            Tt = Ttb[u]
            psK = small()
            psQ = small()
            if u < NCH:
                S32t, Sbft = S32[hd], Sbf[hd]
                mm(psK.t[0:C, :], k_.t[:, c0:c0 + C], Sbft.t[:, :], [k_, Sbft], [psK])
                mm(psQ.t[0:C, :], q_.t[:, c0:c0 + C], Sbft.t[:, :], [q_, Sbft], [psQ])
            else:
                selb = cst.t[:, cfg.c_selb:cfg.c_selb + SS * C].rearrange("p (s m) -> p s m", s=SS)
                tt(kx.t[:, :, :], k_.t[:, c0:c0 + C].unsqueeze(1).broadcast_to([128, SS, C]), selb, ALU.mult,
                   [k_, cst], [kx])
                tt(qx.t[:, :, :], q_.t[:, c0:c0 + C].unsqueeze(1).broadcast_to([128, SS, C]), selb, ALU.mult,
                   [q_, cst], [qx])
                sb_ = [Ssbf[sset][s_].b for s_ in range(SS)]
                P.group("pe", [lambda e, s_=s_: e.matmul(psK.t[0:C, :], kx.t[:, s_, :], Ssbf[sset][s_].t[:, :],
                                                        start=(s_ == 0), stop=(s_ == SS - 1)) for s_ in range(SS)],
                        [kx.b] + sb_, [psK.b])
                P.group("pe", [lambda e, s_=s_: e.matmul(psQ.t[0:C, :], qx.t[:, s_, :], Ssbf[sset][s_].t[:, :],
                                                        start=(s_ == 0), stop=(s_ == SS - 1)) for s_ in range(SS)],
                        [qx.b] + sb_, [psQ.b])
            stt(Yt[ri].t[0:C, :], psK.t[0:C, :], neG[u].t[0:C, hd:hd + 1], Vtm[u].t[0:C, :],
                ALU.mult, ALU.add, [psK, neG[u], Vtm[u]], [Yt[ri]])
            act(o1s[ri].t[0:C, :], psQ.t[0:C, :], ACT.Copy, [psQ, eG[u]], [o1s[ri]], scale=eG[u].t[0:C, hd:hd + 1])
            yield
            psU = small()
            mm(psU.t[0:C, :], Tt.t[0:C, 0:C], Yt[ri].t[0:C, :], [Tt, Yt[ri]], [psU])
            act(ut[ri].t[0:C, :], psU.t[0:C, :], ACT.Copy, [psU, bt[u]], [ut[ri]], scale=bt[u].t[0:C, hd:hd + 1])
            yield
            if u < NCH:
                psD = small()
                mm(psD.t[:, :], ke[u].t[0:C, :], ut[ri].t[0:C, :], [ke[u], ut[ri]], [psD])
                stt(S32t.t[:, :], S32t.t[:, :], geb[u].t[:, hd:hd + 1], psD.t[:, :], ALU.mult, ALU.add,
                    [S32t, geb[u], psD], [S32t])
                act(Sbft.t[:, :], S32t.t[:, :], ACT.Copy, [S32t], [Sbft])
            else:
                tt(kex.t[:, :, :], ke[u].t[0:C, :].unsqueeze(1).broadcast_to([C, SS, 128]),
                   cst.t[0:C, cfg.c_selt:cfg.c_selt + SS].unsqueeze(2).broadcast_to([C, SS, 128]), ALU.mult,
                   [ke[u], cst], [kex])
                for s_ in range(SS):
                    psD = small()
                    mm(psD.t[:, :], kex.t[:, s_, :], ut[ri].t[0:C, :], [kex, ut[ri]], [psD])
                    sn = Ssn[0][s_]
                    stt(sn.t[:, :], Ss32[sset].t[:, s_, :], geb[u].t[:, s_ * NH + hd:s_ * NH + hd + 1],
                        psD.t[:, :], ALU.mult, ALU.add, [Ss32[sset], geb[u], psD], [sn])
                    out_ticks.append(sdma(o_dl_s[l, st * SS + s_, hd], sn.t[:, :], f"ods{s_ % 2}", [sn], []))
                    if s_ % 2 == 1:
                        yield
            psO = small()
            mm(psO.t[0:C, :], pmT[u].t[0:C, 0:C], ut[ri].t[0:C, :], [pmT[u], ut[ri]], [psO])
            tt(ot[ri].t[0:C, :], psO.t[0:C, :], o1s[ri].t[0:C, :], ALU.add, [psO, o1s[ri]], [ot[ri]])
            act(junk[ri].t[0:C, :], ot[ri].t[0:C, :], ACT.Square, [ot[ri]], [junk[ri], ssq[ri]],
                accum=ssq[ri].t[0:C, 0:1])
            act(rsq[ri].t[0:C, :], ssq[ri].t[0:C, :], ACT.Sqrt, [ssq[ri]], [rsq[ri]], scale=1.0 / 128,
                bias=epsc.t[0:C, 0:1])
            recip(rsq[ri].t[0:C, :], rsq[ri].t[0:C, :], [rsq[ri]], [rsq[ri]])
            ts(onb[ri].t[0:C, :], ot[ri].t[0:C, :], rsq[ri].t[0:C, 0:1], ALU.mult, [ot[ri], rsq[ri]], [onb[ri]])
            yield
            psT = small()
            pvT = psT.t[:, :].bitcast(BF16)
            tr(pvT[:, 0:C], onb[ri].t[0:C, :], idb.t[0:C, 0:C], [onb[ri], idb], [psT])
            stt(yh[:, NG + hd, c0:c0 + C], pvT[:, 0:C], parcol(cfg.p_dnw), z_.t[:, c0:c0 + C],
                ALU.mult, ALU.mult, [psT, par, z_], [b_yh[NG + hd]])
            yield

        def p2_chain(us, ri0, alt=True):
            for i, u in enumerate(us):
                yield from p2_unit(u, (ri0 + i) % 2 if alt else ri0)

        def run(*gens):
            gens = list(gens)
            while gens:
                for g in list(gens):
                    try:
                        next(g)
                    except StopIteration:
                        gens.remove(g)

        SU = NCH
        run(inv_gen(pbatches[0]), inv_gen([SU]))
        sgen = [p2_unit(SU, 2)]
        for i in range(1, len(pbatches)):
            run(inv_gen(pbatches[i]), p2_chain(pbatches[i - 1], 0), *sgen)
            sgen = []
        run(p2_chain(pbatches[-1], 0), *sgen)


    def conv_apply(Ut, W, wcol0, out_t, reads_extra):
        hw = W - 1
        so = hw + TP
        sw = hw + 4
        Us = Ut.t[:, so:so + SS * sw].rearrange("p (s w) -> p s w", s=SS)
        outs = out_t.t[:, TP:NT].rearrange("p (s w) -> p s w", s=SS)
        act(out_t.t[:, 0:TP], Ut.t[:, 0:TP], ACT.Copy, [Ut, par], [out_t], scale=parcol(wcol0))
        act(outs, Us[:, :, 0:4], ACT.Copy, [Ut, par], [out_t], scale=parcol(wcol0))
        for i in range(1, W):
            stt(out_t.t[:, 0:TP], Ut.t[:, i:i + TP], parcol(wcol0 + i), out_t.t[:, 0:TP], ALU.mult, ALU.add,
                [Ut, par, out_t], [out_t])
            stt(outs, Us[:, :, i:i + 4], parcol(wcol0 + i), outs, ALU.mult, ALU.add, [Ut, par, out_t], [out_t])

    def evac_halo(pb, Ut, W):
        hw = W - 1
        so = hw + TP
        sw = hw + 4
        Us = Ut.t[:, so:so + SS * sw].rearrange("p (s w) -> p s w", s=SS)
        act(Ut.t[:, hw:hw + TP], pb.t[:, 0:TP], ACT.Copy, [pb], [Ut])
        act(Us[:, :, hw:hw + 4], pb.t[:, TP:NT].rearrange("p (s w) -> p s w", s=SS), ACT.Copy, [pb], [Ut])
        return Us

    def run_pass(l, st):
        src = xin[st] if l == 0 else xs[st]
        sdma(xT[:, :, :], src, "xld", ([b_xs[st]] if l > 0 else []), b_x)
        fresh = (st == 0 and (not PIPE or l == 0))
        if PIPE and st == 0 and l > 0:
            r0 = xch_dst[0:128, :]
            sdma(S32all[:, :, :], r0[:, 0:NH * 128].rearrange("p (h v) -> p h v", h=NH), "xl0", [b_xdst],
                 [t.b for t in S32])
            sdma(haloA.t[:, :, :], r0[:, NH * 128:NH * 128 + NG * 2].rearrange("p (g t) -> p g t", g=NG), "xl1",
                 [b_xdst], [haloA])
            sdma(haloB.t[:, :, :, :], r0[:, NH * 128 + NG * 2:XF].rearrange("p (h w t) -> p h w t", h=NH, w=3),
                 "xl2", [b_xdst], [haloB])
            ts(S32all[:, :, :], S32all[:, :, :], maskt.t[:, 0:1], ALU.mult, [t.b for t in S32] + [maskt],
               [t.b for t in S32])
            ts(haloA.t[:, :, :], haloA.t[:, :, :], maskt.t[:, 0:1], ALU.mult, [haloA, maskt], [haloA])
            ts(haloB.t[:, :, :, :], haloB.t[:, :, :, :], maskt.t[:, 0:1], ALU.mult, [haloB, maskt], [haloB])
        norm_to_h(cfg.p_n1)
        P.dma("pool", lambda e: e.dma_start(out=wab_sb.t[:, :, :], in_=w_ab[l]), "wab", [], [wab_sb.b])
        ab_proj()
        def grpA_part1(g):
            p = g % 2
            sl, wv = wload(w_A[l, g], KC, 384)
            Ut = U[0]
            hw = 2
            so = hw + TP
            sw = hw + 4
            Us = Ut.t[:, so:so + SS * sw].rearrange("p (s w) -> p s w", s=SS)
            for i in range(3):
                pb = big_mm(wv, KC, i * 128, hT, b_h, sl)
                if i == 0:
                    act(pj[0].t[:, :], pb.t[:, 0:NT], ACT.Copy, [pb], [pj[0]])
                elif i == 1:
                    act(pj[1].t[:, :], pb.t[:, 0:NT], ACT.Copy, [pb], [pj[1]])
                else:
                    tt(Ut.t[:, hw:hw + TP], pb.t[:, 0:TP], pj[1].t[:, 0:TP], ALU.mult, [pb, pj[1]], [Ut])
                    tt(Us[:, :, hw:hw + 4], pb.t[:, TP:NT].rearrange("p (s w) -> p s w", s=SS),
                       pj[1].t[:, TP:NT].rearrange("p (s w) -> p s w", s=SS), ALU.mult, [pb, pj[1]], [Ut])
            if fresh:
                A_("dve", lambda e, Ut=Ut: e.memset(Ut.t[:, 0:2], 0.0), [], [Ut])
            else:
                act(Ut.t[:, 0:2], haloA.t[:, g, :], ACT.Copy, [haloA], [Ut])
            sdma(Us[:, :, 0:2], s_ca[l, :, g, st * SS:(st + 1) * SS, :], "sca", [], [Ut])
            act(haloA.t[:, g, :], Ut.t[:, TP:TP + 2], ACT.Copy, [Ut], [haloA])
            out_ticks.append(sdma(o_ca_s[l, :, g, st * SS:(st + 1) * SS, :], Us[:, :, 4:6], "ocas", [Ut], []))
            conv_apply(Ut, 3, cfg.p_caw + g * 3, acc, [])
            tt(cvp[p][0].t[:, :], pj[0].t[:, :], acc.t[:, :], ALU.mult, [pj[0], acc], [cvp[p][0]])
            act(sqp[p][0].t[:, :], cvp[p][0].t[:, :], ACT.Square, [cvp[p][0]], [sqp[p][0]])

        def grpA_part2(g):
            p = g % 2
            pb = ones_sum([sqp[p][0]])
            rstd_from(pb, 1.0 / 128, cvp[p][1])
            stt(yh[:, g, 0:NT], cvp[p][0].t[:, :], parcol(cfg.p_canw + g), cvp[p][1].t[:, :], ALU.mult, ALU.mult,
                [cvp[p][0], par, cvp[p][1]], [b_yh[g]])

        grpA_part1(0)
        for g in range(NG):
            if g + 1 < NG:
                grpA_part1(g + 1)
            grpA_part2(g)
        if st == NST - 1:
            out_ticks.append(sdma(o_ca_p[l], haloA.t[:, :, :], "ocap", [haloA], []))
        def headB_part1(hd):
            p2 = hd % 2
            if fresh:
                A_("dve", lambda e, hd=hd: e.memset(S32[hd].t[:, :], 0.0), [], [S32[hd]])
                A_("dve", lambda e, hd=hd: e.memset(Sbf[hd].t[:, :], 0.0), [], [Sbf[hd]])
            elif st == 0:
                act(Sbf[hd].t[:, :], S32[hd].t[:, :], ACT.Copy, [S32[hd]], [Sbf[hd]])
            sl, wv = wload(w_B[l, hd], KC, 512)
            for i in range(3):
                pb = big_mm(wv, KC, i * 128, hT, b_h, sl)
                Ut = U[i]
                Us = evac_halo(pb, Ut, 4)
                if fresh:
                    A_("dve", lambda e, Ut=Ut: e.memset(Ut.t[:, 0:3], 0.0), [], [Ut])
                else:
                    act(Ut.t[:, 0:3], haloB.t[:, hd, i, :], ACT.Copy, [haloB], [Ut])
                sdma(Us[:, :, 0:3], s_cq[l, :, hd, i, st * SS:(st + 1) * SS, :], f"scq{i}", [], [Ut])
                act(haloB.t[:, hd, i, :], Ut.t[:, TP:TP + 3], ACT.Copy, [Ut], [haloB])
                out_ticks.append(sdma(o_cq_s[l, :, hd, i, st * SS:(st + 1) * SS, :], Us[:, :, 4:7], f"ocqs{i}",
                                      [Ut], []))
            pb = big_mm(wv, KC, 3 * 128, hT, b_h, sl)
            act(zs[p2].t[:, :], pb.t[:, 0:NT], ACT.Silu, [pb], [zs[p2]])
            for i in range(3):
                conv_apply(U[i], 4, cfg.p_cbw + (hd * 3 + i) * 4, acc, [])
                if i == 2:
                    act(vb[p2].t[:, :], acc.t[:, :], ACT.Silu, [acc], [vb[p2]])
                else:
                    act(cvp[p2][i].t[:, :], acc.t[:, :], ACT.Silu, [acc], [cvp[p2][i]])
            for i in range(2):
                act(sqp[p2][i].t[:, :], cvp[p2][i].t[:, :], ACT.Square, [cvp[p2][i]], [sqp[p2][i]])

        def headB_part2(hd):
            p2 = hd % 2
            sset = hd % 2
            sst = Ss32[sset]
            sdma(sst.t[:, :, :], s_dl[l, st * SS:(st + 1) * SS, hd].rearrange("s k v -> k s v"), "sdl",
                 [], [sst])
            for s_ in range(SS):
                act(Ssbf[sset][s_].t[:, :], sst.t[:, s_, :], ACT.Copy, [sst], [Ssbf[sset][s_]])
            for i in range(2):
                pb = ones_sum([sqp[p2][i]])
                rstd_from(pb, 1.0, pj[i])
                if i == 0:
                    stt(qn[p2].t[:, :], cvp[p2][0].t[:, :], float(128 ** -0.5), pj[0].t[:, :], ALU.mult, ALU.mult,
                        [cvp[p2][0], pj[0]], [qn[p2]])
                else:
                    tt(kn[p2].t[:, :], cvp[p2][1].t[:, :], pj[1].t[:, :], ALU.mult, [cvp[p2][1], pj[1]], [kn[p2]])
            delta_head(hd, p2, st, l, hd % 2)
            if st == NST - 1:
                out_ticks.append(sdma(o_dl_p[l, hd], S32[hd].t[:, :], "odp", [S32[hd]], []))

        headB_part1(0)
        for hd in range(NH):
            if hd + 1 < NH:
                headB_part1(hd + 1)
            headB_part2(hd)
        if st == NST - 1:
            out_ticks.append(sdma(o_cq_p[l], haloB.t[:, :, :, :], "ocqp", [haloB], []))
            if PIPE and l < L - 1:
                sdma(xch_src[:, 0:NH * 128].rearrange("p (h v) -> p h v", h=NH), S32all[:, :, :], "xs0",
                     [t.b for t in S32], [b_xsrc])
                sdma(xch_src[:, NH * 128:NH * 128 + NG * 2].rearrange("p (g t) -> p g t", g=NG), haloA.t[:, :, :],
                     "xs1", [haloA], [b_xsrc])
                sdma(xch_src[:, NH * 128 + NG * 2:XF].rearrange("p (h w t) -> p h w t", h=NH, w=3),
                     haloB.t[:, :, :, :], "xs2", [haloB], [b_xsrc])
                P.dma("pool", lambda e: e.collective_compute(
                    "AllGather", ALU.bypass, replica_groups=cfg.GROUPS,
                    ins=[xch_src_t.ap().opt()], outs=[xch_dst_t.ap().opt()]), "cc", [b_xsrc], [b_xdst], inc=1)
        for ot_ in range(cfg.NOT):
            sl, wv = wload(w_o[l, ot_], YC, cfg.CWO)
            for j in range(cfg.CWO // 128):
                oc = ot_ * (cfg.CWO // 128) + j
                pb = big_mm(wv, YC, j * 128, yh, b_yh[0:YC], sl)
                tt(xT[:, oc, :], pb.t[:, 0:NT], xT[:, oc, :], ALU.add, [pb, b_x[oc]], [b_x[oc]])
        norm_to_h(cfg.p_n2)
        for hg in range(cfg.NHG):
            for ut_ in range(cfg.NUT):
                sl, wv = wload(w_u[l, hg * cfg.NUT + ut_], KC, 512)
                for j in range(4):
                    hc = ut_ * 4 + j
                    pb = big_mm(wv, KC, j * 128, hT, b_h, sl)
                    act(tmpN.t[:, :], pb.t[:, 0:NT], ACT.Relu, [pb], [tmpN])
                    tt(yh[:, hc, 0:NT], tmpN.t[:, :], tmpN.t[:, :], ALU.mult, [tmpN], [b_yh[hc]])
            for ot_ in range(cfg.NOT):
                sl, wv = wload(w_d[l, hg, ot_], cfg.HK, cfg.CWO)
                for j in range(cfg.CWO // 128):
                    oc = ot_ * (cfg.CWO // 128) + j
                    pb = big_mm(wv, cfg.HK, j * 128, yh, b_yh[0:cfg.HK], sl)
                    tt(xT[:, oc, :], pb.t[:, 0:NT], xT[:, oc, :], ALU.add, [pb, b_x[oc]], [b_x[oc]])
        if l == L - 1:
            pb = big()
            for k in range(KC):
                s = sq[k % 2]
                act(s.t[:, :], xT[:, k, :], ACT.Square, [b_x[k]], [s])
                for (a, b) in TTS:
                    P.op("pe", lambda e, k=k, s=s, a=a, b=b, pb=pb: e.matmul(
                        pb.t[:, a:b], onesb.t[:, :], s.t[:, a:b], start=(k == 0), stop=(k == KC - 1)),
                        [onesb.b, s.b], [pb.b], signal=True)
            rstd_from(pb, 1.0 / D, rstdB)
            for k in range(KC):
                stt(xT[:, k, :], xT[:, k, :], fnw.t[:, k:k + 1], rstdB.t[:, :], ALU.mult, ALU.mult,
                    [b_x[k], fnw, rstdB], [b_x[k]])
            out_ticks.append(sdma(yout[st], xT[:, :, :], "xst", b_x, []))
        else:
            sdma(xs[st], xT[:, :, :], "xst", b_x, [b_xs[st]])

    for l in range(L):
        sdma(par.t[:, :], par_d[l], "par", [], [par])
        act(negA.t[:, :], par.t[:, cfg.p_alog:cfg.p_alog + NH], ACT.Exp, [par], [negA])
        ts(negA.t[:, :], negA.t[:, :], -1.0, ALU.mult, [negA], [negA])
        for st in range(NST):
            run_pass(l, st)
    P.wait_all("sp", out_ticks)
    P.finish()
    return nc


def prep_weights(cfg, inp):
    D, L, KC, NG, NH = cfg.D, cfg.L, cfg.KC, cfg.NG, cfg.NH
    DC = cfg.DC
    w_in = np.asarray(inp["w_in"], np.float32)

    def ktile(w):
        c = w.shape[-1]
        return np.ascontiguousarray(w.reshape(L, -1, 128, c).transpose(0, 2, 1, 3))
    oB, oC, oH, oQ, oZ, oa = 0, DC, 2 * DC, 3 * DC, 6 * DC, 7 * DC
    w_ab = ktile(w_in[:, :, oa:oa + 2 * NH])
    wA = np.empty((L, NG, 128, KC, 384), np.float32)
    for g in range(NG):
        cols = np.concatenate([np.arange(o + g * 128, o + (g + 1) * 128) for o in (oB, oC, oH)])
        wA[:, g] = ktile(w_in[:, :, cols])
    wB = np.empty((L, NH, 128, KC, 512), np.float32)
    for h in range(NH):
        cols = np.concatenate([np.arange(o + h * 128, o + (h + 1) * 128)
                               for o in (oQ, oQ + DC, oQ + 2 * DC, oZ)])
        wB[:, h] = ktile(w_in[:, :, cols])
    w_out = np.asarray(inp["w_out"], np.float32)
    wo = np.stack([ktile(w_out[:, :, t * cfg.CWO:(t + 1) * cfg.CWO]) for t in range(cfg.NOT)], 1)
    w_up = np.asarray(inp["w_up"], np.float32)
    wu = np.stack([ktile(w_up[:, :, t * 512:(t + 1) * 512]) for t in range(cfg.DFF // 512)], 1)
    w_dn = np.asarray(inp["w_down"], np.float32)
    wd = np.empty((L, cfg.NHG, cfg.NOT, 128, cfg.HK, cfg.CWO), np.float32)
    for hg in range(cfg.NHG):
        for t in range(cfg.NOT):
            wd[:, hg, t] = ktile(w_dn[:, hg * cfg.HGS:(hg + 1) * cfg.HGS, t * cfg.CWO:(t + 1) * cfg.CWO])
    par = np.zeros((L, 128, cfg.NPAR), np.float32)
    fm = lambda v: v.reshape(L, -1, 128).transpose(0, 2, 1)
    par[:, :, cfg.p_n1:cfg.p_n1 + KC] = fm(np.asarray(inp["norm_mix_w"]))
    par[:, :, cfg.p_n2:cfg.p_n2 + KC] = fm(np.asarray(inp["norm_ffn_w"]))
    caw = np.asarray(inp["conv_a_w"]).reshape(L, 3, NG, 128).transpose(0, 3, 2, 1)
    par[:, :, cfg.p_caw:cfg.p_caw + NG * 3] = caw.reshape(L, 128, NG * 3)
    par[:, :, cfg.p_canw:cfg.p_canw + NG] = fm(np.asarray(inp["conv_a_norm_w"]))
    cbw = np.asarray(inp["conv_qkv_w"]).reshape(L, 4, 3, NH, 128).transpose(0, 4, 3, 2, 1)
    par[:, :, cfg.p_cbw:cfg.p_cbw + NH * 12] = cbw.reshape(L, 128, NH * 12)
    par[:, :, cfg.p_dnw] = np.asarray(inp["dn_norm_w"])
    par[:, :, cfg.p_dtb:cfg.p_dtb + NH] = np.asarray(inp["dt_bias"])[:, None, :]
    par[:, :, cfg.p_alog:cfg.p_alog + NH] = np.asarray(inp["a_log"])[:, None, :]
    fnw = np.ascontiguousarray(np.asarray(inp["final_norm_w"], np.float32).reshape(-1, 128).T)
    return dict(w_ab=w_ab, w_A=wA, w_B=wB, w_o=wo, w_u=wu, w_d=wd, par=par, fnw=fnw,
                consts=make_consts(cfg))


def prep_core(cfg, inp, xp_seq, seq0):
    D, L, KC, NG, NH, NST, TP, SS, NT = cfg.D, cfg.L, cfg.KC, cfg.NG, cfg.NH, cfg.NST, cfg.TP, cfg.SS, cfg.NT
    NSEQ = cfg.NSEQ
    xsamp = np.asarray(inp["x_sample"], np.float32)[seq0:seq0 + NSEQ]
    xin = np.empty((NST, 128, KC, NT), np.float32)
    for st in range(NST):
        tok = np.concatenate([xp_seq[st * TP:(st + 1) * TP], xsamp[st * SS:(st + 1) * SS].reshape(-1, D)], 0)
        xin[st] = tok.T.reshape(KC, 128, NT).transpose(1, 0, 2)
    sca = np.asarray(inp["state_conv_a"], np.float32)[:, seq0:seq0 + NSEQ]
    s_ca = np.ascontiguousarray(sca.reshape(L, NSEQ, 2, NG, 128).transpose(0, 4, 3, 1, 2))
    scq = np.asarray(inp["state_conv_qkv"], np.float32)[:, seq0:seq0 + NSEQ]
    s_cq = np.ascontiguousarray(scq.reshape(L, NSEQ, 3, 3, NH, 128).transpose(0, 5, 4, 3, 1, 2))
    sdl = np.asarray(inp["state_delta"], np.float32)[:, seq0:seq0 + NSEQ]
    s_dl = np.ascontiguousarray(sdl.transpose(0, 1, 2, 4, 3))
    return dict(xin=xin, s_ca=s_ca, s_cq=s_cq, s_dl=s_dl)


_CACHE = {}
LAYERED = ("w_ab", "w_A", "w_B", "w_o", "w_u", "w_d", "par", "s_ca", "s_cq", "s_dl")


def _role_shift(arr, role):
    z = np.zeros_like(arr[:1])
    return np.concatenate([arr, z], 0) if role == 0 else np.concatenate([z, arr], 0)


def kernel(**inputs):
    cfg = Cfg(NST=2, TP=512, SS=8, PIPE=True)
    NCORES = 8
    if "nc" not in _CACHE:
        _CACHE["nc"] = build_program(cfg)
    nc = _CACHE["nc"]
    wts = prep_weights(cfg, inputs)
    xp = np.asarray(inputs["x_prompt"], np.float32)
    B, S, D = xp.shape
    half = cfg.NST * cfg.TP
    shared = []
    for role in range(2):
        d = {}
        for k, v in wts.items():
            d[k] = _role_shift(v, role) if k in LAYERED else v
        d["mask"] = np.full((128, 1), float(role), np.float32)
        shared.append(d)
    in_maps = []
    for c in range(NCORES):
        b, role = c // 2, c % 2
        m = dict(shared[role])
        pc = prep_core(cfg, inputs, xp[b, role * half:(role + 1) * half], c * cfg.NSEQ)
        for k, v in pc.items():
            m[k] = _role_shift(v, role) if k in LAYERED else v
        in_maps.append(m)
    res = run_bass_kernel_spmd(nc, in_maps, core_ids=list(range(NCORES)))
    return assemble(cfg, res.results, B)


def assemble(cfg, results, B):
    D, L, KC, NG, NH, NST, TP, SS, NT, NSEQ = (cfg.D, cfg.L, cfg.KC, cfg.NG, cfg.NH, cfg.NST, cfg.TP,
                                                 cfg.SS, cfg.NT, cfg.NSEQ)
    DC = cfg.DC
    ncores = len(results)
    half = NST * TP
    y_p = np.empty((B, 2 * half, D), np.float32)
    y_s = np.empty((ncores * NSEQ, 4, D), np.float32)
    ca_p = np.empty((L, B, 2, DC), np.float32)
    cq_p = np.empty((L, B, 3, 3 * DC), np.float32)
    dl_p = np.empty((L, B, NH, 128, 128), np.float32)
    ca_s = np.empty((L, ncores * NSEQ, 2, DC), np.float32)
    cq_s = np.empty((L, ncores * NSEQ, 3, 3 * DC), np.float32)
    dl_s = np.empty((L, ncores * NSEQ, NH, 128, 128), np.float32)
    for c, r in enumerate(results):
        b, role = c // 2, c % 2
        st0 = slice(role, role + L)
        yo = np.asarray(r["yout"])
        tok = yo.transpose(0, 3, 2, 1).reshape(NST, NT, D)
        for st in range(NST):
            y_p[b, role * half + st * TP: role * half + (st + 1) * TP] = tok[st, :TP]
            y_s[c * NSEQ + st * SS: c * NSEQ + (st + 1) * SS] = tok[st, TP:].reshape(SS, 4, D)
        sl = slice(c * NSEQ, (c + 1) * NSEQ)
        ca_s[:, sl] = np.asarray(r["o_ca_s"])[st0].transpose(0, 3, 4, 2, 1).reshape(L, NSEQ, 2, DC)
        cq_s[:, sl] = np.asarray(r["o_cq_s"])[st0].transpose(0, 4, 5, 3, 2, 1).reshape(L, NSEQ, 3, 3 * DC)
        dl_s[:, sl] = np.asarray(r["o_dl_s"])[st0].transpose(0, 1, 2, 4, 3)
        if role == 1:
            ca_p[:, b] = np.asarray(r["o_ca_p"])[st0].transpose(0, 3, 2, 1).reshape(L, 2, DC)
            cq_p[:, b] = np.asarray(r["o_cq_p"])[st0].transpose(0, 4, 3, 2, 1).reshape(L, 3, 3 * DC)
            dl_p[:, b] = np.asarray(r["o_dl_p"])[st0].transpose(0, 1, 3, 2)
    return (y_p, y_s, ca_p, cq_p, dl_p, ca_s, cq_s, dl_s)
```
